# Optimizing a Trainium2 kernel written in Bass

```python
import jax, jax.numpy as jnp
from jax import lax
import numpy as np

D_MODEL = 1024
BATCH = 8
SEQ = 2048
DEPTH = 2
DEC_BATCH = 128
DEC_SEQ = 8
PAST_LEN = 16384
PAGE_SIZE = 128

HEAD_DIM = 64
D_A = D_MODEL // 2
N_HEADS_A = D_A // HEAD_DIM
LORA_W = 64
LORA_A = 64
LORA_G = 128
D_SHIFT = 3 * D_A + LORA_W + LORA_A + LORA_G
D_POOL = D_MODEL - D_A
POOL_WINDOWS = (2, 4, 8, 16)
N_POOL_GROUPS = len(POOL_WINDOWS)
POOL_GD = D_POOL // N_POOL_GROUPS
POOL_BUF = max(POOL_WINDOWS) - 1
D_IN = D_SHIFT + D_POOL + 2 * D_MODEL
D_FF = 4 * D_MODEL
RMS_EPS = 1e-6
GN_EPS = HEAD_DIM * 1e-5
L2_EPS = 1e-12

kernel_name = 'rwkv7_multiscale_pool_gated_hybrid_step'


def _rmsnorm(x, g):
    xf = x.astype(jnp.float32)
    y = xf * lax.rsqrt(jnp.mean(xf * xf, axis=-1, keepdims=True) + RMS_EPS)
    return (y * g.astype(jnp.float32)).astype(x.dtype)


def _wkv7_scan(r, w, k, v, a, b, s0):
    def step(s, inp):
        rt, wt, kt, vt, at, bt = inp
        sa = jnp.einsum('bhij,bhj->bhi', s, at)
        s = s * wt[:, :, None, :] + sa[..., None] * bt[:, :, None, :] + vt[..., None] * kt[:, :, None, :]
        y = jnp.einsum('bhij,bhj->bhi', s, rt)
        return s, y
    xs = (jnp.moveaxis(r, 1, 0), jnp.moveaxis(w, 1, 0), jnp.moveaxis(k, 1, 0),
          jnp.moveaxis(v, 1, 0), jnp.moveaxis(a, 1, 0), jnp.moveaxis(b, 1, 0))
    s_final, ys = lax.scan(step, s0, xs)
    return jnp.moveaxis(ys, 0, 1), s_final


def _mixer(h, shift_prev, pool_prev, wkv_prev, pos0,
           w_in, mu_shift, decay0, w_decay2, a0, w_a2, w_g2, k_k, k_a, r_k,
           ln_x_g, ln_x_b, w_a_up, w_pool, pool_scale, w_b_up, w_o):
    f32 = jnp.float32
    bsz, t_len, _ = h.shape
    z = h @ w_in
    z_rw = z[..., :D_SHIFT]
    u = z[..., D_SHIFT:D_SHIFT + D_POOL]
    gate_a = jax.nn.sigmoid(z[..., D_SHIFT + D_POOL:D_SHIFT + D_POOL + D_MODEL].astype(f32))
    gate_b = jax.nn.sigmoid(z[..., D_SHIFT + D_POOL + D_MODEL:].astype(f32))

    z_prev = jnp.concatenate([shift_prev[:, None, :].astype(z.dtype), z_rw[:, :-1]], axis=1)
    zs = (z_rw + (z_prev - z_rw) * mu_shift).astype(f32)
    o1, o2, o3 = D_A, 2 * D_A, 3 * D_A
    o4 = o3 + LORA_W
    o5 = o4 + LORA_A
    r = zs[..., :o1]
    k = zs[..., o1:o2]
    v = zs[..., o2:o3]
    lw = zs[..., o3:o4]
    la = zs[..., o4:o5]
    lg = zs[..., o5:]
    w_log = -jax.nn.softplus(-(decay0.astype(f32) + jnp.tanh(lw) @ w_decay2.astype(f32))) - 0.5
    decay = jnp.exp(-jnp.exp(w_log))
    a_in = jax.nn.sigmoid(a0.astype(f32) + la @ w_a2.astype(f32))
    g = jax.nn.sigmoid(lg) @ w_g2.astype(f32)
    hs = (bsz, t_len, N_HEADS_A, HEAD_DIM)
    kk = (k * k_k.astype(f32)).reshape(hs)
    kk = kk / jnp.maximum(jnp.sqrt(jnp.sum(kk * kk, axis=-1, keepdims=True)), L2_EPS)
    k = k * (1.0 + (a_in - 1.0) * k_a.astype(f32))
    rh = r.reshape(hs)
    kh = k.reshape(hs)
    vh = v.reshape(hs)
    ah = a_in.reshape(hs)
    y, wkv_new = _wkv7_scan(rh, decay.reshape(hs), kh, vh, -kk, kk * ah, wkv_prev.astype(f32))
    mu = jnp.mean(y, axis=-1, keepdims=True)
    var = jnp.mean(jnp.square(y - mu), axis=-1, keepdims=True)
    yn = ((y - mu) * lax.rsqrt(var + GN_EPS)).reshape(bsz, t_len, D_A)
    yn = yn * ln_x_g.astype(f32) + ln_x_b.astype(f32)
    bonus = (jnp.sum(rh * kh * r_k.astype(f32), axis=-1, keepdims=True) * vh).reshape(bsz, t_len, D_A)
    a_out = (((yn + bonus) * g).astype(h.dtype)) @ w_a_up

    full = jnp.concatenate([pool_prev.astype(u.dtype), u], axis=1)
    csum = lax.cumsum(full.astype(f32), axis=1)
    csum = jnp.concatenate([jnp.zeros_like(csum[:, :1]), csum], axis=1)
    pos = pos0 + jnp.arange(t_len)
    means = []
    for gi, win in enumerate(POOL_WINDOWS):
        sl = slice(gi * POOL_GD, (gi + 1) * POOL_GD)
        win_sum = (csum[:, POOL_BUF + 1:POOL_BUF + 1 + t_len, sl]
                   - csum[:, POOL_BUF + 1 - win:POOL_BUF + 1 - win + t_len, sl])
        cnt = jnp.minimum(pos + 1, win).astype(f32)[None, :, None]
        means.append(win_sum / cnt)
    p = (jnp.concatenate(means, axis=-1) - u.astype(f32)).astype(h.dtype)
    p = p.reshape(bsz, t_len, N_POOL_GROUPS, POOL_GD)
    p = jnp.einsum('btgc,gcd->btgd', p, w_pool).reshape(bsz, t_len, D_POOL) * pool_scale
    b_out = p @ w_b_up

    merged = (gate_a * a_out + gate_b * b_out).astype(h.dtype)
    out = merged @ w_o
    return out, z_rw[:, -1], full[:, -POOL_BUF:], wkv_new.astype(wkv_prev.dtype)


def _trunk(x, shift0, pool0, wkv0, pos0, norm1_g, norm2_g, final_norm_g, mixer_w, w_ff1, w_ff2):
    shifts, pools, wkvs = [], [], []
    for l in range(DEPTH):
        h = _rmsnorm(x, norm1_g[l])
        m, s_sh, s_pl, s_wk = _mixer(h, shift0[l], pool0[l], wkv0[l], pos0,
                                     *[wt[l] for wt in mixer_w])
        x = x + m
        h = _rmsnorm(x, norm2_g[l])
        x = x + jnp.square(jax.nn.relu(h @ w_ff1[l])) @ w_ff2[l]
        shifts.append(s_sh)
        pools.append(s_pl)
        wkvs.append(s_wk)
    return _rmsnorm(x, final_norm_g), jnp.stack(shifts), jnp.stack(pools), jnp.stack(wkvs)


def setup_inputs(seed: int = 0) -> dict:
    key = jax.random.key(seed)
    ks = jax.random.split(key, 32)

    def nrm(k, shape, scale):
        return jax.random.normal(k, shape, jnp.float32) * scale

    L = DEPTH
    return {
        'x_prompt': nrm(ks[0], (BATCH, SEQ, D_MODEL), 1.0),
        'x_sample': nrm(ks[1], (DEC_BATCH, DEC_SEQ, D_MODEL), 1.0),
        'state_shift': nrm(ks[2], (L, DEC_BATCH, D_SHIFT), 1.0),
        'state_pool': nrm(ks[3], (L, DEC_BATCH, POOL_BUF, D_POOL), 1.0),
        'state_wkv': nrm(ks[4], (L, DEC_BATCH, N_HEADS_A, HEAD_DIM, HEAD_DIM), 0.3),
        'norm1_g': 1.0 + nrm(ks[5], (L, D_MODEL), 0.05),
        'w_in': nrm(ks[6], (L, D_MODEL, D_IN), D_MODEL ** -0.5),
        'mu_shift': jax.random.uniform(ks[7], (L, D_SHIFT), jnp.float32, 0.1, 0.9),
        'decay0': -1.0 + nrm(ks[8], (L, D_A), 0.5),
        'w_decay2': nrm(ks[9], (L, LORA_W, D_A), 0.1 * LORA_W ** -0.5),
        'a0': nrm(ks[10], (L, D_A), 0.1),
        'w_a2': nrm(ks[11], (L, LORA_A, D_A), 0.5 * LORA_A ** -0.5),
        'w_g2': nrm(ks[12], (L, LORA_G, D_A), LORA_G ** -0.5),
        'k_k': 0.85 + nrm(ks[13], (L, D_A), 0.05),
        'k_a': 1.0 + nrm(ks[14], (L, D_A), 0.05),
        'r_k': nrm(ks[15], (L, N_HEADS_A, HEAD_DIM), 0.1),
        'ln_x_g': 1.0 + nrm(ks[16], (L, D_A), 0.05),
        'ln_x_b': nrm(ks[17], (L, D_A), 0.02),
        'w_a_up': nrm(ks[18], (L, D_A, D_MODEL), D_A ** -0.5),
        'w_pool': nrm(ks[19], (L, N_POOL_GROUPS, POOL_GD, POOL_GD), POOL_GD ** -0.5),
        'pool_scale': 1.0 + nrm(ks[20], (L, D_POOL), 0.1),
        'w_b_up': nrm(ks[21], (L, D_POOL, D_MODEL), D_POOL ** -0.5),
        'w_o': nrm(ks[22], (L, D_MODEL, D_MODEL), D_MODEL ** -0.5),
        'norm2_g': 1.0 + nrm(ks[23], (L, D_MODEL), 0.05),
        'w_ff1': nrm(ks[24], (L, D_MODEL, D_FF), D_MODEL ** -0.5),
        'w_ff2': nrm(ks[25], (L, D_FF, D_MODEL), D_FF ** -0.5),
        'final_norm_g': 1.0 + nrm(ks[26], (D_MODEL,), 0.05),
    }


def reference(x_prompt, x_sample, state_shift, state_pool, state_wkv,
              norm1_g, w_in, mu_shift, decay0, w_decay2, a0, w_a2, w_g2, k_k, k_a, r_k,
              ln_x_g, ln_x_b, w_a_up, w_pool, pool_scale, w_b_up, w_o,
              norm2_g, w_ff1, w_ff2, final_norm_g):
    mixer_w = (w_in, mu_shift, decay0, w_decay2, a0, w_a2, w_g2, k_k, k_a, r_k,
               ln_x_g, ln_x_b, w_a_up, w_pool, pool_scale, w_b_up, w_o)
    bsz = x_prompt.shape[0]
    shift_zero = jnp.zeros((DEPTH, bsz, D_SHIFT), state_shift.dtype)
    pool_zero = jnp.zeros((DEPTH, bsz, POOL_BUF, D_POOL), state_pool.dtype)
    wkv_zero = jnp.zeros((DEPTH, bsz, N_HEADS_A, HEAD_DIM, HEAD_DIM), state_wkv.dtype)
    y_prompt, p_shift, p_pool, p_wkv = _trunk(
        x_prompt, shift_zero, pool_zero, wkv_zero, 0,
        norm1_g, norm2_g, final_norm_g, mixer_w, w_ff1, w_ff2)
    y_sample, s_shift, s_pool, s_wkv = _trunk(
        x_sample, state_shift, state_pool, state_wkv, PAST_LEN,
        norm1_g, norm2_g, final_norm_g, mixer_w, w_ff1, w_ff2)
    return (y_prompt, y_sample, p_shift, p_pool, p_wkv, s_shift, s_pool, s_wkv)
```

```python
import contextlib
import numpy as np
import concourse.bass as bass
import concourse.mybir as mybir
from concourse.bass_utils import run_bass_kernel_spmd

F32 = mybir.dt.float32
BF16 = mybir.dt.bfloat16
AF = mybir.ActivationFunctionType
ALU = mybir.AluOpType

NCORES = 8
D = 1024
DA = 512
DSHIFT = 1792
DIN = 4352
DFF = 4096
NT = 256
NSLOT = 8
NPIECE = 114
NPV = 70
RMS_EPS = 1e-6
GN_EPS = 64 * 1e-5
DEC_SCALE = float(np.exp(-0.5))
WIN_ORDER = [12, 13] + list(range(12)) + [14, 15, 16, 17]


class Op:
    __slots__ = ("eng", "fn", "reads", "writes", "dma", "idx", "deps", "sem", "val", "has_dep")

    def __init__(self, eng, fn, reads, writes, dma):
        self.eng = eng
        self.fn = fn
        self.reads = reads
        self.writes = writes
        self.dma = dma
        self.deps = ()
        self.sem = None
        self.val = 0
        self.has_dep = False


class Prog:
    ENGS = ("pe", "act", "dve", "pool", "sp")
    NDMA = 8

    def __init__(self, nc):
        self.nc = nc
        self.ops = []
        self.dry = False

    def add(self, eng, fn, reads=(), writes=(), dma=False):
        if self.dry:
            return None
        op = Op(eng, fn, tuple(reads), tuple(writes), dma)
        op.idx = len(self.ops)
        self.ops.append(op)
        return op

    def finish(self, final_keys):
        nc = self.nc
        ops = self.ops
        last_w = {}
        readers = {}
        for op in ops:
            deps = set()
            for k in op.reads:
                if k in last_w:
                    deps.add(last_w[k])
            for k in op.writes:
                if k in last_w:
                    deps.add(last_w[k])
                latest = {}
                for r in readers.get(k, ()):
                    ro = ops[r]
                    if ro.dma:
                        deps.add(r)
                    elif ro.eng not in latest or latest[ro.eng] < r:
                        latest[ro.eng] = r
                deps.update(latest.values())
            deps.discard(op.idx)
            op.deps = deps
            for k in op.writes:
                last_w[k] = op.idx
                readers[k] = []
            for k in op.reads:
                readers.setdefault(k, []).append(op.idx)
        final_deps = set()
        for op in ops:
            if op.dma and any(k in final_keys for k in op.writes):
                final_deps.add(op.idx)
        for op in ops:
            for d in op.deps:
                if op.eng == "pe" and ops[d].eng == "pe" and not ops[d].dma and not op.dma:
                    continue
                ops[d].has_dep = True
        with contextlib.ExitStack() as st:
            sem_eng = {e: st.enter_context(nc.semaphore("s_" + e)) for e in ("pe", "act", "dve", "pool")}
            dma_sems = {e: [st.enter_context(nc.semaphore("d_%s%d" % (e, i))) for i in range(self.NDMA)]
                        for e in ("sp", "pool")}
            cnt = {e: 0 for e in self.ENGS}
            dcnt = {e: 0 for e in dma_sems}
            dvals = {e: [0] * self.NDMA for e in dma_sems}
            dlast = {e: [None] * self.NDMA for e in dma_sems}
            for op in ops:
                if op.dma:
                    i = dcnt[op.eng]
                    dcnt[op.eng] += 1
                    slot = i % self.NDMA
                    prev = dlast[op.eng][slot]
                    if prev is not None:
                        op.deps = set(op.deps) | {prev}
                    dvals[op.eng][slot] += 16
                    op.sem = dma_sems[op.eng][slot]
                    op.val = dvals[op.eng][slot]
                    dlast[op.eng][slot] = op.idx
                elif op.has_dep:
                    cnt[op.eng] += 1
                    op.sem = sem_eng[op.eng]
                    op.val = cnt[op.eng]
            self.sem_counts = dict(cnt)
            per_eng = {e: [o for o in ops if o.eng == e] for e in self.ENGS}
            with nc.Block() as block:
                def run(e, engobj):
                    waited = {}
                    for op in per_eng[e]:
                        need = {}
                        for d in op.deps:
                            p = ops[d]
                            if e == "pe" and p.eng == "pe" and not p.dma and not op.dma:
                                continue
                            key = id(p.sem)
                            if waited.get(key, 0) >= p.val:
                                continue
                            if key not in need or need[key][1] < p.val:
                                need[key] = (p.sem, p.val)
                        for key, (s, v) in need.items():
                            engobj.wait_ge(s, v)
                            waited[key] = v
                        ins = op.fn(engobj)
                        if op.sem is not None:
                            ins.then_inc(op.sem, 16 if op.dma else 1)
                    return waited

                @block.sync
                def _(eng):
                    waited = run("sp", eng)
                    need = {}
                    for d in final_deps:
                        p = ops[d]
                        key = id(p.sem)
                        if waited.get(key, 0) >= p.val:
                            continue
                        if key not in need or need[key][1] < p.val:
                            need[key] = (p.sem, p.val)
                    for key, (s, v) in need.items():
                        eng.wait_ge(s, v)

                @block.tensor
                def _(eng):
                    run("pe", eng)

                @block.scalar
                def _(eng):
                    run("act", eng)

                @block.vector
                def _(eng):
                    run("dve", eng)

                @block.gpsimd
                def _(eng):
                    run("pool", eng)


class Geo:
    def __init__(self, Nt, G, sample, first, last, tok0):
        self.Nt = Nt
        self.G = G
        self.L = Nt // G
        self.nch = Nt // 128
        self.sample = sample
        self.first = first
        self.last = last
        self.tok0 = tok0


def build_program(TP):
    nc = bass.Bass("TRN2", target_bir_lowering=False)
    n_pt = TP // NT
    P = Prog(nc)

    def din(name, shape):
        return nc.dram_tensor(name, shape, F32, kind="ExternalInput").ap()

    def dout(name, shape):
        return nc.dram_tensor(name, shape, F32, kind="ExternalOutput").ap()

    xp = din("xp", [8, 128, TP])
    xs = din("xs", [8, 128, 128])
    st_shift = din("st_shift", [2, 128, 14, 16])
    st_pool = din("st_pool", [2, 128, 4, 16, 15])
    st_wkv = din("st_wkv", [2, 4, 128, 16, 64])
    pvd = din("pv", [2, 128, NPV])
    wsmall_d = din("wsmall", [2, 128, 2048])
    wstream_d = din("wstream", [2, NPIECE, 128, 1024])
    yp = dout("yp", [8, 128, TP])
    ys = dout("ys", [8, 128, 128])
    o_shift_p = dout("o_shift_p", [2, 128, 14])
    o_shift_s = dout("o_shift_s", [2, 128, 14, 16])
    o_pool_p = dout("o_pool_p", [2, 128, 4, 15])
    o_pool_s = dout("o_pool_s", [2, 128, 4, 16, 15])
    o_wkv_p = dout("o_wkv_p", [2, 128, 4, 64])
    o_wkv_s = dout("o_wkv_s", [2, 4, 128, 16, 64])
    OUT_KEYS = {"yp", "ys", "o_shift_p", "o_shift_s", "o_pool_p", "o_pool_s", "o_wkv_p", "o_wkv_s"}

    def sb(name, shape, dt=F32):
        return nc.alloc_sbuf_tensor(name, shape, dt)

    ident = sb("ident", [128, 128], BF16)
    onesmean = sb("onesmean", [128, 128], BF16)
    blockmean = sb("blockmean", [128, 128], BF16)
    blockones = sb("blockones", [128, 128], BF16)
    m_su = [sb("m_su%d" % i, [128, 128], BF16) for i in range(2)]
    m_iu = [sb("m_iu%d" % i, [128, 128], BF16) for i in range(2)]
    m_sl = [sb("m_sl%d" % i, [128, 128], BF16) for i in range(2)]
    qmask = sb("qmask", [128, 16], BF16)
    qsel = sb("qsel", [128, 16, 128], BF16)
    rmask = [sb("rmask0", [128, NT]), sb("rmask1", [128, 128])]
    rcs = sb("rcs", [128, 4, 16])
    epsc = sb("epsc", [128, 4])
    pv = sb("pvt", [128, 2, NPV])
    omka = sb("omka", [128, 2, 4])
    halfp = sb("halfp", [128, 2, 8])
    mhalf = sb("mhalf", [128, NT])
    wsm = sb("wsm", [128, 2, 2048], BF16)
    x2 = [sb("x%d" % i, [128, 8, NT]) for i in range(2)]
    hT2 = [sb("hT%d" % i, [128, 8, NT], BF16) for i in range(2)]
    xsq2 = [[sb("xsq%d_%d" % (ph, i), [128, NT], BF16) for i in range(2)] for ph in range(2)]
    rstd2 = [sb("rstd%d" % ph, [128, NT]) for ph in range(2)]
    sdt2 = [sb("sdt%d" % ph, [128, NT]) for ph in range(2)]
    ZW = NT + 1
    UW = 16 * 23
    zbuf = sb("zbuf", [128, 14, ZW])
    ubuf = sb("ubuf", [128, 4, UW])
    stg_z = sb("stg_z", [128, 14, 16])
    carry_z = sb("carry_z", [128, 2, 14])
    carry_u = sb("carry_u", [128, 2, 4, 15])
    Hst = sb("Hst", [128, 2, 4, 64])
    Hpad = sb("Hpad", [128, 2, 4, 128], BF16)
    tl = sb("tl", [128, NT], BF16)
    sgl = sb("sgl", [128, NT], BF16)
    tnames = ["td", "r_c", "k_c", "v_c", "sg", "csg", "cprev", "a_in", "ssd", "rn", "kkn", "bsc",
              "tmpa", "kmod", "Epv", "Emn", "Epl"]
    T = {n: sb("t_" + n, [128, NT]) for n in tnames}
    TB = {n: sb("tb_" + n, [128, NT], BF16) for n in ["kksq", "rkb", "yb", "ycsq"]}
    aT = sb("aT", [128, 4, NT], BF16)
    bT = sb("bT", [128, 4, NT], BF16)
    kT = sb("kT", [128, 4, NT], BF16)
    rT = sb("rT", [128, 4, NT], BF16)
    vb = sb("vb", [128, 4, NT], BF16)
    aTp = sb("aTp", [128, 4, 2, NT], BF16)
    rTp = sb("rTp", [128, 4, 2, NT], BF16)
    g_all = sb("g_all", [128, 4, NT])
    bonus = sb("bonus", [128, 4, NT])
    WC = sb("WC", [128, 4, 16])
    yT = sb("yT", [128, 4, NT])
    amix2 = [sb("amix%d" % i, [128, 4, NT], BF16) for i in range(2)]
    NCH = NT // 128
    Btok = sb("Btok", [128, NCH, 512], BF16)
    Ktok = sb("Ktok", [128, NCH, 512], BF16)
    Vtok = sb("Vtok", [128, NCH, 512], BF16)
    Vpad = sb("Vpad", [128, NCH, 8, 128], BF16)
    ArbT = sb("ArbT", [128, 4, 128], BF16)
    LakT = sb("LakT", [128, 4, 128], BF16)
    ArkT = sb("ArkT", [128, 4, 128], BF16)
    Mm = [sb("Mm%d" % i, [128, 4, 128], BF16) for i in range(2)]
    Nm = [sb("Nm%d" % i, [128, 4, 128], BF16) for i in range(2)]
    Qm = [sb("Qm%d" % i, [128, 4, 128], BF16) for i in range(2)]
    R0 = sb("R0", [128, 4, 64], BF16)
    Usb = sb("Usb", [128, 4, 64], BF16)
    Upad = sb("Upad", [128, 4, 128], BF16)
    tmpH = sb("tmpH", [128, 2, 64])
    H0s = sb("H0s", [128, 16, 64])
    Hpad_s = sb("Hpad_s", [128, 16, 128], BF16)
    Hout_s = H0s
    Uexp = sb("Uexp", [128, 16, 128], BF16)
    Vexp = sb("Vexp", [128, 16, 128], BF16)
    aTm = sb("aTm", [128, 16, 128], BF16)
    sA = sb("sA", [128, UW])
    sB = sb("sB", [128, UW])
    p_in = sb("p_in", [128, 4, NT], BF16)
    pmix2 = [sb("pmix%d" % i, [128, 4, NT], BF16) for i in range(2)]
    tmpp = sb("tmpp", [128, NT])
    ga = sb("ga", [128, NT])
    gb = sb("gb", [128, NT])
    merged = sb("merged", [128, 8, NT], BF16)
    rtmp = [sb("rtmp%d" % i, [128, NT]) for i in range(2)]
    NHID = 8
    hid = sb("hid", [128, NHID, NT], BF16)
    wslots = [sb("wslot%d" % i, [128, 8, 128], BF16) for i in range(NSLOT)]
    banks = [nc.alloc_psum_tensor("bank%d" % i, [128, 512], F32) for i in range(8)]
    bank_ctr = [0, 0]

    phase = [0]
    BANK_POOLS = ([0, 1, 2, 3], [4, 5, 6])

    def nb():
        ph = phase[0]
        pool_ = BANK_POOLS[ph]
        b = pool_[bank_ctr[ph] % len(pool_)]
        bank_ctr[ph] += 1
        return banks[b], ("ps", b)

    def dve(name, *a, reads, writes, **kw):
        P.add("dve", lambda e: getattr(e, name)(*a, **kw), reads, writes)

    def act(out, in_, func, reads, writes, bias=0.0, scale=1.0):
        P.add("act", lambda e: e.activation(out, in_, func, bias=bias, scale=scale), reads, writes)

    def pool(name, *a, reads, writes, **kw):
        P.add("pool", lambda e: getattr(e, name)(*a, **kw), reads, writes)

    def mm(out, lhsT, rhs, start, stop, reads, writes):
        P.add("pe", lambda e: e.matmul(out, lhsT, rhs, start=start, stop=stop), reads, writes)

    def dma(q, out, in_, reads, writes):
        P.add(q, lambda e: e.dma_start(out=out, in_=in_), reads, writes, dma=True)

    def bc(ap2d, n):
        return ap2d.unsqueeze(1).to_broadcast([ap2d.shape[0], n, ap2d.shape[1]])

    tiles = [Geo(NT, 1, False, ti == 0, ti == n_pt - 1, ti * NT) for ti in range(n_pt)]
    tiles.append(Geo(128, 16, True, False, False, 0))
    wst = {"issued": 0, "pos": 0, "order": [], "mode": "record"}
    seq = []

    def w_take(l, j):
        if wst["mode"] == "record" or P.dry:
            wst["order"].append((l, j))
            wst["pos"] = wst.get("pos", 0) + 1
            return wslots[0], ("ws", 0)
        i = wst["pos"]
        lim = min(len(seq), i + NSLOT - 1)
        while wst["issued"] < lim:
            ii = wst["issued"]
            ll, jj = seq[ii]
            dma("pool", wslots[ii % NSLOT][:].rearrange("p k n -> p (k n)"), wstream_d[ll, jj],
                reads=[], writes=[("ws", ii % NSLOT)])
            wst["issued"] += 1
        assert seq[i] == (l, j), (seq[i], l, j)
        wst["pos"] = i + 1
        return wslots[i % NSLOT], ("ws", i % NSLOT)

    pool("memset", ident[:], 0.0, reads=[], writes=["ident"])
    pool("affine_select", ident[:], ident[:], pattern=[[-1, 128]], compare_op=ALU.not_equal, fill=1.0,
         base=0, channel_multiplier=1, reads=["ident"], writes=["ident"])
    pool("memset", onesmean[:], 1.0 / 1024.0, reads=[], writes=["onesmean"])
    for nm, tt, val in (("blockmean", blockmean, 1.0 / 64.0), ("blockones", blockones, 1.0)):
        pool("memset", tt[:], val, reads=[], writes=[nm])
        v3 = tt[:].rearrange("p (a b) -> p a b", b=64)
        pool("affine_select", v3, v3, pattern=[[-64, 2], [0, 64]], compare_op=ALU.is_ge, fill=0.0,
             base=0, channel_multiplier=1, reads=[nm], writes=[nm])
        pool("affine_select", v3, v3, pattern=[[64, 2], [0, 64]], compare_op=ALU.is_ge, fill=0.0,
             base=63, channel_multiplier=-1, reads=[nm], writes=[nm])
    for i in range(2):
        for nm, tt, cmp_, cm in (("m_su", m_su[i], ALU.is_gt, -1), ("m_iu", m_iu[i], ALU.is_ge, -1),
                                 ("m_sl", m_sl[i], ALU.is_gt, 1)):
            key = nm + str(i)
            pool("memset", tt[:], 1.0, reads=[], writes=[key])
            pool("affine_select", tt[:], tt[:], pattern=[[-cm, 128]], compare_op=cmp_, fill=0.0,
                 base=0, channel_multiplier=cm, reads=[key], writes=[key])
            if i == 1:
                v3 = tt[:].rearrange("p (q r) -> p q r", r=8)
                pool("affine_select", v3, v3, pattern=[[-8, 16], [0, 8]], compare_op=ALU.is_ge, fill=0.0,
                     base=0, channel_multiplier=1, reads=[key], writes=[key])
                pool("affine_select", v3, v3, pattern=[[8, 16], [0, 8]], compare_op=ALU.is_ge, fill=0.0,
                     base=7, channel_multiplier=-1, reads=[key], writes=[key])
    pool("memset", qmask[:], 1.0, reads=[], writes=["qmask"])
    pool("affine_select", qmask[:], qmask[:], pattern=[[-8, 16]], compare_op=ALU.is_ge, fill=0.0,
         base=0, channel_multiplier=1, reads=["qmask"], writes=["qmask"])
    pool("affine_select", qmask[:], qmask[:], pattern=[[8, 16]], compare_op=ALU.is_ge, fill=0.0,
         base=7, channel_multiplier=-1, reads=["qmask"], writes=["qmask"])
    pool("memset", qsel[:], 0.0, reads=[], writes=["qsel"])
    for q in range(16):
        pool("memset", qsel[:, q, 8 * q:8 * q + 8], 1.0, reads=["qsel"], writes=["qsel"])
    for i, (tt, per) in enumerate(((rmask[0], 128), (rmask[1], 8))):
        pool("memset", tt[:], 1.0, reads=[], writes=["rmask%d" % i])
        pool("memset", tt[:].rearrange("p (a b) -> p a b", b=per)[:, :, 0:1], 0.0,
             reads=["rmask%d" % i], writes=["rmask%d" % i])
    for gi in range(4):
        win = 2 << gi
        pool("memset", rcs[:, gi, :], 1.0 / win, reads=[], writes=["rcs"])
        for t in range(win - 1):
            pool("memset", rcs[:, gi, t:t + 1], 1.0 / (t + 1), reads=["rcs"], writes=["rcs"])
    pool("memset", epsc[:, 0:1], RMS_EPS, reads=[], writes=["epsc"])
    pool("memset", epsc[:, 1:2], GN_EPS, reads=["epsc"], writes=["epsc"])
    pool("memset", epsc[:, 2:3], 0.0, reads=["epsc"], writes=["epsc"])
    pool("memset", mhalf[:], -0.5, reads=[], writes=["mhalf"])
    pool("memset", carry_z[:], 0.0, reads=[], writes=["carry_z"])
    pool("memset", carry_u[:], 0.0, reads=[], writes=["carry_u"])
    pool("memset", Hst[:], 0.0, reads=[], writes=["Hst0", "Hst1"])
    pool("memset", Hpad[:], 0.0, reads=[], writes=["Hpad0", "Hpad1"])
    pool("memset", Hpad_s[:], 0.0, reads=[], writes=["Hpad_s"])
    pool("memset", Upad[:], 0.0, reads=[], writes=["Upad"])
    pool("memset", aTp[:], 0.0, reads=[], writes=[("aTp", c) for c in range(4)])
    pool("memset", rTp[:], 0.0, reads=[], writes=[("rTp", c) for c in range(4)])
    pool("memset", Vpad[:], 0.0, reads=[], writes=["Vpad"])
    dma("sp", pv[:], pvd.rearrange("l p n -> p l n"), reads=[], writes=["pv"])
    dma("pool", wsm[:], wsmall_d.rearrange("l p n -> p l n"), reads=[], writes=["wsm"])
    dve("tensor_scalar", omka[:], pv[:, :, 42:46], -1.0, 1.0, ALU.mult, ALU.add, reads=["pv"], writes=["omka"])
    dve("tensor_scalar_mul", halfp[:], pv[:, :, 30:38], 0.5, reads=["pv"], writes=["halfp"])

    def pcol(l, col):
        return pv[:, l, col:col + 1]

    def rmsnorm(geo, l, gcol0, out_bf16, pp, ph):
        Nt = geo.Nt
        x, hT, xsq, rstd, sdt = x2[pp], hT2[pp], xsq2[ph], rstd2[ph], sdt2[ph]
        bank, bk = nb()
        for kc in range(8):
            xs_ = xsq[kc % 2]
            act(xs_[:, :Nt], x[:, kc, :Nt], AF.Square, reads=[("x", pp, kc)], writes=[("xsq", ph, kc % 2)])
            mm(bank[:, :Nt], onesmean[:], xs_[:, :Nt], kc == 0, kc == 7,
               reads=["onesmean", ("xsq", ph, kc % 2)], writes=[bk])
        act(sdt[:, :Nt], bank[:, :Nt], AF.Identity, reads=[bk, "epsc"], writes=[("sdt", ph)], bias=epsc[:, 0:1])
        pool("tensor_tensor", rstd[:, :Nt], sdt[:, :Nt], mhalf[:, :Nt], ALU.pow, reads=[("sdt", ph), "mhalf"], writes=[("rstd", ph)])
        if out_bf16:
            for kc in range(8):
                dve("scalar_tensor_tensor", hT[:, kc, :Nt], x[:, kc, :Nt], pcol(l, gcol0 + kc), rstd[:, :Nt],
                    ALU.mult, ALU.mult, reads=[("x", pp, kc), ("rstd", ph), "pv"], writes=[("hT", pp, kc)])

    def big_mm(geo, slot, sk, nk, rhs_fn, rhs_keys):
        Nt = geo.Nt
        bank, bk = nb()
        for k in range(nk):
            mm(bank[:, :Nt], slot[:, k, :], rhs_fn(k), k == 0, k == nk - 1,
               reads=[sk] + rhs_keys(k), writes=[bk])
        return bank, bk

    def g3(ap2d, geo):
        return ap2d.rearrange("p (g t) -> p g t", g=geo.G)

    def zview(c, geo):
        return zbuf[:, c, 0:geo.G * (1 + geo.L)].rearrange("p (g t) -> p g t", g=geo.G)

    def uview(gi, geo):
        return ubuf[:, gi, 0:geo.G * (15 + geo.L)].rearrange("p (g t) -> p g t", g=geo.G)

    def shift(geo, l, c, out_ap, wkey, rows=slice(0, 128)):
        Nt, L = geo.Nt, geo.L
        zv = zview(c, geo)
        dve("tensor_tensor", g3(T["td"][rows, :Nt], geo), zv[rows, :, 0:L], zv[rows, :, 1:1 + L], ALU.subtract,
            reads=[("z", c)], writes=["td"])
        dve("scalar_tensor_tensor", g3(out_ap, geo), g3(T["td"][rows, :Nt], geo), pv[rows, l, 16 + c:17 + c],
            zv[rows, :, 1:1 + L], ALU.mult, ALU.add, reads=["td", ("z", c), "pv"], writes=[wkey])

    def front(geo, l, pp):
        Nt, G, L, nch = geo.Nt, geo.G, geo.L, geo.nch
        sm = 1 if geo.sample else 0
        Hk = "Hst%d" % l
        Hpk = "Hpad%d" % l
        x, hT, amix, pmix = x2[pp], hT2[pp], amix2[pp], pmix2[pp]
        if l == 0:
            xkeys = [("x", pp, k) for k in range(8)]
            if geo.sample:
                dma("sp", x[:, :, :Nt], xs.rearrange("k p t -> p k t"), reads=[], writes=xkeys)
            else:
                dma("sp", x[:, :, :Nt], xp.rearrange("k p t -> p k t")[:, :, geo.tok0:geo.tok0 + Nt],
                    reads=[], writes=xkeys)
        zall = [("z", c) for c in range(14)]
        uall = [("u", gi) for gi in range(4)]
        if geo.sample:
            dma("sp", stg_z[:], st_shift[l], reads=[], writes=["stg_z"])
            dve("tensor_copy", zbuf[:, :, 0:G * (1 + L)].rearrange("p c (g t) -> p c g t", g=G)[:, :, :, 0], stg_z[:],
                reads=["stg_z"], writes=zall)
            for gi in range(4):
                dma("sp", uview(gi, geo)[:, :, 0:15], st_pool[l, :, gi], reads=[], writes=[("u", gi)])
        else:
            dve("tensor_copy", zbuf[:, :, 0:1], carry_z[:, l, :].unsqueeze(2), reads=["carry_z"], writes=zall)
            dve("tensor_copy", ubuf[:, :, 0:15], carry_u[:, l, :, :], reads=["carry_u"], writes=uall)
        rmsnorm(geo, l, 0, True, pp, 0)
        yield
        hkeys = lambda k: [("hT", pp, k)]
        hrhs = lambda k: hT[:, k, :Nt]
        for j, c in enumerate(WIN_ORDER):
            slot, sk = w_take(l, j)
            bank, bk = big_mm(geo, slot, sk, 8, hrhs, hkeys)
            if c < 14:
                act(zview(c, geo)[:, :, 1:1 + L], g3(bank[:, :Nt], geo), AF.Copy, reads=[bk], writes=[("z", c)])
            else:
                gi = c - 14
                act(uview(gi, geo)[:, :, 15:15 + L], g3(bank[:, :Nt], geo), AF.Copy, reads=[bk], writes=[("u", gi)])
            yield
        if geo.sample:
            dve("tensor_copy", stg_z[:], zbuf[:, :, 0:G * (1 + L)].rearrange("p c (g t) -> p c g t", g=G)[:, :, :, L],
                reads=zall, writes=["stg_z"])
            dma("sp", o_shift_s[l], stg_z[:], reads=["stg_z"], writes=["o_shift_s"])
            for gi in range(4):
                dma("sp", o_pool_s[l, :, gi], uview(gi, geo)[:, :, L:L + 15], reads=[("u", gi)], writes=["o_pool_s"])
        else:
            dve("tensor_copy", carry_z[:, l, :].unsqueeze(2), zbuf[:, :, L:L + 1], reads=zall, writes=["carry_z"])
            dve("tensor_copy", carry_u[:, l, :, :], ubuf[:, :, L:L + 15], reads=uall, writes=["carry_u"])
            if geo.last:
                dma("sp", o_shift_p[l], carry_z[:, l, :], reads=["carry_z"], writes=["o_shift_p"])
                dma("sp", o_pool_p[l], carry_u[:, l, :, :], reads=["carry_u"], writes=["o_pool_p"])
        shift(geo, l, 12, T["r_c"][:, :Nt], "r_c")
        act(tl[0:64, :Nt], T["r_c"][0:64, :Nt], AF.Tanh, reads=["r_c"], writes=["tl"])
        act(tl[64:128, :Nt], T["r_c"][64:128, :Nt], AF.Copy, reads=["r_c"], writes=["tl"])
        shift(geo, l, 13, T["k_c"][:, :Nt], "k_c")
        yield
        act(T["tmpa"][:, :Nt], T["k_c"][:, :Nt], AF.Tanh, reads=["k_c"], writes=["tmpa"], scale=0.5)
        dve("tensor_scalar", sgl[:, :Nt], T["tmpa"][:, :Nt], 0.5, 0.5, ALU.mult, ALU.add, reads=["tmpa"], writes=["sgl"])
        for c in range(4):
            r_c, k_c, v_c = T["r_c"][:, :Nt], T["k_c"][:, :Nt], T["v_c"][:, :Nt]
            bank_d, bk_d = nb()
            mm(bank_d[:, :Nt], wsm[:, l, c * 128:(c + 1) * 128], tl[:, :Nt], True, True,
               reads=["wsm", "tl"], writes=[bk_d])
            bank_a, bk_a = nb()
            mm(bank_a[:, :Nt], wsm[:, l, 512 + c * 128:512 + (c + 1) * 128], tl[:, :Nt], True, True,
               reads=["wsm", "tl"], writes=[bk_a])
            bank_g, bk_g = nb()
            mm(bank_g[:, :Nt], wsm[:, l, 1024 + c * 128:1024 + (c + 1) * 128], sgl[:, :Nt], True, True,
               reads=["wsm", "sgl"], writes=[bk_g])
            shift(geo, l, 4 + c, T["k_c"][:, :Nt], "k_c")
            yield
            act(T["sg"][:, :Nt], bank_d[:, :Nt], AF.Tanh, reads=[bk_d, "halfp"], writes=["sg"], bias=halfp[:, l, c:c + 1], scale=0.5)
            act(T["a_in"][:, :Nt], bank_a[:, :Nt], AF.Tanh, reads=[bk_a, "halfp"], writes=["a_in"], bias=halfp[:, l, 4 + c:5 + c], scale=0.5)
            act(g_all[:, c, :Nt], bank_g[:, :Nt], AF.Copy, reads=[bk_g], writes=[("g_all", c)])
            act(TB["kksq"][:, :Nt], k_c, AF.Square, reads=["k_c", "pv"], writes=["kksq"], scale=pcol(l, 38 + c))
            bank_s, bk_s = nb()
            mm(bank_s[:, :Nt], blockones[:], TB["kksq"][:, :Nt], True, True, reads=["blockones", "kksq"], writes=[bk_s])
            yield
            dve("tensor_scalar", T["sg"][:, :Nt], T["sg"][:, :Nt], 0.5, 0.5, ALU.mult, ALU.add, reads=["sg"], writes=["sg"])
            dve("tensor_tensor_scan", T["csg"][:, :Nt], rmask[sm][:, :Nt], T["sg"][:, :Nt], 0.0, ALU.mult, ALU.add,
                reads=["sg", "rmask%d" % sm], writes=["csg"])
            dve("tensor_tensor", T["cprev"][:, :Nt], T["csg"][:, :Nt], T["sg"][:, :Nt], ALU.subtract,
                reads=["csg", "sg"], writes=["cprev"])
            dve("tensor_scalar_max", T["ssd"][:, :Nt], bank_s[:, :Nt], 1e-24, reads=[bk_s], writes=["ssd"])
            pool("tensor_tensor", T["rn"][:, :Nt], T["ssd"][:, :Nt], mhalf[:, :Nt], ALU.pow, reads=["ssd", "mhalf"], writes=["rn"])
            act(T["Epl"][:, :Nt], T["csg"][:, :Nt], AF.Exp, reads=["csg"], writes=["Epl"], scale=-DEC_SCALE)
            act(T["Emn"][:, :Nt], T["csg"][:, :Nt], AF.Exp, reads=["csg"], writes=["Emn"], scale=DEC_SCALE)
            act(T["Epv"][:, :Nt], T["cprev"][:, :Nt], AF.Exp, reads=["cprev"], writes=["Epv"], scale=-DEC_SCALE)
            dve("tensor_scalar", T["a_in"][:, :Nt], T["a_in"][:, :Nt], 0.5, 0.5, ALU.mult, ALU.add, reads=["a_in"], writes=["a_in"])
            shift(geo, l, c, T["r_c"][:, :Nt], "r_c")
            shift(geo, l, 8 + c, T["v_c"][:, :Nt], "v_c")
            yield
            dve("scalar_tensor_tensor", T["kkn"][:, :Nt], k_c, pcol(l, 38 + c), T["rn"][:, :Nt], ALU.mult, ALU.mult,
                reads=["k_c", "rn", "pv"], writes=["kkn"])
            dve("tensor_scalar", T["tmpa"][:, :Nt], T["a_in"][:, :Nt], pcol(l, 42 + c), omka[:, l, c:c + 1],
                ALU.mult, ALU.add, reads=["a_in", "pv", "omka"], writes=["tmpa"])
            dve("tensor_tensor", T["kmod"][:, :Nt], k_c, T["tmpa"][:, :Nt], ALU.mult,
                reads=["k_c", "tmpa"], writes=["kmod"])
            dve("tensor_tensor", T["bsc"][:, :Nt], T["kkn"][:, :Nt], T["a_in"][:, :Nt], ALU.mult,
                reads=["kkn", "a_in"], writes=["bsc"])
            dve("scalar_tensor_tensor", TB["rkb"][:, :Nt], r_c, pcol(l, 46 + c), T["kmod"][:, :Nt], ALU.mult, ALU.mult,
                reads=["r_c", "kmod", "pv"], writes=["rkb"])
            bank_b, bk_b = nb()
            mm(bank_b[:, :Nt], blockones[:], TB["rkb"][:, :Nt], True, True, reads=["blockones", "rkb"], writes=[bk_b])
            act(vb[:, c, :Nt], v_c, AF.Copy, reads=["v_c"], writes=[("vb", c)])
            yield
            ngr = G if geo.sample else nch
            per = L if geo.sample else 128
            dve("tensor_copy", WC[:, c, 0:ngr],
                T["Epl"][:, :Nt].rearrange("p (a b) -> p a b", b=per)[:, :, per - 1],
                reads=["Epl"], writes=[("WC", c)])
            dve("scalar_tensor_tensor", aT[:, c, :Nt], T["kkn"][:, :Nt], -1.0, T["Epv"][:, :Nt], ALU.mult, ALU.mult,
                reads=["kkn", "Epv"], writes=[("aT", c)])
            dve("tensor_tensor", bT[:, c, :Nt], T["bsc"][:, :Nt], T["Emn"][:, :Nt], ALU.mult,
                reads=["bsc", "Emn"], writes=[("bT", c)])
            dve("tensor_tensor", kT[:, c, :Nt], T["kmod"][:, :Nt], T["Emn"][:, :Nt], ALU.mult,
                reads=["kmod", "Emn"], writes=[("kT", c)])
            dve("tensor_tensor", rT[:, c, :Nt], r_c, T["Epl"][:, :Nt], ALU.mult,
                reads=["r_c", "Epl"], writes=[("rT", c)])
            dve("tensor_tensor", bonus[:, c, :Nt], bank_b[:, :Nt], v_c, ALU.mult, reads=[bk_b, "v_c"], writes=[("bonus", c)])
            for hh in range(2):
                pr = slice(hh * 64, hh * 64 + 64)
                act(aTp[pr, c, hh, :Nt], aT[pr, c, :Nt], AF.Copy, reads=[("aT", c)], writes=[("aTp", c)])
                act(rTp[pr, c, hh, :Nt], rT[pr, c, :Nt], AF.Copy, reads=[("rT", c)], writes=[("rTp", c)])
            yield
        for ci in range(nch):
            tok = slice(ci * 128, (ci + 1) * 128)
            for nm, src, dst in (("bT", bT, Btok), ("kT", kT, Ktok), ("vb", vb, Vtok)):
                bank, bk = nb()
                bb = bank[:].bitcast(BF16)
                for c in range(4):
                    P.add("pe", lambda e, bb=bb, src=src, c=c, tok=tok: e.transpose(bb[:, c * 128:(c + 1) * 128], src[:, c, tok], ident[:]),
                          reads=[(nm, c), "ident"], writes=[bk])
                act(dst[:, ci, :], bb[:, 0:512], AF.Copy, reads=[bk], writes=[(nm + "tok", ci)])
            v4 = Vtok[:, ci, :].rearrange("p (c h i) -> p c h i", c=4, h=2)
            vp = Vpad[:, ci, :, :].rearrange("p (c h) n -> p c h n", h=2)
            for hh in range(2):
                act(vp[:, :, hh, hh * 64:(hh + 1) * 64], v4[:, :, hh, :], AF.Copy,
                    reads=[("vbtok", ci)], writes=[("Vpad", ci)])
            yield
        Kfac = 3 if geo.sample else 7
        jobs = [[0], [1], [2], [3]] if geo.sample else [[0, 1], [2, 3]]
        for ci in range(nch):
            tok = slice(ci * 128, (ci + 1) * 128)
            bankY, bkY = banks[7], ("ps", 7)
            for cs in jobs:
                nh = 2 * len(cs)
                heads = [(c, hh) for c in cs for hh in range(2)]

                def five(lhs, lhs_nm, rhs, rhs_nm):
                    bank, bk = nb()
                    for hl, (c, hh) in enumerate(heads):
                        la = lhs[:, c, hh, tok] if lhs_nm in ("aTp", "rTp") else lhs[:, c, tok]
                        ra = rhs[:, c, hh, tok] if rhs_nm in ("aTp", "rTp") else rhs[:, c, tok]
                        mm(bank[:, hl * 128:(hl + 1) * 128], la, ra, True, True,
                           reads=[(lhs_nm, c), (rhs_nm, c)], writes=[bk])
                    return bank, bk

                def evm(dst, dkey, bank, bk, mask, mkey):
                    dve("tensor_tensor", dst[:, 0:nh, :], bank[:, 0:nh * 128].rearrange("p (h t) -> p h t", h=nh),
                        bc(mask[:], nh), ALU.mult, reads=[bk, mkey], writes=[dkey])

                bank, bk = five(bT, "bT", aTp, "aTp")
                evm(Nm[0], "Nm0", bank, bk, m_su[sm], "m_su%d" % sm)
                bank, bk = five(aTp, "aTp", bT, "bT")
                evm(Mm[0], "Mm0", bank, bk, m_sl[sm], "m_sl%d" % sm)
                bank, bk = five(bT, "bT", rTp, "rTp")
                evm(ArbT, "ArbT", bank, bk, m_iu[sm], "m_iu%d" % sm)
                bank, bk = five(kT, "kT", aTp, "aTp")
                evm(LakT, "LakT", bank, bk, m_su[sm], "m_su%d" % sm)
                bank, bk = five(kT, "kT", rTp, "rTp")
                evm(ArkT, "ArkT", bank, bk, m_iu[sm], "m_iu%d" % sm)
                yield
                dve("tensor_tensor", Qm[0][:, 0:nh, :], Nm[0][:, 0:nh, :], bc(ident[:], nh), ALU.add,
                    reads=["Nm0", "ident"], writes=["Qm0"])
                qi = 0
                for k in range(Kfac - 1):
                    a, b = k % 2, (k + 1) % 2
                    bankM, bkM = nb()
                    for hl in range(nh):
                        mm(bankM[:, hl * 128:(hl + 1) * 128], Nm[a][:, hl, :], Mm[a][:, hl, :], True, True,
                           reads=["Nm%d" % a, "Mm%d" % a], writes=[bkM])
                    if k < Kfac - 2:
                        bankN, bkN = nb()
                        for hl in range(nh):
                            mm(bankN[:, hl * 128:(hl + 1) * 128], Mm[a][:, hl, :], Nm[a][:, hl, :], True, True,
                               reads=["Nm%d" % a, "Mm%d" % a], writes=[bkN])
                    yield
                    act(Mm[b][:, 0:nh, :], bankM[:, 0:nh * 128].rearrange("p (h t) -> p h t", h=nh), AF.Copy,
                        reads=[bkM], writes=["Mm%d" % b])
                    if k < Kfac - 2:
                        act(Nm[b][:, 0:nh, :], bankN[:, 0:nh * 128].rearrange("p (h t) -> p h t", h=nh), AF.Copy,
                            reads=[bkN], writes=["Nm%d" % b])
                    yield
                    bank, bk = nb()
                    for hl in range(nh):
                        mm(bank[:, hl * 128:(hl + 1) * 128], Mm[b][:, hl, :], Qm[qi][:, hl, :], True, True,
                           reads=["Mm%d" % b, "Qm%d" % qi], writes=[bk])
                    yield
                    dve("tensor_tensor", Qm[1 - qi][:, 0:nh, :], bank[:, 0:nh * 128].rearrange("p (h t) -> p h t", h=nh),
                        Qm[qi][:, 0:nh, :], ALU.add, reads=[bk, "Qm%d" % qi], writes=["Qm%d" % (1 - qi)])
                    qi = 1 - qi
                Q = Qm[qi]
                Qk = "Qm%d" % qi
                if geo.sample:
                    c = cs[0]
                    dma("sp", H0s[:], st_wkv[l, c], reads=[], writes=["H0s"])
                    for hh in range(2):
                        pr = slice(hh * 64, hh * 64 + 64)
                        act(Hpad_s[pr, :, hh * 64:(hh + 1) * 64], H0s[pr, :, :], AF.Copy, reads=["H0s"], writes=["Hpad_s"])
                    dve("tensor_tensor", aTm[:], aT[:, c, 0:128].unsqueeze(1).to_broadcast([128, 16, 128]), qsel[:], ALU.mult,
                        reads=[("aT", c), "qsel"], writes=["aTm"])
                bankR, bkR = nb()
                for cl, c in enumerate(cs):
                    if geo.sample:
                        for q in range(16):
                            mm(bankR[:, cl * 128:(cl + 1) * 128], aTm[:, q, :], Hpad_s[:, q, :], q == 0, False,
                               reads=["aTm", "Hpad_s"], writes=[bkR])
                    else:
                        mm(bankR[:, cl * 128:(cl + 1) * 128], aT[:, c, tok], Hpad[:, l, c, :], True, False,
                           reads=[("aT", c), Hpk], writes=[bkR])
                    for hh in range(2):
                        hl = 2 * cl + hh
                        h = 2 * c + hh
                        mm(bankR[:, hl * 64:(hl + 1) * 64], LakT[:, hl, :], Vtok[:, ci, h * 64:(h + 1) * 64], False, hh == 1,
                           reads=["LakT", ("vbtok", ci)], writes=[bkR])
                act(R0[:, 0:nh, :], bankR[:, 0:nh * 64].rearrange("p (h i) -> p h i", h=nh), AF.Copy, reads=[bkR], writes=["R0"])
                bankU, bkU = nb()
                for hl in range(nh):
                    mm(bankU[:, hl * 64:(hl + 1) * 64], Q[:, hl, :], R0[:, hl, :], True, True, reads=[Qk, "R0"], writes=[bkU])
                act(Usb[:, 0:nh, :], bankU[:, 0:nh * 64].rearrange("p (h i) -> p h i", h=nh), AF.Copy, reads=[bkU], writes=["Usb"])
                for hl, (c, hh) in enumerate(heads):
                    dve("tensor_copy", Upad[:, hl, hh * 64:(hh + 1) * 64], bankU[:, hl * 64:(hl + 1) * 64],
                        reads=[bkU], writes=["Upad"])
                yield
                for cl, c in enumerate(cs):
                    first = True
                    if not geo.sample:
                        mm(bankY[:, c * 128:(c + 1) * 128], Hpad[:, l, c, :], rT[:, c, tok], True, False,
                           reads=[Hpk, ("rT", c)], writes=[bkY])
                        first = False
                    for hh in range(2):
                        hl = 2 * cl + hh
                        h = 2 * c + hh
                        mm(bankY[:, c * 128:(c + 1) * 128], Upad[:, hl, :], ArbT[:, hl, :], first, False,
                           reads=["Upad", "ArbT"], writes=[bkY])
                        first = False
                        mm(bankY[:, c * 128:(c + 1) * 128], Vpad[:, ci, h, :], ArkT[:, hl, :], False, (not geo.sample) and hh == 1,
                           reads=[("Vpad", ci), "ArkT"], writes=[bkY])
                    if geo.sample:
                        for q in range(16):
                            mm(bankY[:, c * 128 + q * 8:c * 128 + q * 8 + 8], Hpad_s[:, q, :], rT[:, c, q * 8:q * 8 + 8],
                               False, q == 15, reads=["Hpad_s", ("rT", c)], writes=[bkY])
                yield
                if not geo.sample:
                    bankS, bkS = nb()
                    for cl, c in enumerate(cs):
                        mm(bankS[:, cl * 128:(cl + 1) * 128], Btok[:, ci, c * 128:(c + 1) * 128],
                           Usb[:, 2 * cl:2 * cl + 2, :].rearrange("p h i -> p (h i)"), True, False,
                           reads=[("bTtok", ci), "Usb"], writes=[bkS])
                        mm(bankS[:, cl * 128:(cl + 1) * 128], Ktok[:, ci, c * 128:(c + 1) * 128],
                           Vtok[:, ci, c * 128:(c + 1) * 128], False, True,
                           reads=[("kTtok", ci), ("vbtok", ci)], writes=[bkS])
                    c0 = cs[0]
                    ncs = len(cs)
                    for hh in range(2):
                        pr = slice(hh * 64, hh * 64 + 64)
                        psv = bankS[pr, 0:ncs * 128].rearrange("p (c n) -> p c n", c=ncs)[:, :, hh * 64:(hh + 1) * 64]
                        dve("tensor_tensor", tmpH[pr, 0:ncs, :], psv, Hst[pr, l, c0:c0 + ncs, :], ALU.add,
                            reads=[bkS, Hk], writes=["tmpH"])
                        dve("tensor_tensor", Hst[pr, l, c0:c0 + ncs, :], tmpH[pr, 0:ncs, :],
                            WC[pr, c0:c0 + ncs, ci:ci + 1].to_broadcast([64, ncs, 64]), ALU.mult,
                            reads=["tmpH"] + [("WC", c) for c in cs], writes=[Hk])
                        act(Hpad[pr, l, c0:c0 + ncs, hh * 64:(hh + 1) * 64], Hst[pr, l, c0:c0 + ncs, :], AF.Copy,
                            reads=[Hk], writes=[Hpk])
                else:
                    c = cs[0]
                    dve("tensor_tensor", Uexp[:], Usb[:, 0:2, :].rearrange("p h i -> p (h i)").unsqueeze(1).to_broadcast([128, 16, 128]),
                        qmask[:].unsqueeze(2).to_broadcast([128, 16, 128]), ALU.mult, reads=["Usb", "qmask"], writes=["Uexp"])
                    dve("tensor_tensor", Vexp[:], Vtok[:, ci, c * 128:(c + 1) * 128].unsqueeze(1).to_broadcast([128, 16, 128]),
                        qmask[:].unsqueeze(2).to_broadcast([128, 16, 128]), ALU.mult, reads=[("vbtok", ci), "qmask"], writes=["Vexp"])
                    for nbk in range(4):
                        bankS, bkS = nb()
                        mm(bankS[:], Btok[:, ci, c * 128:(c + 1) * 128],
                           Uexp[:, 4 * nbk:4 * nbk + 4, :].rearrange("p q n -> p (q n)"), True, False,
                           reads=[("bTtok", ci), "Uexp"], writes=[bkS])
                        mm(bankS[:], Ktok[:, ci, c * 128:(c + 1) * 128],
                           Vexp[:, 4 * nbk:4 * nbk + 4, :].rearrange("p q n -> p (q n)"), False, True,
                           reads=[("kTtok", ci), "Vexp"], writes=[bkS])
                        for hh in range(2):
                            pr = slice(hh * 64, hh * 64 + 64)
                            psv = bankS[pr, :].rearrange("p (q n) -> p q n", q=4)[:, :, hh * 64:(hh + 1) * 64]
                            dve("tensor_tensor", Hout_s[pr, 4 * nbk:4 * nbk + 4, :], psv, H0s[pr, 4 * nbk:4 * nbk + 4, :], ALU.add,
                                reads=[bkS, "H0s"], writes=["H0s"])
                            dve("tensor_tensor", Hout_s[pr, 4 * nbk:4 * nbk + 4, :], Hout_s[pr, 4 * nbk:4 * nbk + 4, :],
                                WC[pr, c, 4 * nbk:4 * nbk + 4].unsqueeze(2).to_broadcast([64, 4, 64]), ALU.mult,
                                reads=["H0s", ("WC", c)], writes=["H0s"])
                    dma("sp", o_wkv_s[l, c], Hout_s[:], reads=["H0s"], writes=["o_wkv_s"])
            act(yT[:, :, tok], bankY[:, :].rearrange("p (c t) -> p c t", c=4), AF.Copy, reads=[bkY], writes=[("yT", ci)])
            yield
        if geo.last:
            dve("tensor_copy", H0s[:, 0:4, :], Hst[:, l, :, :], reads=[Hk], writes=["H0s"])
            dma("sp", o_wkv_p[l], H0s[:, 0:4, :], reads=["H0s"], writes=["o_wkv_p"])
        ytk = [("yT", ci) for ci in range(nch)]
        for c in range(4):
            act(TB["yb"][:, :Nt], yT[:, c, :Nt], AF.Copy, reads=ytk, writes=["yb"])
            bank, bk = nb()
            mm(bank[:, :Nt], blockmean[:], TB["yb"][:, :Nt], True, True, reads=["blockmean", "yb"], writes=[bk])
            dve("tensor_tensor", T["kkn"][:, :Nt], yT[:, c, :Nt], bank[:, :Nt], ALU.subtract, reads=ytk + [bk], writes=["kkn"])
            act(TB["ycsq"][:, :Nt], T["kkn"][:, :Nt], AF.Square, reads=["kkn"], writes=["ycsq"])
            bank, bk = nb()
            mm(bank[:, :Nt], blockmean[:], TB["ycsq"][:, :Nt], True, True, reads=["blockmean", "ycsq"], writes=[bk])
            act(T["ssd"][:, :Nt], bank[:, :Nt], AF.Identity, reads=[bk, "epsc"], writes=["ssd"], bias=epsc[:, 1:2])
            pool("tensor_tensor", T["rn"][:, :Nt], T["ssd"][:, :Nt], mhalf[:, :Nt], ALU.pow, reads=["ssd", "mhalf"], writes=["rn"])
            dve("tensor_tensor", T["bsc"][:, :Nt], T["kkn"][:, :Nt], T["rn"][:, :Nt], ALU.mult, reads=["kkn", "rn"], writes=["bsc"])
            dve("tensor_scalar", T["bsc"][:, :Nt], T["bsc"][:, :Nt], pcol(l, 50 + c), pcol(l, 54 + c), ALU.mult, ALU.add,
                reads=["bsc", "pv"], writes=["bsc"])
            dve("tensor_tensor", T["bsc"][:, :Nt], T["bsc"][:, :Nt], bonus[:, c, :Nt], ALU.add,
                reads=["bsc", ("bonus", c)], writes=["bsc"])
            dve("tensor_tensor", amix[:, c, :Nt], T["bsc"][:, :Nt], g_all[:, c, :Nt], ALU.mult,
                reads=["bsc", ("g_all", c)], writes=[("amix", pp, c)])
            yield
        W = G * (15 + L)
        for gi in range(4):
            uv = uview(gi, geo)
            sAv = sA[:, 0:W].rearrange("p (g t) -> p g t", g=G)
            sBv = sB[:, 0:W].rearrange("p (g t) -> p g t", g=G)
            E = 15 + L
            src, srck = uv, ("u", gi)
            bufs = [(sAv, "sA"), (sBv, "sB")]
            nsteps = gi + 1
            for s in range(nsteps):
                d = 1 << s
                lo = 15 if s == nsteps - 1 else (2 << s) - 1
                lo = max(lo, (2 << s) - 1) if s < nsteps - 1 else 15
                dst, dstk = bufs[s % 2]
                dve("tensor_tensor", dst[:, :, lo:E], src[:, :, lo:E], src[:, :, lo - d:E - d], ALU.add,
                    reads=[srck], writes=[dstk])
                src, srck = dst, dstk
            ws_ = src[:, :, 15:E]
            win = 2 << gi
            dve("scalar_tensor_tensor", g3(p_in[:, gi, :Nt], geo), ws_, 1.0 / win, uv[:, :, 15:E], ALU.mult, ALU.subtract,
                reads=[srck, ("u", gi)], writes=[("p_in", gi)])
            if geo.first:
                dve("tensor_tensor", tmpp[:, 0:16], src[:, 0, 15:31], rcs[:, gi, :], ALU.mult,
                    reads=[srck, "rcs"], writes=["tmpp"])
                dve("tensor_tensor", p_in[:, gi, 0:16], tmpp[:, 0:16], uv[:, 0, 15:31], ALU.subtract,
                    reads=["tmpp", ("u", gi)], writes=[("p_in", gi)])
            bank, bk = nb()
            mm(bank[:, :Nt], wsm[:, l, 1536 + gi * 128:1536 + (gi + 1) * 128], p_in[:, gi, :Nt], True, True,
               reads=["wsm", ("p_in", gi)], writes=[bk])
            act(pmix[:, gi, :Nt], bank[:, :Nt], AF.Identity, reads=[bk, "pv"], writes=[("pmix", pp, gi)], scale=pcol(l, 58 + gi))
            yield

    def back(geo, l, pp):
        Nt, G, L, nch = geo.Nt, geo.G, geo.L, geo.nch
        x, hT, amix, pmix = x2[pp], hT2[pp], amix2[pp], pmix2[pp]
        hkeys = lambda k: [("hT", pp, k)]
        hrhs = lambda k: hT[:, k, :Nt]
        for m in range(8):
            slot, sk = w_take(l, 18 + 3 * m)
            bankA, bkA = big_mm(geo, slot, sk, 8, hrhs, hkeys)
            act(ga[:, :Nt], bankA[:, :Nt], AF.Tanh, reads=[bkA], writes=["ga"], scale=0.5)
            slot, sk = w_take(l, 18 + 3 * m + 1)
            bankB, bkB = big_mm(geo, slot, sk, 8, hrhs, hkeys)
            act(gb[:, :Nt], bankB[:, :Nt], AF.Tanh, reads=[bkB], writes=["gb"], scale=0.5)
            slot, sk = w_take(l, 18 + 3 * m + 2)
            banka, bka = nb()
            for c in range(4):
                mm(banka[:, :Nt], slot[:, c, :], amix[:, c, :Nt], c == 0, c == 3, reads=[sk, ("amix", pp, c)], writes=[bka])
            bankb, bkb = nb()
            for gi in range(4):
                mm(bankb[:, :Nt], slot[:, 4 + gi, :], pmix[:, gi, :Nt], gi == 0, gi == 3, reads=[sk, ("pmix", pp, gi)], writes=[bkb])
            dve("scalar_tensor_tensor", ga[:, :Nt], ga[:, :Nt], 1.0, banka[:, :Nt], ALU.add, ALU.mult, reads=["ga", bka], writes=["ga"])
            dve("scalar_tensor_tensor", gb[:, :Nt], gb[:, :Nt], 1.0, bankb[:, :Nt], ALU.add, ALU.mult, reads=["gb", bkb], writes=["gb"])
            dve("tensor_tensor", merged[:, m, :Nt], ga[:, :Nt], gb[:, :Nt], ALU.add, reads=["ga", "gb"], writes=[("merged", m)])
            yield
        for m in range(8):
            slot, sk = w_take(l, 42 + m)
            bank, bk = big_mm(geo, slot, sk, 8, lambda k: merged[:, k, :Nt], lambda k: [("merged", k)])
            dve("scalar_tensor_tensor", x[:, m, :Nt], bank[:, :Nt], 0.5, x[:, m, :Nt], ALU.mult, ALU.add,
                reads=[("x", pp, m), bk], writes=[("x", pp, m)])
        yield
        rmsnorm(geo, l, 8, True, pp, 1)
        yield
        pj = 50
        for fq in range(4):
            for fl in range(8):
                slot, sk = w_take(l, pj)
                pj += 1
                bank, bk = big_mm(geo, slot, sk, 8, hrhs, hkeys)
                rt = rtmp[fl % 2]
                act(rt[:, :Nt], bank[:, :Nt], AF.Relu, reads=[bk], writes=[("rtmp", fl % 2)])
                dve("tensor_tensor", hid[:, fl, :Nt], rt[:, :Nt], rt[:, :Nt], ALU.mult,
                    reads=[("rtmp", fl % 2)], writes=[("hid", fl)])
                yield
            for m in range(8):
                slot, sk = w_take(l, pj)
                pj += 1
                bank, bk = big_mm(geo, slot, sk, 8, lambda k: hid[:, k, :Nt], lambda k: [("hid", k)])
                dve("tensor_tensor", x[:, m, :Nt], x[:, m, :Nt], bank[:, :Nt], ALU.add, reads=[("x", pp, m), bk], writes=[("x", pp, m)])
        assert pj == NPIECE

    def back_full(geo, l, pp):
        yield from back(geo, l, pp)
        if l == 1:
            Nt = geo.Nt
            x = x2[pp]
            rmsnorm(geo, 0, 62, False, pp, 1)
            for kc in range(8):
                yt = rtmp[kc % 2]
                dve("scalar_tensor_tensor", yt[:, :Nt], x[:, kc, :Nt], pcol(0, 62 + kc), rstd2[1][:, :Nt], ALU.mult, ALU.mult,
                    reads=[("x", pp, kc), ("rstd", 1), "pv"], writes=[("rtmp", kc % 2)])
                if geo.sample:
                    dma("sp", ys[kc], yt[:, :Nt], reads=[("rtmp", kc % 2)], writes=["ys"])
                else:
                    dma("sp", yp[kc, :, geo.tok0:geo.tok0 + Nt], yt[:, :Nt], reads=[("rtmp", kc % 2)], writes=["yp"])
            yield

    def count_steps(gen):
        n = 0
        for _ in gen:
            n += 1
        return n

    def emit_all():
        bank_ctr[0] = 0
        bank_ctr[1] = 0
        wst["order"] = []
        nslots = 4 * ((len(tiles) - 1) // 2) + ((len(tiles) - 1) % 2) + 4
        for slot in range(nslots):
            active = []
            for i, geo in enumerate(tiles):
                st = slot - (4 * (i // 2) + (i % 2))
                if 0 <= st < 4:
                    l, isback = st // 2, st % 2
                    active.append((geo, l, i % 2, isback))
            gens = []
            for geo, l, pp, isback in active:
                key = (geo.sample, l, isback)
                if key not in step_cache:
                    was = P.dry
                    P.dry = True
                    saved = (list(bank_ctr), list(wst["order"]), wst.get("pos", 0))
                    phase[0] = isback
                    step_cache[key] = count_steps((back_full if isback else front)(geo, l, pp))
                    bank_ctr[0], bank_ctr[1] = saved[0]
                    wst["order"], wst["pos"] = saved[1], saved[2]
                    P.dry = was
                gens.append([(back_full if isback else front)(geo, l, pp), 0, step_cache[key], isback])
            while gens:
                gens.sort(key=lambda g: g[1] / g[2])
                g = gens[0]
                try:
                    phase[0] = g[3]
                    next(g[0])
                    g[1] += 1
                except StopIteration:
                    gens.remove(g)

    step_cache = {}
    P.dry = True
    wst["mode"] = "record"
    emit_all()
    seq = wst["order"]
    P.dry = False
    wst["mode"] = "emit"
    wst["issued"] = 0
    wst["pos"] = 0
    emit_all()
    assert wst["pos"] == len(seq)
    P.finish(OUT_KEYS)
    return nc, P


def _wstream(w_in, w_a_up, w_b_up, w_o, w_ff1, w_ff2):
    L = w_in.shape[0]
    out = np.empty((L, NPIECE, 128, 1024), np.float32)

    def pk(mat, ncol0):
        K = mat.shape[0] // 128
        return mat[:, ncol0:ncol0 + 128].reshape(K, 128, 128).transpose(1, 0, 2)

    for l in range(L):
        j = 0
        for c in WIN_ORDER:
            out[l, j] = pk(w_in[l], c * 128).reshape(128, 1024)
            j += 1
        for m in range(8):
            out[l, j] = pk(w_in[l], (18 + m) * 128).reshape(128, 1024)
            j += 1
            out[l, j] = pk(w_in[l], (26 + m) * 128).reshape(128, 1024)
            j += 1
            ab = np.concatenate([pk(w_a_up[l], m * 128), pk(w_b_up[l], m * 128)], axis=1)
            out[l, j] = ab.reshape(128, 1024)
            j += 1
        for m in range(8):
            out[l, j] = pk(w_o[l], m * 128).reshape(128, 1024)
            j += 1
        for fq in range(4):
            for fl in range(8):
                out[l, j] = pk(w_ff1[l], (fq * 8 + fl) * 128).reshape(128, 1024)
                j += 1
            for m in range(8):
                out[l, j] = pk(w_ff2[l][fq * 1024:(fq + 1) * 1024], m * 128).reshape(128, 1024)
                j += 1
        assert j == NPIECE
    return out


_CACHE = {}


def kernel(x_prompt, x_sample, state_shift, state_pool, state_wkv,
           norm1_g, w_in, mu_shift, decay0, w_decay2, a0, w_a2, w_g2, k_k, k_a, r_k,
           ln_x_g, ln_x_b, w_a_up, w_pool, pool_scale, w_b_up, w_o,
           norm2_g, w_ff1, w_ff2, final_norm_g):
    f = lambda a: np.ascontiguousarray(np.asarray(a, dtype=np.float32))
    x_prompt, x_sample, state_shift, state_pool, state_wkv = map(f, (x_prompt, x_sample, state_shift, state_pool, state_wkv))
    B, TP, _ = x_prompt.shape
    assert B == NCORES and TP % NT == 0
    L = 2

    def cols(v, n):
        return f(v).reshape(n, 128).T

    pvs = np.zeros((L, 128, NPV), np.float32)
    for l in range(L):
        pvs[l, :, 0:8] = cols(norm1_g[l], 8)
        pvs[l, :, 8:16] = cols(norm2_g[l], 8)
        pvs[l, :, 16:30] = cols(mu_shift[l], 14)
        pvs[l, :, 30:34] = cols(decay0[l], 4)
        pvs[l, :, 34:38] = cols(a0[l], 4)
        pvs[l, :, 38:42] = cols(k_k[l], 4)
        pvs[l, :, 42:46] = cols(k_a[l], 4)
        pvs[l, :, 46:50] = cols(f(r_k[l]).reshape(-1), 4)
        pvs[l, :, 50:54] = cols(ln_x_g[l], 4)
        pvs[l, :, 54:58] = cols(ln_x_b[l], 4)
        pvs[l, :, 58:62] = cols(pool_scale[l], 4)
        pvs[l, :, 62:70] = cols(final_norm_g, 8)
    wsmall = np.zeros((L, 128, 2048), np.float32)
    for l in range(L):
        wsmall[l, 0:64, 0:512] = f(w_decay2[l])
        wsmall[l, 64:128, 512:1024] = f(w_a2[l])
        wsmall[l, :, 1024:1536] = f(w_g2[l])
        wsmall[l, :, 1536:2048] = f(w_pool[l]).transpose(1, 0, 2).reshape(128, 512)
    wstream = _wstream(f(w_in), f(w_a_up), f(w_b_up), f(w_o), f(w_ff1), f(w_ff2))

    key = TP
    if key not in _CACHE:
        _CACHE[key] = build_program(TP)[0]
    nc = _CACHE[key]
    in_maps = []
    for c in range(NCORES):
        q0 = 16 * c
        xpc = np.ascontiguousarray(x_prompt[c].T.reshape(8, 128, TP))
        xsc = np.ascontiguousarray(x_sample[q0:q0 + 16].reshape(128, D).T.reshape(8, 128, 128))
        sh = np.ascontiguousarray(state_shift[:, q0:q0 + 16, :].reshape(L, 16, 14, 128).transpose(0, 3, 2, 1))
        pl = np.ascontiguousarray(state_pool[:, q0:q0 + 16].reshape(L, 16, 15, 4, 128).transpose(0, 4, 3, 1, 2))
        wk = state_wkv[:, q0:q0 + 16].reshape(L, 16, 4, 2, 64, 64).transpose(0, 2, 3, 5, 1, 4)
        wk = np.ascontiguousarray(wk.reshape(L, 4, 128, 16, 64))
        in_maps.append({"xp": xpc, "xs": xsc, "st_shift": sh, "st_pool": pl, "st_wkv": wk,
                        "pv": pvs, "wsmall": wsmall, "wstream": wstream})
    res = run_bass_kernel_spmd(nc, in_maps, core_ids=list(range(NCORES)))
    R = res.results
    y_prompt = np.stack([R[c]["yp"].reshape(D, TP).T for c in range(NCORES)])
    y_sample = np.concatenate([R[c]["ys"].reshape(D, 128).T.reshape(16, 8, D) for c in range(NCORES)])
    p_shift = np.stack([R[c]["o_shift_p"].reshape(L, 128, 14).transpose(0, 2, 1).reshape(L, DSHIFT) for c in range(NCORES)], axis=1)
    s_shift = np.concatenate([R[c]["o_shift_s"].reshape(L, 128, 14, 16).transpose(0, 3, 2, 1).reshape(L, 16, DSHIFT)
                              for c in range(NCORES)], axis=1)
    p_pool = np.stack([R[c]["o_pool_p"].reshape(L, 128, 4, 15).transpose(0, 3, 2, 1).reshape(L, 15, 512) for c in range(NCORES)], axis=1)
    s_pool = np.concatenate([R[c]["o_pool_s"].reshape(L, 128, 4, 16, 15).transpose(0, 3, 4, 2, 1).reshape(L, 16, 15, 512)
                             for c in range(NCORES)], axis=1)
    p_wkv = np.stack([R[c]["o_wkv_p"].reshape(L, 2, 64, 4, 64).transpose(0, 3, 1, 4, 2).reshape(L, 8, 64, 64)
                      for c in range(NCORES)], axis=1)
    s_wkv = np.concatenate([R[c]["o_wkv_s"].reshape(L, 4, 2, 64, 16, 64).transpose(0, 4, 1, 2, 5, 3).reshape(L, 16, 8, 64, 64)
                            for c in range(NCORES)], axis=1)
    out = (y_prompt, y_sample, p_shift, p_pool, p_wkv, s_shift, s_pool, s_wkv)
    return tuple(np.ascontiguousarray(o, dtype=np.float32) for o in out)
```

```python
import contextlib
import numpy as np
import concourse.bass as bass
import concourse.mybir as mybir
from concourse.bass_utils import run_bass_kernel_spmd

F32 = mybir.dt.float32
BF16 = mybir.dt.bfloat16
AF = mybir.ActivationFunctionType
ALU = mybir.AluOpType

NCORES = 8
D = 1024
DA = 512
DSHIFT = 1792
DIN = 4352
DFF = 4096
NT = 256
NSLOT = 8
NPIECE = 114
NPV = 70
RMS_EPS = 1e-6
GN_EPS = 64 * 1e-5
DEC_SCALE = float(np.exp(-0.5))
WIN_ORDER = [12, 13] + list(range(12)) + [14, 15, 16, 17]


class Op:
    __slots__ = ("eng", "fn", "reads", "writes", "dma", "idx", "deps", "sem", "val", "has_dep")

    def __init__(self, eng, fn, reads, writes, dma):
        self.eng = eng
        self.fn = fn
        self.reads = reads
        self.writes = writes
        self.dma = dma
        self.deps = ()
        self.sem = None
        self.val = 0
        self.has_dep = False


class Prog:
    ENGS = ("pe", "act", "dve", "pool", "sp")
    NDMA = 8

    def __init__(self, nc):
        self.nc = nc
        self.ops = []
        self.dry = False

    def add(self, eng, fn, reads=(), writes=(), dma=False):
        if self.dry:
            return None
        op = Op(eng, fn, tuple(reads), tuple(writes), dma)
        op.idx = len(self.ops)
        self.ops.append(op)
        return op

    def finish(self, final_keys):
        nc = self.nc
        ops = self.ops
        last_w = {}
        readers = {}
        for op in ops:
            deps = set()
            for k in op.reads:
                if k in last_w:
                    deps.add(last_w[k])
            for k in op.writes:
                if k in last_w:
                    deps.add(last_w[k])
                latest = {}
                for r in readers.get(k, ()):
                    ro = ops[r]
                    if ro.dma:
                        deps.add(r)
                    elif ro.eng not in latest or latest[ro.eng] < r:
                        latest[ro.eng] = r
                deps.update(latest.values())
            deps.discard(op.idx)
            op.deps = deps
            for k in op.writes:
                last_w[k] = op.idx
                readers[k] = []
            for k in op.reads:
                readers.setdefault(k, []).append(op.idx)
        final_deps = set()
        for op in ops:
            if op.dma and any(k in final_keys for k in op.writes):
                final_deps.add(op.idx)
        for op in ops:
            for d in op.deps:
                if op.eng == "pe" and ops[d].eng == "pe" and not ops[d].dma and not op.dma:
                    continue
                ops[d].has_dep = True
        with contextlib.ExitStack() as st:
            sem_eng = {e: st.enter_context(nc.semaphore("s_" + e)) for e in ("pe", "act", "dve", "pool")}
            dma_sems = {e: [st.enter_context(nc.semaphore("d_%s%d" % (e, i))) for i in range(self.NDMA)]
                        for e in ("sp", "pool")}
            cnt = {e: 0 for e in self.ENGS}
            dcnt = {e: 0 for e in dma_sems}
            dvals = {e: [0] * self.NDMA for e in dma_sems}
            dlast = {e: [None] * self.NDMA for e in dma_sems}
            for op in ops:
                if op.dma:
                    i = dcnt[op.eng]
                    dcnt[op.eng] += 1
                    slot = i % self.NDMA
                    prev = dlast[op.eng][slot]
                    if prev is not None:
                        op.deps = set(op.deps) | {prev}
                    dvals[op.eng][slot] += 16
                    op.sem = dma_sems[op.eng][slot]
                    op.val = dvals[op.eng][slot]
                    dlast[op.eng][slot] = op.idx
                elif op.has_dep:
                    cnt[op.eng] += 1
                    op.sem = sem_eng[op.eng]
                    op.val = cnt[op.eng]
            self.sem_counts = dict(cnt)
            per_eng = {e: [o for o in ops if o.eng == e] for e in self.ENGS}
            with nc.Block() as block:
                def run(e, engobj):
                    waited = {}
                    for op in per_eng[e]:
                        need = {}
                        for d in op.deps:
                            p = ops[d]
                            if e == "pe" and p.eng == "pe" and not p.dma and not op.dma:
                                continue
                            key = id(p.sem)
                            if waited.get(key, 0) >= p.val:
                                continue
                            if key not in need or need[key][1] < p.val:
                                need[key] = (p.sem, p.val)
                        for key, (s, v) in need.items():
                            engobj.wait_ge(s, v)
                            waited[key] = v
                        ins = op.fn(engobj)
                        if op.sem is not None:
                            ins.then_inc(op.sem, 16 if op.dma else 1)
                    return waited

                @block.sync
                def _(eng):
                    waited = run("sp", eng)
                    need = {}
                    for d in final_deps:
                        p = ops[d]
                        key = id(p.sem)
                        if waited.get(key, 0) >= p.val:
                            continue
                        if key not in need or need[key][1] < p.val:
                            need[key] = (p.sem, p.val)
                    for key, (s, v) in need.items():
                        eng.wait_ge(s, v)

                @block.tensor
                def _(eng):
                    run("pe", eng)

                @block.scalar
                def _(eng):
                    run("act", eng)

                @block.vector
                def _(eng):
                    run("dve", eng)

                @block.gpsimd
                def _(eng):
                    run("pool", eng)


class Geo:
    def __init__(self, Nt, G, sample, first, last, tok0):
        self.Nt = Nt
        self.G = G
        self.L = Nt // G
        self.nch = Nt // 128
        self.sample = sample
        self.first = first
        self.last = last
        self.tok0 = tok0


def build_program(TP):
    nc = bass.Bass("TRN2", target_bir_lowering=False)
    n_pt = TP // NT
    P = Prog(nc)

    def din(name, shape):
        return nc.dram_tensor(name, shape, F32, kind="ExternalInput").ap()

    def dout(name, shape):
        return nc.dram_tensor(name, shape, F32, kind="ExternalOutput").ap()

    xp = din("xp", [8, 128, TP])
    xs = din("xs", [8, 128, 128])
    st_shift = din("st_shift", [2, 128, 14, 16])
    st_pool = din("st_pool", [2, 128, 4, 16, 15])
    st_wkv = din("st_wkv", [2, 4, 128, 16, 64])
    pvd = din("pv", [2, 128, NPV])
    wsmall_d = din("wsmall", [2, 128, 2048])
    wstream_d = din("wstream", [2, NPIECE, 128, 1024])
    yp = dout("yp", [8, 128, TP])
    ys = dout("ys", [8, 128, 128])
    o_shift_p = dout("o_shift_p", [2, 128, 14])
    o_shift_s = dout("o_shift_s", [2, 128, 14, 16])
    o_pool_p = dout("o_pool_p", [2, 128, 4, 15])
    o_pool_s = dout("o_pool_s", [2, 128, 4, 16, 15])
    o_wkv_p = dout("o_wkv_p", [2, 128, 4, 64])
    o_wkv_s = dout("o_wkv_s", [2, 4, 128, 16, 64])
    OUT_KEYS = {"yp", "ys", "o_shift_p", "o_shift_s", "o_pool_p", "o_pool_s", "o_wkv_p", "o_wkv_s"}

    def sb(name, shape, dt=F32):
        return nc.alloc_sbuf_tensor(name, shape, dt)

    ident = sb("ident", [128, 128], BF16)
    onesmean = sb("onesmean", [128, 128], BF16)
    blockmean = sb("blockmean", [128, 128], BF16)
    blockones = sb("blockones", [128, 128], BF16)
    m_su = [sb("m_su%d" % i, [128, 128], BF16) for i in range(2)]
    m_iu = [sb("m_iu%d" % i, [128, 128], BF16) for i in range(2)]
    m_sl = [sb("m_sl%d" % i, [128, 128], BF16) for i in range(2)]
    qmask = sb("qmask", [128, 16], BF16)
    qsel = sb("qsel", [128, 16, 128], BF16)
    rmask = [sb("rmask0", [128, NT]), sb("rmask1", [128, 128])]
    rcs = sb("rcs", [128, 4, 16])
    epsc = sb("epsc", [128, 4])
    pv = sb("pvt", [128, 2, NPV])
    omka = sb("omka", [128, 2, 4])
    halfp = sb("halfp", [128, 2, 8])
    wsm = sb("wsm", [128, 2, 2048], BF16)
    x2 = [sb("x%d" % i, [128, 8, NT]) for i in range(2)]
    hT2 = [sb("hT%d" % i, [128, 8, NT], BF16) for i in range(2)]
    xsq2 = [[sb("xsq%d_%d" % (ph, i), [128, NT], BF16) for i in range(2)] for ph in range(2)]
    rstd2 = [sb("rstd%d" % ph, [128, NT]) for ph in range(2)]
    sdt2 = [sb("sdt%d" % ph, [128, NT]) for ph in range(2)]
    ZW = NT + 1
    UW = 16 * 23
    zbuf = sb("zbuf", [128, 14, ZW])
    ubuf = sb("ubuf", [128, 4, UW])
    stg_z = sb("stg_z", [128, 14, 16])
    carry_z = sb("carry_z", [128, 2, 14])
    carry_u = sb("carry_u", [128, 2, 4, 15])
    Hst = sb("Hst", [128, 2, 4, 64])
    Hpad = sb("Hpad", [128, 2, 4, 128], BF16)
    tl = sb("tl", [128, NT], BF16)
    sgl = sb("sgl", [128, NT], BF16)
    tnames = ["td", "r_c", "k_c", "v_c", "sg", "csg", "cprev", "a_in", "ssd", "rn", "kkn", "bsc",
              "tmpa", "kmod", "Epv", "Emn", "Epl"]
    T = {n: sb("t_" + n, [128, NT]) for n in tnames}
    TB = {n: sb("tb_" + n, [128, NT], BF16) for n in ["kksq", "rkb", "yb", "ycsq"]}
    aT = sb("aT", [128, 4, NT], BF16)
    bT = sb("bT", [128, 4, NT], BF16)
    kT = sb("kT", [128, 4, NT], BF16)
    rT = sb("rT", [128, 4, NT], BF16)
    vb = sb("vb", [128, 4, NT], BF16)
    aTp = sb("aTp", [128, 4, 2, NT], BF16)
    rTp = sb("rTp", [128, 4, 2, NT], BF16)
    g_all = sb("g_all", [128, 4, NT])
    bonus = sb("bonus", [128, 4, NT])
    WC = sb("WC", [128, 4, 16])
    yT = sb("yT", [128, 4, NT])
    amix2 = [sb("amix%d" % i, [128, 4, NT], BF16) for i in range(2)]
    NCH = NT // 128
    Btok = sb("Btok", [128, NCH, 512], BF16)
    Ktok = sb("Ktok", [128, NCH, 512], BF16)
    Vtok = sb("Vtok", [128, NCH, 512], BF16)
    Vpad = sb("Vpad", [128, NCH, 8, 128], BF16)
    ArbT = sb("ArbT", [128, 4, 128], BF16)
    LakT = sb("LakT", [128, 4, 128], BF16)
    ArkT = sb("ArkT", [128, 4, 128], BF16)
    Mm = [sb("Mm%d" % i, [128, 4, 128], BF16) for i in range(2)]
    Nm = [sb("Nm%d" % i, [128, 4, 128], BF16) for i in range(2)]
    Qm = [sb("Qm%d" % i, [128, 4, 128], BF16) for i in range(2)]
    R0 = sb("R0", [128, 4, 64], BF16)
    Usb = sb("Usb", [128, 4, 64], BF16)
    Upad = sb("Upad", [128, 4, 128], BF16)
    tmpH = sb("tmpH", [128, 2, 64])
    H0s = sb("H0s", [128, 16, 64])
    Hpad_s = sb("Hpad_s", [128, 16, 128], BF16)
    Hout_s = H0s
    Uexp = sb("Uexp", [128, 16, 128], BF16)
    Vexp = sb("Vexp", [128, 16, 128], BF16)
    aTm = sb("aTm", [128, 16, 128], BF16)
    sA = sb("sA", [128, UW])
    sB = sb("sB", [128, UW])
    p_in = sb("p_in", [128, 4, NT], BF16)
    pmix2 = [sb("pmix%d" % i, [128, 4, NT], BF16) for i in range(2)]
    tmpp = sb("tmpp", [128, NT])
    ga = sb("ga", [128, NT])
    gb = sb("gb", [128, NT])
    merged = sb("merged", [128, 8, NT], BF16)
    rtmp = [sb("rtmp%d" % i, [128, NT]) for i in range(2)]
    NHID = 8
    hid = sb("hid", [128, NHID, NT], BF16)
    wslots = [sb("wslot%d" % i, [128, 8, 128], BF16) for i in range(NSLOT)]
    banks = [nc.alloc_psum_tensor("bank%d" % i, [128, 512], F32) for i in range(8)]
    bank_ctr = [0, 0]

    phase = [0]
    BANK_POOLS = ([0, 1, 2, 3], [4, 5, 6])

    def nb():
        ph = phase[0]
        pool_ = BANK_POOLS[ph]
        b = pool_[bank_ctr[ph] % len(pool_)]
        bank_ctr[ph] += 1
        return banks[b], ("ps", b)

    def dve(name, *a, reads, writes, **kw):
        P.add("dve", lambda e: getattr(e, name)(*a, **kw), reads, writes)

    def act(out, in_, func, reads, writes, bias=0.0, scale=1.0):
        P.add("act", lambda e: e.activation(out, in_, func, bias=bias, scale=scale), reads, writes)

    def pool(name, *a, reads, writes, **kw):
        P.add("pool", lambda e: getattr(e, name)(*a, **kw), reads, writes)

    def mm(out, lhsT, rhs, start, stop, reads, writes):
        P.add("pe", lambda e: e.matmul(out, lhsT, rhs, start=start, stop=stop), reads, writes)

    def dma(q, out, in_, reads, writes):
        P.add(q, lambda e: e.dma_start(out=out, in_=in_), reads, writes, dma=True)

    def bc(ap2d, n):
        return ap2d.unsqueeze(1).to_broadcast([ap2d.shape[0], n, ap2d.shape[1]])

    tiles = [Geo(NT, 1, False, ti == 0, ti == n_pt - 1, ti * NT) for ti in range(n_pt)]
    tiles.insert(0, Geo(128, 16, True, False, False, 0))
    wst = {"issued": 0, "pos": 0, "order": [], "mode": "record"}
    seq = []

    def w_take(l, j):
        if wst["mode"] == "record" or P.dry:
            wst["order"].append((l, j))
            wst["pos"] = wst.get("pos", 0) + 1
            return wslots[0], ("ws", 0)
        i = wst["pos"]
        lim = min(len(seq), i + NSLOT - 1)
        while wst["issued"] < lim:
            ii = wst["issued"]
            ll, jj = seq[ii]
            dma("pool", wslots[ii % NSLOT][:].rearrange("p k n -> p (k n)"), wstream_d[ll, jj],
                reads=[], writes=[("ws", ii % NSLOT)])
            wst["issued"] += 1
        assert seq[i] == (l, j), (seq[i], l, j)
        wst["pos"] = i + 1
        return wslots[i % NSLOT], ("ws", i % NSLOT)

    pool("memset", ident[:], 0.0, reads=[], writes=["ident"])
    pool("affine_select", ident[:], ident[:], pattern=[[-1, 128]], compare_op=ALU.not_equal, fill=1.0,
         base=0, channel_multiplier=1, reads=["ident"], writes=["ident"])
    pool("memset", onesmean[:], 1.0 / 1024.0, reads=[], writes=["onesmean"])
    for nm, tt, val in (("blockmean", blockmean, 1.0 / 64.0), ("blockones", blockones, 1.0)):
        pool("memset", tt[:], val, reads=[], writes=[nm])
        v3 = tt[:].rearrange("p (a b) -> p a b", b=64)
        pool("affine_select", v3, v3, pattern=[[-64, 2], [0, 64]], compare_op=ALU.is_ge, fill=0.0,
             base=0, channel_multiplier=1, reads=[nm], writes=[nm])
        pool("affine_select", v3, v3, pattern=[[64, 2], [0, 64]], compare_op=ALU.is_ge, fill=0.0,
             base=63, channel_multiplier=-1, reads=[nm], writes=[nm])
    for i in range(2):
        for nm, tt, cmp_, cm in (("m_su", m_su[i], ALU.is_gt, -1), ("m_iu", m_iu[i], ALU.is_ge, -1),
                                 ("m_sl", m_sl[i], ALU.is_gt, 1)):
            key = nm + str(i)
            pool("memset", tt[:], 1.0, reads=[], writes=[key])
            pool("affine_select", tt[:], tt[:], pattern=[[-cm, 128]], compare_op=cmp_, fill=0.0,
                 base=0, channel_multiplier=cm, reads=[key], writes=[key])
            if i == 1:
                v3 = tt[:].rearrange("p (q r) -> p q r", r=8)
                pool("affine_select", v3, v3, pattern=[[-8, 16], [0, 8]], compare_op=ALU.is_ge, fill=0.0,
                     base=0, channel_multiplier=1, reads=[key], writes=[key])
                pool("affine_select", v3, v3, pattern=[[8, 16], [0, 8]], compare_op=ALU.is_ge, fill=0.0,
                     base=7, channel_multiplier=-1, reads=[key], writes=[key])
    pool("memset", qmask[:], 1.0, reads=[], writes=["qmask"])
    pool("affine_select", qmask[:], qmask[:], pattern=[[-8, 16]], compare_op=ALU.is_ge, fill=0.0,
         base=0, channel_multiplier=1, reads=["qmask"], writes=["qmask"])
    pool("affine_select", qmask[:], qmask[:], pattern=[[8, 16]], compare_op=ALU.is_ge, fill=0.0,
         base=7, channel_multiplier=-1, reads=["qmask"], writes=["qmask"])
    pool("memset", qsel[:], 0.0, reads=[], writes=["qsel"])
    for q in range(16):
        pool("memset", qsel[:, q, 8 * q:8 * q + 8], 1.0, reads=["qsel"], writes=["qsel"])
    for i, (tt, per) in enumerate(((rmask[0], 128), (rmask[1], 8))):
        pool("memset", tt[:], 1.0, reads=[], writes=["rmask%d" % i])
        pool("memset", tt[:].rearrange("p (a b) -> p a b", b=per)[:, :, 0:1], 0.0,
             reads=["rmask%d" % i], writes=["rmask%d" % i])
    for gi in range(4):
        win = 2 << gi
        pool("memset", rcs[:, gi, :], 1.0 / win, reads=[], writes=["rcs"])
        for t in range(win - 1):
            pool("memset", rcs[:, gi, t:t + 1], 1.0 / (t + 1), reads=["rcs"], writes=["rcs"])
    pool("memset", epsc[:, 0:1], RMS_EPS, reads=[], writes=["epsc"])
    pool("memset", epsc[:, 1:2], GN_EPS, reads=["epsc"], writes=["epsc"])
    pool("memset", epsc[:, 2:3], 0.0, reads=["epsc"], writes=["epsc"])
    pool("memset", carry_z[:], 0.0, reads=[], writes=["carry_z"])
    pool("memset", carry_u[:], 0.0, reads=[], writes=["carry_u"])
    pool("memset", Hst[:], 0.0, reads=[], writes=["Hst0", "Hst1"])
    pool("memset", Hpad[:], 0.0, reads=[], writes=["Hpad0", "Hpad1"])
    pool("memset", Hpad_s[:], 0.0, reads=[], writes=["Hpad_s"])
    pool("memset", Upad[:], 0.0, reads=[], writes=["Upad"])
    pool("memset", aTp[:], 0.0, reads=[], writes=[("aTp", c) for c in range(4)])
    pool("memset", rTp[:], 0.0, reads=[], writes=[("rTp", c) for c in range(4)])
    pool("memset", Vpad[:], 0.0, reads=[], writes=["Vpad"])
    dma("sp", pv[:], pvd.rearrange("l p n -> p l n"), reads=[], writes=["pv"])
    dma("pool", wsm[:], wsmall_d.rearrange("l p n -> p l n"), reads=[], writes=["wsm"])
    dve("tensor_scalar", omka[:], pv[:, :, 42:46], -1.0, 1.0, ALU.mult, ALU.add, reads=["pv"], writes=["omka"])
    dve("tensor_scalar_mul", halfp[:], pv[:, :, 30:38], 0.5, reads=["pv"], writes=["halfp"])

    def pcol(l, col):
        return pv[:, l, col:col + 1]

    def rmsnorm(geo, l, gcol0, out_bf16, pp, ph):
        Nt = geo.Nt
        x, hT, xsq, rstd, sdt = x2[pp], hT2[pp], xsq2[ph], rstd2[ph], sdt2[ph]
        bank, bk = nb()
        for kc in range(8):
            xs_ = xsq[kc % 2]
            act(xs_[:, :Nt], x[:, kc, :Nt], AF.Square, reads=[("x", pp, kc)], writes=[("xsq", ph, kc % 2)])
            mm(bank[:, :Nt], onesmean[:], xs_[:, :Nt], kc == 0, kc == 7,
               reads=["onesmean", ("xsq", ph, kc % 2)], writes=[bk])
        act(sdt[:, :Nt], bank[:, :Nt], AF.Sqrt, reads=[bk, "epsc"], writes=[("sdt", ph)], bias=epsc[:, 0:1])
        dve("reciprocal", rstd[:, :Nt], sdt[:, :Nt], reads=[("sdt", ph)], writes=[("rstd", ph)])
        if out_bf16:
            for kc in range(8):
                dve("scalar_tensor_tensor", hT[:, kc, :Nt], x[:, kc, :Nt], pcol(l, gcol0 + kc), rstd[:, :Nt],
                    ALU.mult, ALU.mult, reads=[("x", pp, kc), ("rstd", ph), "pv"], writes=[("hT", pp, kc)])

    def big_mm(geo, slot, sk, nk, rhs_fn, rhs_keys):
        Nt = geo.Nt
        bank, bk = nb()
        for k in range(nk):
            mm(bank[:, :Nt], slot[:, k, :], rhs_fn(k), k == 0, k == nk - 1,
               reads=[sk] + rhs_keys(k), writes=[bk])
        return bank, bk

    def g3(ap2d, geo):
        return ap2d.rearrange("p (g t) -> p g t", g=geo.G)

    def zview(c, geo):
        return zbuf[:, c, 0:geo.G * (1 + geo.L)].rearrange("p (g t) -> p g t", g=geo.G)

    def uview(gi, geo):
        return ubuf[:, gi, 0:geo.G * (15 + geo.L)].rearrange("p (g t) -> p g t", g=geo.G)

    def shift(geo, l, c, out_ap, wkey, rows=slice(0, 128)):
        Nt, L = geo.Nt, geo.L
        zv = zview(c, geo)
        dve("tensor_tensor", g3(T["td"][rows, :Nt], geo), zv[rows, :, 0:L], zv[rows, :, 1:1 + L], ALU.subtract,
            reads=[("z", c)], writes=["td"])
        dve("scalar_tensor_tensor", g3(out_ap, geo), g3(T["td"][rows, :Nt], geo), pv[rows, l, 16 + c:17 + c],
            zv[rows, :, 1:1 + L], ALU.mult, ALU.add, reads=["td", ("z", c), "pv"], writes=[wkey])

    def front(geo, l, pp):
        Nt, G, L, nch = geo.Nt, geo.G, geo.L, geo.nch
        sm = 1 if geo.sample else 0
        Hk = "Hst%d" % l
        Hpk = "Hpad%d" % l
        x, hT, amix, pmix = x2[pp], hT2[pp], amix2[pp], pmix2[pp]
        if l == 0:
            xkeys = [("x", pp, k) for k in range(8)]
            if geo.sample:
                dma("sp", x[:, :, :Nt], xs.rearrange("k p t -> p k t"), reads=[], writes=xkeys)
            else:
                dma("sp", x[:, :, :Nt], xp.rearrange("k p t -> p k t")[:, :, geo.tok0:geo.tok0 + Nt],
                    reads=[], writes=xkeys)
        zall = [("z", c) for c in range(14)]
        uall = [("u", gi) for gi in range(4)]
        if geo.sample:
            dma("sp", stg_z[:], st_shift[l], reads=[], writes=["stg_z"])
            dve("tensor_copy", zbuf[:, :, 0:G * (1 + L)].rearrange("p c (g t) -> p c g t", g=G)[:, :, :, 0], stg_z[:],
                reads=["stg_z"], writes=zall)
            for gi in range(4):
                dma("sp", uview(gi, geo)[:, :, 0:15], st_pool[l, :, gi], reads=[], writes=[("u", gi)])
        else:
            dve("tensor_copy", zbuf[:, :, 0:1], carry_z[:, l, :].unsqueeze(2), reads=["carry_z"], writes=zall)
            dve("tensor_copy", ubuf[:, :, 0:15], carry_u[:, l, :, :], reads=["carry_u"], writes=uall)
        rmsnorm(geo, l, 0, True, pp, 0)
        yield
        hkeys = lambda k: [("hT", pp, k)]
        hrhs = lambda k: hT[:, k, :Nt]
        for j, c in enumerate(WIN_ORDER):
            slot, sk = w_take(l, j)
            bank, bk = big_mm(geo, slot, sk, 8, hrhs, hkeys)
            if c < 14:
                act(zview(c, geo)[:, :, 1:1 + L], g3(bank[:, :Nt], geo), AF.Copy, reads=[bk], writes=[("z", c)])
            else:
                gi = c - 14
                act(uview(gi, geo)[:, :, 15:15 + L], g3(bank[:, :Nt], geo), AF.Copy, reads=[bk], writes=[("u", gi)])
            yield
        if geo.sample:
            dve("tensor_copy", stg_z[:], zbuf[:, :, 0:G * (1 + L)].rearrange("p c (g t) -> p c g t", g=G)[:, :, :, L],
                reads=zall, writes=["stg_z"])
            dma("sp", o_shift_s[l], stg_z[:], reads=["stg_z"], writes=["o_shift_s"])
            for gi in range(4):
                dma("sp", o_pool_s[l, :, gi], uview(gi, geo)[:, :, L:L + 15], reads=[("u", gi)], writes=["o_pool_s"])
        else:
            dve("tensor_copy", carry_z[:, l, :].unsqueeze(2), zbuf[:, :, L:L + 1], reads=zall, writes=["carry_z"])
            dve("tensor_copy", carry_u[:, l, :, :], ubuf[:, :, L:L + 15], reads=uall, writes=["carry_u"])
            if geo.last:
                dma("sp", o_shift_p[l], carry_z[:, l, :], reads=["carry_z"], writes=["o_shift_p"])
                dma("sp", o_pool_p[l], carry_u[:, l, :, :], reads=["carry_u"], writes=["o_pool_p"])
        shift(geo, l, 12, T["r_c"][:, :Nt], "r_c")
        act(tl[0:64, :Nt], T["r_c"][0:64, :Nt], AF.Tanh, reads=["r_c"], writes=["tl"])
        act(tl[64:128, :Nt], T["r_c"][64:128, :Nt], AF.Copy, reads=["r_c"], writes=["tl"])
        shift(geo, l, 13, T["k_c"][:, :Nt], "k_c")
        yield
        act(T["tmpa"][:, :Nt], T["k_c"][:, :Nt], AF.Tanh, reads=["k_c"], writes=["tmpa"], scale=0.5)
        dve("tensor_scalar", sgl[:, :Nt], T["tmpa"][:, :Nt], 0.5, 0.5, ALU.mult, ALU.add, reads=["tmpa"], writes=["sgl"])
        for c in range(4):
            r_c, k_c, v_c = T["r_c"][:, :Nt], T["k_c"][:, :Nt], T["v_c"][:, :Nt]
            bank_d, bk_d = nb()
            mm(bank_d[:, :Nt], wsm[:, l, c * 128:(c + 1) * 128], tl[:, :Nt], True, True,
               reads=["wsm", "tl"], writes=[bk_d])
            bank_a, bk_a = nb()
            mm(bank_a[:, :Nt], wsm[:, l, 512 + c * 128:512 + (c + 1) * 128], tl[:, :Nt], True, True,
               reads=["wsm", "tl"], writes=[bk_a])
            bank_g, bk_g = nb()
            mm(bank_g[:, :Nt], wsm[:, l, 1024 + c * 128:1024 + (c + 1) * 128], sgl[:, :Nt], True, True,
               reads=["wsm", "sgl"], writes=[bk_g])
            shift(geo, l, 4 + c, T["k_c"][:, :Nt], "k_c")
            yield
            act(T["sg"][:, :Nt], bank_d[:, :Nt], AF.Tanh, reads=[bk_d, "halfp"], writes=["sg"], bias=halfp[:, l, c:c + 1], scale=0.5)
            act(T["a_in"][:, :Nt], bank_a[:, :Nt], AF.Tanh, reads=[bk_a, "halfp"], writes=["a_in"], bias=halfp[:, l, 4 + c:5 + c], scale=0.5)
            act(g_all[:, c, :Nt], bank_g[:, :Nt], AF.Copy, reads=[bk_g], writes=[("g_all", c)])
            act(TB["kksq"][:, :Nt], k_c, AF.Square, reads=["k_c", "pv"], writes=["kksq"], scale=pcol(l, 38 + c))
            bank_s, bk_s = nb()
            mm(bank_s[:, :Nt], blockones[:], TB["kksq"][:, :Nt], True, True, reads=["blockones", "kksq"], writes=[bk_s])
            yield
            dve("tensor_scalar", T["sg"][:, :Nt], T["sg"][:, :Nt], 0.5, 0.5, ALU.mult, ALU.add, reads=["sg"], writes=["sg"])
            dve("tensor_tensor_scan", T["csg"][:, :Nt], rmask[sm][:, :Nt], T["sg"][:, :Nt], 0.0, ALU.mult, ALU.add,
                reads=["sg", "rmask%d" % sm], writes=["csg"])
            dve("tensor_tensor", T["cprev"][:, :Nt], T["csg"][:, :Nt], T["sg"][:, :Nt], ALU.subtract,
                reads=["csg", "sg"], writes=["cprev"])
            act(T["ssd"][:, :Nt], bank_s[:, :Nt], AF.Sqrt, reads=[bk_s], writes=["ssd"])
            act(T["Epl"][:, :Nt], T["csg"][:, :Nt], AF.Exp, reads=["csg"], writes=["Epl"], scale=-DEC_SCALE)
            act(T["Emn"][:, :Nt], T["csg"][:, :Nt], AF.Exp, reads=["csg"], writes=["Emn"], scale=DEC_SCALE)
            act(T["Epv"][:, :Nt], T["cprev"][:, :Nt], AF.Exp, reads=["cprev"], writes=["Epv"], scale=-DEC_SCALE)
            dve("tensor_scalar", T["a_in"][:, :Nt], T["a_in"][:, :Nt], 0.5, 0.5, ALU.mult, ALU.add, reads=["a_in"], writes=["a_in"])
            shift(geo, l, c, T["r_c"][:, :Nt], "r_c")
            shift(geo, l, 8 + c, T["v_c"][:, :Nt], "v_c")
            yield
            dve("tensor_scalar_max", T["ssd"][:, :Nt], T["ssd"][:, :Nt], 1e-12, reads=["ssd"], writes=["ssd"])
            dve("reciprocal", T["rn"][:, :Nt], T["ssd"][:, :Nt], reads=["ssd"], writes=["rn"])
            dve("scalar_tensor_tensor", T["kkn"][:, :Nt], k_c, pcol(l, 38 + c), T["rn"][:, :Nt], ALU.mult, ALU.mult,
                reads=["k_c", "rn", "pv"], writes=["kkn"])
            dve("tensor_scalar", T["tmpa"][:, :Nt], T["a_in"][:, :Nt], pcol(l, 42 + c), omka[:, l, c:c + 1],
                ALU.mult, ALU.add, reads=["a_in", "pv", "omka"], writes=["tmpa"])
            dve("tensor_tensor", T["kmod"][:, :Nt], k_c, T["tmpa"][:, :Nt], ALU.mult,
                reads=["k_c", "tmpa"], writes=["kmod"])
            dve("tensor_tensor", T["bsc"][:, :Nt], T["kkn"][:, :Nt], T["a_in"][:, :Nt], ALU.mult,
                reads=["kkn", "a_in"], writes=["bsc"])
            dve("scalar_tensor_tensor", TB["rkb"][:, :Nt], r_c, pcol(l, 46 + c), T["kmod"][:, :Nt], ALU.mult, ALU.mult,
                reads=["r_c", "kmod", "pv"], writes=["rkb"])
            bank_b, bk_b = nb()
            mm(bank_b[:, :Nt], blockones[:], TB["rkb"][:, :Nt], True, True, reads=["blockones", "rkb"], writes=[bk_b])
            act(vb[:, c, :Nt], v_c, AF.Copy, reads=["v_c"], writes=[("vb", c)])
            yield
            ngr = G if geo.sample else nch
            per = L if geo.sample else 128
            dve("tensor_copy", WC[:, c, 0:ngr],
                T["Epl"][:, :Nt].rearrange("p (a b) -> p a b", b=per)[:, :, per - 1],
                reads=["Epl"], writes=[("WC", c)])
            dve("scalar_tensor_tensor", aT[:, c, :Nt], T["kkn"][:, :Nt], -1.0, T["Epv"][:, :Nt], ALU.mult, ALU.mult,
                reads=["kkn", "Epv"], writes=[("aT", c)])
            dve("tensor_tensor", bT[:, c, :Nt], T["bsc"][:, :Nt], T["Emn"][:, :Nt], ALU.mult,
                reads=["bsc", "Emn"], writes=[("bT", c)])
            dve("tensor_tensor", kT[:, c, :Nt], T["kmod"][:, :Nt], T["Emn"][:, :Nt], ALU.mult,
                reads=["kmod", "Emn"], writes=[("kT", c)])
            dve("tensor_tensor", rT[:, c, :Nt], r_c, T["Epl"][:, :Nt], ALU.mult,
                reads=["r_c", "Epl"], writes=[("rT", c)])
            dve("tensor_tensor", bonus[:, c, :Nt], bank_b[:, :Nt], v_c, ALU.mult, reads=[bk_b, "v_c"], writes=[("bonus", c)])
            for hh in range(2):
                pr = slice(hh * 64, hh * 64 + 64)
                act(aTp[pr, c, hh, :Nt], aT[pr, c, :Nt], AF.Copy, reads=[("aT", c)], writes=[("aTp", c)])
                act(rTp[pr, c, hh, :Nt], rT[pr, c, :Nt], AF.Copy, reads=[("rT", c)], writes=[("rTp", c)])
            yield
        for ci in range(nch):
            tok = slice(ci * 128, (ci + 1) * 128)
            for nm, src, dst in (("bT", bT, Btok), ("kT", kT, Ktok), ("vb", vb, Vtok)):
                bank, bk = nb()
                bb = bank[:].bitcast(BF16)
                for c in range(4):
                    P.add("pe", lambda e, bb=bb, src=src, c=c, tok=tok: e.transpose(bb[:, c * 128:(c + 1) * 128], src[:, c, tok], ident[:]),
                          reads=[(nm, c), "ident"], writes=[bk])
                act(dst[:, ci, :], bb[:, 0:512], AF.Copy, reads=[bk], writes=[(nm + "tok", ci)])
            v4 = Vtok[:, ci, :].rearrange("p (c h i) -> p c h i", c=4, h=2)
            vp = Vpad[:, ci, :, :].rearrange("p (c h) n -> p c h n", h=2)
            for hh in range(2):
                act(vp[:, :, hh, hh * 64:(hh + 1) * 64], v4[:, :, hh, :], AF.Copy,
                    reads=[("vbtok", ci)], writes=[("Vpad", ci)])
            yield
        Kfac = 3 if geo.sample else 7
        jobs = [[0], [1], [2], [3]] if geo.sample else [[0, 1], [2, 3]]
        for ci in range(nch):
            tok = slice(ci * 128, (ci + 1) * 128)
            bankY, bkY = banks[7], ("ps", 7)
            for cs in jobs:
                nh = 2 * len(cs)
                heads = [(c, hh) for c in cs for hh in range(2)]

                def five(lhs, lhs_nm, rhs, rhs_nm):
                    bank, bk = nb()
                    for hl, (c, hh) in enumerate(heads):
                        la = lhs[:, c, hh, tok] if lhs_nm in ("aTp", "rTp") else lhs[:, c, tok]
                        ra = rhs[:, c, hh, tok] if rhs_nm in ("aTp", "rTp") else rhs[:, c, tok]
                        mm(bank[:, hl * 128:(hl + 1) * 128], la, ra, True, True,
                           reads=[(lhs_nm, c), (rhs_nm, c)], writes=[bk])
                    return bank, bk

                def evm(dst, dkey, bank, bk, mask, mkey):
                    dve("tensor_tensor", dst[:, 0:nh, :], bank[:, 0:nh * 128].rearrange("p (h t) -> p h t", h=nh),
                        bc(mask[:], nh), ALU.mult, reads=[bk, mkey], writes=[dkey])

                bank, bk = five(bT, "bT", aTp, "aTp")
                evm(Nm[0], "Nm0", bank, bk, m_su[sm], "m_su%d" % sm)
                bank, bk = five(aTp, "aTp", bT, "bT")
                evm(Mm[0], "Mm0", bank, bk, m_sl[sm], "m_sl%d" % sm)
                bank, bk = five(bT, "bT", rTp, "rTp")
                evm(ArbT, "ArbT", bank, bk, m_iu[sm], "m_iu%d" % sm)
                bank, bk = five(kT, "kT", aTp, "aTp")
                evm(LakT, "LakT", bank, bk, m_su[sm], "m_su%d" % sm)
                bank, bk = five(kT, "kT", rTp, "rTp")
                evm(ArkT, "ArkT", bank, bk, m_iu[sm], "m_iu%d" % sm)
                yield
                dve("tensor_tensor", Qm[0][:, 0:nh, :], Nm[0][:, 0:nh, :], bc(ident[:], nh), ALU.add,
                    reads=["Nm0", "ident"], writes=["Qm0"])
                qi = 0
                for k in range(Kfac - 1):
                    a, b = k % 2, (k + 1) % 2
                    bankM, bkM = nb()
                    for hl in range(nh):
                        mm(bankM[:, hl * 128:(hl + 1) * 128], Nm[a][:, hl, :], Mm[a][:, hl, :], True, True,
                           reads=["Nm%d" % a, "Mm%d" % a], writes=[bkM])
                    if k < Kfac - 2:
                        bankN, bkN = nb()
                        for hl in range(nh):
                            mm(bankN[:, hl * 128:(hl + 1) * 128], Mm[a][:, hl, :], Nm[a][:, hl, :], True, True,
                               reads=["Nm%d" % a, "Mm%d" % a], writes=[bkN])
                    yield
                    act(Mm[b][:, 0:nh, :], bankM[:, 0:nh * 128].rearrange("p (h t) -> p h t", h=nh), AF.Copy,
                        reads=[bkM], writes=["Mm%d" % b])
                    if k < Kfac - 2:
                        act(Nm[b][:, 0:nh, :], bankN[:, 0:nh * 128].rearrange("p (h t) -> p h t", h=nh), AF.Copy,
                            reads=[bkN], writes=["Nm%d" % b])
                    yield
                    bank, bk = nb()
                    for hl in range(nh):
                        mm(bank[:, hl * 128:(hl + 1) * 128], Mm[b][:, hl, :], Qm[qi][:, hl, :], True, True,
                           reads=["Mm%d" % b, "Qm%d" % qi], writes=[bk])
                    yield
                    dve("tensor_tensor", Qm[1 - qi][:, 0:nh, :], bank[:, 0:nh * 128].rearrange("p (h t) -> p h t", h=nh),
                        Qm[qi][:, 0:nh, :], ALU.add, reads=[bk, "Qm%d" % qi], writes=["Qm%d" % (1 - qi)])
                    qi = 1 - qi
                Q = Qm[qi]
                Qk = "Qm%d" % qi
                if geo.sample:
                    c = cs[0]
                    dma("sp", H0s[:], st_wkv[l, c], reads=[], writes=["H0s"])
                    for hh in range(2):
                        pr = slice(hh * 64, hh * 64 + 64)
                        act(Hpad_s[pr, :, hh * 64:(hh + 1) * 64], H0s[pr, :, :], AF.Copy, reads=["H0s"], writes=["Hpad_s"])
                    dve("tensor_tensor", aTm[:], aT[:, c, 0:128].unsqueeze(1).to_broadcast([128, 16, 128]), qsel[:], ALU.mult,
                        reads=[("aT", c), "qsel"], writes=["aTm"])
                bankR, bkR = nb()
                for cl, c in enumerate(cs):
                    if geo.sample:
                        for q in range(16):
                            mm(bankR[:, cl * 128:(cl + 1) * 128], aTm[:, q, :], Hpad_s[:, q, :], q == 0, False,
                               reads=["aTm", "Hpad_s"], writes=[bkR])
                    else:
                        mm(bankR[:, cl * 128:(cl + 1) * 128], aT[:, c, tok], Hpad[:, l, c, :], True, False,
                           reads=[("aT", c), Hpk], writes=[bkR])
                    for hh in range(2):
                        hl = 2 * cl + hh
                        h = 2 * c + hh
                        mm(bankR[:, hl * 64:(hl + 1) * 64], LakT[:, hl, :], Vtok[:, ci, h * 64:(h + 1) * 64], False, hh == 1,
                           reads=["LakT", ("vbtok", ci)], writes=[bkR])
                act(R0[:, 0:nh, :], bankR[:, 0:nh * 64].rearrange("p (h i) -> p h i", h=nh), AF.Copy, reads=[bkR], writes=["R0"])
                bankU, bkU = nb()
                for hl in range(nh):
                    mm(bankU[:, hl * 64:(hl + 1) * 64], Q[:, hl, :], R0[:, hl, :], True, True, reads=[Qk, "R0"], writes=[bkU])
                act(Usb[:, 0:nh, :], bankU[:, 0:nh * 64].rearrange("p (h i) -> p h i", h=nh), AF.Copy, reads=[bkU], writes=["Usb"])
                for hl, (c, hh) in enumerate(heads):
                    dve("tensor_copy", Upad[:, hl, hh * 64:(hh + 1) * 64], bankU[:, hl * 64:(hl + 1) * 64],
                        reads=[bkU], writes=["Upad"])
                yield
                for cl, c in enumerate(cs):
                    first = True
                    if not geo.sample:
                        mm(bankY[:, c * 128:(c + 1) * 128], Hpad[:, l, c, :], rT[:, c, tok], True, False,
                           reads=[Hpk, ("rT", c)], writes=[bkY])
                        first = False
                    for hh in range(2):
                        hl = 2 * cl + hh
                        h = 2 * c + hh
                        mm(bankY[:, c * 128:(c + 1) * 128], Upad[:, hl, :], ArbT[:, hl, :], first, False,
                           reads=["Upad", "ArbT"], writes=[bkY])
                        first = False
                        mm(bankY[:, c * 128:(c + 1) * 128], Vpad[:, ci, h, :], ArkT[:, hl, :], False, (not geo.sample) and hh == 1,
                           reads=[("Vpad", ci), "ArkT"], writes=[bkY])
                    if geo.sample:
                        for q in range(16):
                            mm(bankY[:, c * 128 + q * 8:c * 128 + q * 8 + 8], Hpad_s[:, q, :], rT[:, c, q * 8:q * 8 + 8],
                               False, q == 15, reads=["Hpad_s", ("rT", c)], writes=[bkY])
                yield
                if not geo.sample:
                    bankS, bkS = nb()
                    for cl, c in enumerate(cs):
                        mm(bankS[:, cl * 128:(cl + 1) * 128], Btok[:, ci, c * 128:(c + 1) * 128],
                           Usb[:, 2 * cl:2 * cl + 2, :].rearrange("p h i -> p (h i)"), True, False,
                           reads=[("bTtok", ci), "Usb"], writes=[bkS])
                        mm(bankS[:, cl * 128:(cl + 1) * 128], Ktok[:, ci, c * 128:(c + 1) * 128],
                           Vtok[:, ci, c * 128:(c + 1) * 128], False, True,
                           reads=[("kTtok", ci), ("vbtok", ci)], writes=[bkS])
                    c0 = cs[0]
                    ncs = len(cs)
                    for hh in range(2):
                        pr = slice(hh * 64, hh * 64 + 64)
                        psv = bankS[pr, 0:ncs * 128].rearrange("p (c n) -> p c n", c=ncs)[:, :, hh * 64:(hh + 1) * 64]
                        dve("tensor_tensor", tmpH[pr, 0:ncs, :], psv, Hst[pr, l, c0:c0 + ncs, :], ALU.add,
                            reads=[bkS, Hk], writes=["tmpH"])
                        dve("tensor_tensor", Hst[pr, l, c0:c0 + ncs, :], tmpH[pr, 0:ncs, :],
                            WC[pr, c0:c0 + ncs, ci:ci + 1].to_broadcast([64, ncs, 64]), ALU.mult,
                            reads=["tmpH"] + [("WC", c) for c in cs], writes=[Hk])
                        act(Hpad[pr, l, c0:c0 + ncs, hh * 64:(hh + 1) * 64], Hst[pr, l, c0:c0 + ncs, :], AF.Copy,
                            reads=[Hk], writes=[Hpk])
                else:
                    c = cs[0]
                    dve("tensor_tensor", Uexp[:], Usb[:, 0:2, :].rearrange("p h i -> p (h i)").unsqueeze(1).to_broadcast([128, 16, 128]),
                        qmask[:].unsqueeze(2).to_broadcast([128, 16, 128]), ALU.mult, reads=["Usb", "qmask"], writes=["Uexp"])
                    dve("tensor_tensor", Vexp[:], Vtok[:, ci, c * 128:(c + 1) * 128].unsqueeze(1).to_broadcast([128, 16, 128]),
                        qmask[:].unsqueeze(2).to_broadcast([128, 16, 128]), ALU.mult, reads=[("vbtok", ci), "qmask"], writes=["Vexp"])
                    for nbk in range(4):
                        bankS, bkS = nb()
                        mm(bankS[:], Btok[:, ci, c * 128:(c + 1) * 128],
                           Uexp[:, 4 * nbk:4 * nbk + 4, :].rearrange("p q n -> p (q n)"), True, False,
                           reads=[("bTtok", ci), "Uexp"], writes=[bkS])
                        mm(bankS[:], Ktok[:, ci, c * 128:(c + 1) * 128],
                           Vexp[:, 4 * nbk:4 * nbk + 4, :].rearrange("p q n -> p (q n)"), False, True,
                           reads=[("kTtok", ci), "Vexp"], writes=[bkS])
                        for hh in range(2):
                            pr = slice(hh * 64, hh * 64 + 64)
                            psv = bankS[pr, :].rearrange("p (q n) -> p q n", q=4)[:, :, hh * 64:(hh + 1) * 64]
                            dve("tensor_tensor", Hout_s[pr, 4 * nbk:4 * nbk + 4, :], psv, H0s[pr, 4 * nbk:4 * nbk + 4, :], ALU.add,
                                reads=[bkS, "H0s"], writes=["H0s"])
                            dve("tensor_tensor", Hout_s[pr, 4 * nbk:4 * nbk + 4, :], Hout_s[pr, 4 * nbk:4 * nbk + 4, :],
                                WC[pr, c, 4 * nbk:4 * nbk + 4].unsqueeze(2).to_broadcast([64, 4, 64]), ALU.mult,
                                reads=["H0s", ("WC", c)], writes=["H0s"])
                    dma("sp", o_wkv_s[l, c], Hout_s[:], reads=["H0s"], writes=["o_wkv_s"])
            act(yT[:, :, tok], bankY[:, :].rearrange("p (c t) -> p c t", c=4), AF.Copy, reads=[bkY], writes=[("yT", ci)])
            yield
        if geo.last:
            dve("tensor_copy", H0s[:, 0:4, :], Hst[:, l, :, :], reads=[Hk], writes=["H0s"])
            dma("sp", o_wkv_p[l], H0s[:, 0:4, :], reads=["H0s"], writes=["o_wkv_p"])
        ytk = [("yT", ci) for ci in range(nch)]
        for c in range(4):
            act(TB["yb"][:, :Nt], yT[:, c, :Nt], AF.Copy, reads=ytk, writes=["yb"])
            bank, bk = nb()
            mm(bank[:, :Nt], blockmean[:], TB["yb"][:, :Nt], True, True, reads=["blockmean", "yb"], writes=[bk])
            dve("tensor_tensor", T["kkn"][:, :Nt], yT[:, c, :Nt], bank[:, :Nt], ALU.subtract, reads=ytk + [bk], writes=["kkn"])
            act(TB["ycsq"][:, :Nt], T["kkn"][:, :Nt], AF.Square, reads=["kkn"], writes=["ycsq"])
            bank, bk = nb()
            mm(bank[:, :Nt], blockmean[:], TB["ycsq"][:, :Nt], True, True, reads=["blockmean", "ycsq"], writes=[bk])
            act(T["ssd"][:, :Nt], bank[:, :Nt], AF.Sqrt, reads=[bk, "epsc"], writes=["ssd"], bias=epsc[:, 1:2])
            dve("reciprocal", T["rn"][:, :Nt], T["ssd"][:, :Nt], reads=["ssd"], writes=["rn"])
            dve("tensor_tensor", T["bsc"][:, :Nt], T["kkn"][:, :Nt], T["rn"][:, :Nt], ALU.mult, reads=["kkn", "rn"], writes=["bsc"])
            dve("tensor_scalar", T["bsc"][:, :Nt], T["bsc"][:, :Nt], pcol(l, 50 + c), pcol(l, 54 + c), ALU.mult, ALU.add,
                reads=["bsc", "pv"], writes=["bsc"])
            dve("tensor_tensor", T["bsc"][:, :Nt], T["bsc"][:, :Nt], bonus[:, c, :Nt], ALU.add,
                reads=["bsc", ("bonus", c)], writes=["bsc"])
            dve("tensor_tensor", amix[:, c, :Nt], T["bsc"][:, :Nt], g_all[:, c, :Nt], ALU.mult,
                reads=["bsc", ("g_all", c)], writes=[("amix", pp, c)])
            yield
        W = G * (15 + L)
        for gi in range(4):
            uv = uview(gi, geo)
            sAv = sA[:, 0:W].rearrange("p (g t) -> p g t", g=G)
            sBv = sB[:, 0:W].rearrange("p (g t) -> p g t", g=G)
            E = 15 + L
            src, srck = uv, ("u", gi)
            bufs = [(sAv, "sA"), (sBv, "sB")]
            nsteps = gi + 1
            for s in range(nsteps):
                d = 1 << s
                lo = 15 if s == nsteps - 1 else (2 << s) - 1
                lo = max(lo, (2 << s) - 1) if s < nsteps - 1 else 15
                dst, dstk = bufs[s % 2]
                dve("tensor_tensor", dst[:, :, lo:E], src[:, :, lo:E], src[:, :, lo - d:E - d], ALU.add,
                    reads=[srck], writes=[dstk])
                src, srck = dst, dstk
            ws_ = src[:, :, 15:E]
            win = 2 << gi
            dve("scalar_tensor_tensor", g3(p_in[:, gi, :Nt], geo), ws_, 1.0 / win, uv[:, :, 15:E], ALU.mult, ALU.subtract,
                reads=[srck, ("u", gi)], writes=[("p_in", gi)])
            if geo.first:
                dve("tensor_tensor", tmpp[:, 0:16], src[:, 0, 15:31], rcs[:, gi, :], ALU.mult,
                    reads=[srck, "rcs"], writes=["tmpp"])
                dve("tensor_tensor", p_in[:, gi, 0:16], tmpp[:, 0:16], uv[:, 0, 15:31], ALU.subtract,
                    reads=["tmpp", ("u", gi)], writes=[("p_in", gi)])
            bank, bk = nb()
            mm(bank[:, :Nt], wsm[:, l, 1536 + gi * 128:1536 + (gi + 1) * 128], p_in[:, gi, :Nt], True, True,
               reads=["wsm", ("p_in", gi)], writes=[bk])
            act(pmix[:, gi, :Nt], bank[:, :Nt], AF.Identity, reads=[bk, "pv"], writes=[("pmix", pp, gi)], scale=pcol(l, 58 + gi))
            yield

    def back(geo, l, pp):
        Nt, G, L, nch = geo.Nt, geo.G, geo.L, geo.nch
        x, hT, amix, pmix = x2[pp], hT2[pp], amix2[pp], pmix2[pp]
        hkeys = lambda k: [("hT", pp, k)]
        hrhs = lambda k: hT[:, k, :Nt]
        for m in range(8):
            slot, sk = w_take(l, 18 + 3 * m)
            bankA, bkA = big_mm(geo, slot, sk, 8, hrhs, hkeys)
            act(ga[:, :Nt], bankA[:, :Nt], AF.Tanh, reads=[bkA], writes=["ga"], scale=0.5)
            slot, sk = w_take(l, 18 + 3 * m + 1)
            bankB, bkB = big_mm(geo, slot, sk, 8, hrhs, hkeys)
            act(gb[:, :Nt], bankB[:, :Nt], AF.Tanh, reads=[bkB], writes=["gb"], scale=0.5)
            slot, sk = w_take(l, 18 + 3 * m + 2)
            banka, bka = nb()
            for c in range(4):
                mm(banka[:, :Nt], slot[:, c, :], amix[:, c, :Nt], c == 0, c == 3, reads=[sk, ("amix", pp, c)], writes=[bka])
            bankb, bkb = nb()
            for gi in range(4):
                mm(bankb[:, :Nt], slot[:, 4 + gi, :], pmix[:, gi, :Nt], gi == 0, gi == 3, reads=[sk, ("pmix", pp, gi)], writes=[bkb])
            dve("scalar_tensor_tensor", ga[:, :Nt], ga[:, :Nt], 1.0, banka[:, :Nt], ALU.add, ALU.mult, reads=["ga", bka], writes=["ga"])
            dve("scalar_tensor_tensor", gb[:, :Nt], gb[:, :Nt], 1.0, bankb[:, :Nt], ALU.add, ALU.mult, reads=["gb", bkb], writes=["gb"])
            dve("tensor_tensor", merged[:, m, :Nt], ga[:, :Nt], gb[:, :Nt], ALU.add, reads=["ga", "gb"], writes=[("merged", m)])
            yield
        for m in range(8):
            slot, sk = w_take(l, 42 + m)
            bank, bk = big_mm(geo, slot, sk, 8, lambda k: merged[:, k, :Nt], lambda k: [("merged", k)])
            dve("scalar_tensor_tensor", x[:, m, :Nt], bank[:, :Nt], 0.5, x[:, m, :Nt], ALU.mult, ALU.add,
                reads=[("x", pp, m), bk], writes=[("x", pp, m)])
        yield
        rmsnorm(geo, l, 8, True, pp, 1)
        yield
        pj = 50
        for fq in range(4):
            for fl in range(8):
                slot, sk = w_take(l, pj)
                pj += 1
                bank, bk = big_mm(geo, slot, sk, 8, hrhs, hkeys)
                rt = rtmp[fl % 2]
                act(rt[:, :Nt], bank[:, :Nt], AF.Relu, reads=[bk], writes=[("rtmp", fl % 2)])
                dve("tensor_tensor", hid[:, fl, :Nt], rt[:, :Nt], rt[:, :Nt], ALU.mult,
                    reads=[("rtmp", fl % 2)], writes=[("hid", fl)])
                yield
            for m in range(8):
                slot, sk = w_take(l, pj)
                pj += 1
                bank, bk = big_mm(geo, slot, sk, 8, lambda k: hid[:, k, :Nt], lambda k: [("hid", k)])
                dve("tensor_tensor", x[:, m, :Nt], x[:, m, :Nt], bank[:, :Nt], ALU.add, reads=[("x", pp, m), bk], writes=[("x", pp, m)])
        assert pj == NPIECE

    def back_full(geo, l, pp):
        yield from back(geo, l, pp)
        if l == 1:
            Nt = geo.Nt
            x = x2[pp]
            rmsnorm(geo, 0, 62, False, pp, 1)
            for kc in range(8):
                yt = rtmp[kc % 2]
                dve("scalar_tensor_tensor", yt[:, :Nt], x[:, kc, :Nt], pcol(0, 62 + kc), rstd2[1][:, :Nt], ALU.mult, ALU.mult,
                    reads=[("x", pp, kc), ("rstd", 1), "pv"], writes=[("rtmp", kc % 2)])
                if geo.sample:
                    dma("sp", ys[kc], yt[:, :Nt], reads=[("rtmp", kc % 2)], writes=["ys"])
                else:
                    dma("sp", yp[kc, :, geo.tok0:geo.tok0 + Nt], yt[:, :Nt], reads=[("rtmp", kc % 2)], writes=["yp"])
            yield

    def count_steps(gen):
        n = 0
        for _ in gen:
            n += 1
        return n

    def emit_all():
        bank_ctr[0] = 0
        bank_ctr[1] = 0
        wst["order"] = []
        nslots = 4 * ((len(tiles) - 1) // 2) + ((len(tiles) - 1) % 2) + 4
        for slot in range(nslots):
            active = []
            for i, geo in enumerate(tiles):
                st = slot - (4 * (i // 2) + (i % 2))
                if 0 <= st < 4:
                    l, isback = st // 2, st % 2
                    active.append((geo, l, i % 2, isback))
            gens = []
            for geo, l, pp, isback in active:
                key = (geo.sample, l, isback)
                if key not in step_cache:
                    was = P.dry
                    P.dry = True
                    saved = (list(bank_ctr), list(wst["order"]), wst.get("pos", 0))
                    phase[0] = isback
                    step_cache[key] = count_steps((back_full if isback else front)(geo, l, pp))
                    bank_ctr[0], bank_ctr[1] = saved[0]
                    wst["order"], wst["pos"] = saved[1], saved[2]
                    P.dry = was
                gens.append([(back_full if isback else front)(geo, l, pp), 0, step_cache[key], isback])
            while gens:
                gens.sort(key=lambda g: g[1] / g[2])
                g = gens[0]
                try:
                    phase[0] = g[3]
                    next(g[0])
                    g[1] += 1
                except StopIteration:
                    gens.remove(g)

    step_cache = {}
    P.dry = True
    wst["mode"] = "record"
    emit_all()
    seq = wst["order"]
    P.dry = False
    wst["mode"] = "emit"
    wst["issued"] = 0
    wst["pos"] = 0
    emit_all()
    assert wst["pos"] == len(seq)
    P.finish(OUT_KEYS)
    return nc, P


def _wstream(w_in, w_a_up, w_b_up, w_o, w_ff1, w_ff2):
    L = w_in.shape[0]
    out = np.empty((L, NPIECE, 128, 1024), np.float32)

    def pk(mat, ncol0):
        K = mat.shape[0] // 128
        return mat[:, ncol0:ncol0 + 128].reshape(K, 128, 128).transpose(1, 0, 2)

    for l in range(L):
        j = 0
        for c in WIN_ORDER:
            out[l, j] = pk(w_in[l], c * 128).reshape(128, 1024)
            j += 1
        for m in range(8):
            out[l, j] = pk(w_in[l], (18 + m) * 128).reshape(128, 1024)
            j += 1
            out[l, j] = pk(w_in[l], (26 + m) * 128).reshape(128, 1024)
            j += 1
            ab = np.concatenate([pk(w_a_up[l], m * 128), pk(w_b_up[l], m * 128)], axis=1)
            out[l, j] = ab.reshape(128, 1024)
            j += 1
        for m in range(8):
            out[l, j] = pk(w_o[l], m * 128).reshape(128, 1024)
            j += 1
        for fq in range(4):
            for fl in range(8):
                out[l, j] = pk(w_ff1[l], (fq * 8 + fl) * 128).reshape(128, 1024)
                j += 1
            for m in range(8):
                out[l, j] = pk(w_ff2[l][fq * 1024:(fq + 1) * 1024], m * 128).reshape(128, 1024)
                j += 1
        assert j == NPIECE
    return out


_CACHE = {}


def kernel(x_prompt, x_sample, state_shift, state_pool, state_wkv,
           norm1_g, w_in, mu_shift, decay0, w_decay2, a0, w_a2, w_g2, k_k, k_a, r_k,
           ln_x_g, ln_x_b, w_a_up, w_pool, pool_scale, w_b_up, w_o,
           norm2_g, w_ff1, w_ff2, final_norm_g):
    f = lambda a: np.ascontiguousarray(np.asarray(a, dtype=np.float32))
    x_prompt, x_sample, state_shift, state_pool, state_wkv = map(f, (x_prompt, x_sample, state_shift, state_pool, state_wkv))
    B, TP, _ = x_prompt.shape
    assert B == NCORES and TP % NT == 0
    L = 2

    def cols(v, n):
        return f(v).reshape(n, 128).T

    pvs = np.zeros((L, 128, NPV), np.float32)
    for l in range(L):
        pvs[l, :, 0:8] = cols(norm1_g[l], 8)
        pvs[l, :, 8:16] = cols(norm2_g[l], 8)
        pvs[l, :, 16:30] = cols(mu_shift[l], 14)
        pvs[l, :, 30:34] = cols(decay0[l], 4)
        pvs[l, :, 34:38] = cols(a0[l], 4)
        pvs[l, :, 38:42] = cols(k_k[l], 4)
        pvs[l, :, 42:46] = cols(k_a[l], 4)
        pvs[l, :, 46:50] = cols(f(r_k[l]).reshape(-1), 4)
        pvs[l, :, 50:54] = cols(ln_x_g[l], 4)
        pvs[l, :, 54:58] = cols(ln_x_b[l], 4)
        pvs[l, :, 58:62] = cols(pool_scale[l], 4)
        pvs[l, :, 62:70] = cols(final_norm_g, 8)
    wsmall = np.zeros((L, 128, 2048), np.float32)
    for l in range(L):
        wsmall[l, 0:64, 0:512] = f(w_decay2[l])
        wsmall[l, 64:128, 512:1024] = f(w_a2[l])
        wsmall[l, :, 1024:1536] = f(w_g2[l])
        wsmall[l, :, 1536:2048] = f(w_pool[l]).transpose(1, 0, 2).reshape(128, 512)
    wstream = _wstream(f(w_in), f(w_a_up), f(w_b_up), f(w_o), f(w_ff1), f(w_ff2))

    key = TP
    if key not in _CACHE:
        _CACHE[key] = build_program(TP)[0]
    nc = _CACHE[key]
    in_maps = []
    for c in range(NCORES):
        q0 = 16 * c
        xpc = np.ascontiguousarray(x_prompt[c].T.reshape(8, 128, TP))
        xsc = np.ascontiguousarray(x_sample[q0:q0 + 16].reshape(128, D).T.reshape(8, 128, 128))
        sh = np.ascontiguousarray(state_shift[:, q0:q0 + 16, :].reshape(L, 16, 14, 128).transpose(0, 3, 2, 1))
        pl = np.ascontiguousarray(state_pool[:, q0:q0 + 16].reshape(L, 16, 15, 4, 128).transpose(0, 4, 3, 1, 2))
        wk = state_wkv[:, q0:q0 + 16].reshape(L, 16, 4, 2, 64, 64).transpose(0, 2, 3, 5, 1, 4)
        wk = np.ascontiguousarray(wk.reshape(L, 4, 128, 16, 64))
        in_maps.append({"xp": xpc, "xs": xsc, "st_shift": sh, "st_pool": pl, "st_wkv": wk,
                        "pv": pvs, "wsmall": wsmall, "wstream": wstream})
    res = run_bass_kernel_spmd(nc, in_maps, core_ids=list(range(NCORES)))
    R = res.results
    y_prompt = np.stack([R[c]["yp"].reshape(D, TP).T for c in range(NCORES)])
    y_sample = np.concatenate([R[c]["ys"].reshape(D, 128).T.reshape(16, 8, D) for c in range(NCORES)])
    p_shift = np.stack([R[c]["o_shift_p"].reshape(L, 128, 14).transpose(0, 2, 1).reshape(L, DSHIFT) for c in range(NCORES)], axis=1)
    s_shift = np.concatenate([R[c]["o_shift_s"].reshape(L, 128, 14, 16).transpose(0, 3, 2, 1).reshape(L, 16, DSHIFT)
                              for c in range(NCORES)], axis=1)
    p_pool = np.stack([R[c]["o_pool_p"].reshape(L, 128, 4, 15).transpose(0, 3, 2, 1).reshape(L, 15, 512) for c in range(NCORES)], axis=1)
    s_pool = np.concatenate([R[c]["o_pool_s"].reshape(L, 128, 4, 16, 15).transpose(0, 3, 4, 2, 1).reshape(L, 16, 15, 512)
                             for c in range(NCORES)], axis=1)
    p_wkv = np.stack([R[c]["o_wkv_p"].reshape(L, 2, 64, 4, 64).transpose(0, 3, 1, 4, 2).reshape(L, 8, 64, 64)
                      for c in range(NCORES)], axis=1)
    s_wkv = np.concatenate([R[c]["o_wkv_s"].reshape(L, 4, 2, 64, 16, 64).transpose(0, 4, 1, 2, 5, 3).reshape(L, 16, 8, 64, 64)
                            for c in range(NCORES)], axis=1)
    out = (y_prompt, y_sample, p_shift, p_pool, p_wkv, s_shift, s_pool, s_wkv)
    return tuple(np.ascontiguousarray(o, dtype=np.float32) for o in out)
```

```python
import contextlib
import numpy as np
import concourse.bass as bass
import concourse.mybir as mybir
from concourse.bass_utils import run_bass_kernel_spmd

F32 = mybir.dt.float32
BF16 = mybir.dt.bfloat16
AF = mybir.ActivationFunctionType
ALU = mybir.AluOpType

NCORES = 8
D = 1024
DA = 512
DSHIFT = 1792
DIN = 4352
DFF = 4096
NT = 256
NSLOT = 8
NPIECE = 114
NPV = 70
RMS_EPS = 1e-6
GN_EPS = 64 * 1e-5
DEC_SCALE = float(np.exp(-0.5))
WIN_ORDER = [12, 13] + list(range(12)) + [14, 15, 16, 17]


class Op:
    __slots__ = ("eng", "fn", "reads", "writes", "dma", "idx", "deps", "sem", "val", "has_dep")

    def __init__(self, eng, fn, reads, writes, dma):
        self.eng = eng
        self.fn = fn
        self.reads = reads
        self.writes = writes
        self.dma = dma
        self.deps = ()
        self.sem = None
        self.val = 0
        self.has_dep = False


class Prog:
    ENGS = ("pe", "act", "dve", "pool", "sp")
    NDMA = 8

    def __init__(self, nc):
        self.nc = nc
        self.ops = []
        self.dry = False

    def add(self, eng, fn, reads=(), writes=(), dma=False):
        if self.dry:
            return None
        op = Op(eng, fn, tuple(reads), tuple(writes), dma)
        op.idx = len(self.ops)
        self.ops.append(op)
        return op

    def finish(self, final_keys):
        nc = self.nc
        ops = self.ops
        last_w = {}
        readers = {}
        for op in ops:
            deps = set()
            for k in op.reads:
                if k in last_w:
                    deps.add(last_w[k])
            for k in op.writes:
                if k in last_w:
                    deps.add(last_w[k])
                latest = {}
                for r in readers.get(k, ()):
                    ro = ops[r]
                    if ro.dma:
                        deps.add(r)
                    elif ro.eng not in latest or latest[ro.eng] < r:
                        latest[ro.eng] = r
                deps.update(latest.values())
            deps.discard(op.idx)
            op.deps = deps
            for k in op.writes:
                last_w[k] = op.idx
                readers[k] = []
            for k in op.reads:
                readers.setdefault(k, []).append(op.idx)
        final_deps = set()
        for op in ops:
            if op.dma and any(k in final_keys for k in op.writes):
                final_deps.add(op.idx)
        for op in ops:
            for d in op.deps:
                if op.eng == "pe" and ops[d].eng == "pe" and not ops[d].dma and not op.dma:
                    continue
                ops[d].has_dep = True
        with contextlib.ExitStack() as st:
            sem_eng = {e: st.enter_context(nc.semaphore("s_" + e)) for e in ("pe", "act", "dve", "pool")}
            dma_sems = {e: [st.enter_context(nc.semaphore("d_%s%d" % (e, i))) for i in range(self.NDMA)]
                        for e in ("sp", "pool")}
            cnt = {e: 0 for e in self.ENGS}
            dcnt = {e: 0 for e in dma_sems}
            dvals = {e: [0] * self.NDMA for e in dma_sems}
            dlast = {e: [None] * self.NDMA for e in dma_sems}
            for op in ops:
                if op.dma:
                    i = dcnt[op.eng]
                    dcnt[op.eng] += 1
                    slot = i % self.NDMA
                    prev = dlast[op.eng][slot]
                    if prev is not None:
                        op.deps = set(op.deps) | {prev}
                    dvals[op.eng][slot] += 16
                    op.sem = dma_sems[op.eng][slot]
                    op.val = dvals[op.eng][slot]
                    dlast[op.eng][slot] = op.idx
                elif op.has_dep:
                    cnt[op.eng] += 1
                    op.sem = sem_eng[op.eng]
                    op.val = cnt[op.eng]
            self.sem_counts = dict(cnt)
            per_eng = {e: [o for o in ops if o.eng == e] for e in self.ENGS}
            with nc.Block() as block:
                def run(e, engobj):
                    waited = {}
                    for op in per_eng[e]:
                        need = {}
                        for d in op.deps:
                            p = ops[d]
                            if e == "pe" and p.eng == "pe" and not p.dma and not op.dma:
                                continue
                            key = id(p.sem)
                            if waited.get(key, 0) >= p.val:
                                continue
                            if key not in need or need[key][1] < p.val:
                                need[key] = (p.sem, p.val)
                        for key, (s, v) in need.items():
                            engobj.wait_ge(s, v)
                            waited[key] = v
                        ins = op.fn(engobj)
                        if op.sem is not None:
                            ins.then_inc(op.sem, 16 if op.dma else 1)
                    return waited

                @block.sync
                def _(eng):
                    waited = run("sp", eng)
                    need = {}
                    for d in final_deps:
                        p = ops[d]
                        key = id(p.sem)
                        if waited.get(key, 0) >= p.val:
                            continue
                        if key not in need or need[key][1] < p.val:
                            need[key] = (p.sem, p.val)
                    for key, (s, v) in need.items():
                        eng.wait_ge(s, v)

                @block.tensor
                def _(eng):
                    run("pe", eng)

                @block.scalar
                def _(eng):
                    run("act", eng)

                @block.vector
                def _(eng):
                    run("dve", eng)

                @block.gpsimd
                def _(eng):
                    run("pool", eng)


class Geo:
    def __init__(self, Nt, G, sample, first, last, tok0):
        self.Nt = Nt
        self.G = G
        self.L = Nt // G
        self.nch = Nt // 128
        self.sample = sample
        self.first = first
        self.last = last
        self.tok0 = tok0


def build_program(TP):
    nc = bass.Bass("TRN2", target_bir_lowering=False)
    n_pt = TP // NT
    P = Prog(nc)

    def din(name, shape):
        return nc.dram_tensor(name, shape, F32, kind="ExternalInput").ap()

    def dout(name, shape):
        return nc.dram_tensor(name, shape, F32, kind="ExternalOutput").ap()

    xp = din("xp", [8, 128, TP])
    xs = din("xs", [8, 128, 128])
    st_shift = din("st_shift", [2, 128, 14, 16])
    st_pool = din("st_pool", [2, 128, 4, 16, 15])
    st_wkv = din("st_wkv", [2, 4, 128, 16, 64])
    pvd = din("pv", [2, 128, NPV])
    wsmall_d = din("wsmall", [2, 128, 2048])
    wstream_d = din("wstream", [2, NPIECE, 128, 1024])
    yp = dout("yp", [8, 128, TP])
    ys = dout("ys", [8, 128, 128])
    o_shift_p = dout("o_shift_p", [2, 128, 14])
    o_shift_s = dout("o_shift_s", [2, 128, 14, 16])
    o_pool_p = dout("o_pool_p", [2, 128, 4, 15])
    o_pool_s = dout("o_pool_s", [2, 128, 4, 16, 15])
    o_wkv_p = dout("o_wkv_p", [2, 128, 4, 64])
    o_wkv_s = dout("o_wkv_s", [2, 4, 128, 16, 64])
    OUT_KEYS = {"yp", "ys", "o_shift_p", "o_shift_s", "o_pool_p", "o_pool_s", "o_wkv_p", "o_wkv_s"}

    def sb(name, shape, dt=F32):
        return nc.alloc_sbuf_tensor(name, shape, dt)

    ident = sb("ident", [128, 128], BF16)
    onesmean = sb("onesmean", [128, 128], BF16)
    blockmean = sb("blockmean", [128, 128], BF16)
    blockones = sb("blockones", [128, 128], BF16)
    m_su = [sb("m_su%d" % i, [128, 128], BF16) for i in range(2)]
    m_iu = [sb("m_iu%d" % i, [128, 128], BF16) for i in range(2)]
    m_sl = [sb("m_sl%d" % i, [128, 128], BF16) for i in range(2)]
    qmask = sb("qmask", [128, 16], BF16)
    qsel = sb("qsel", [128, 16, 128], BF16)
    rmask = [sb("rmask0", [128, NT]), sb("rmask1", [128, 128])]
    rcs = sb("rcs", [128, 4, 16])
    epsc = sb("epsc", [128, 4])
    pv = sb("pvt", [128, 2, NPV])
    omka = sb("omka", [128, 2, 4])
    halfp = sb("halfp", [128, 2, 8])
    wsm = sb("wsm", [128, 2, 2048], BF16)
    x2 = [sb("x%d" % i, [128, 8, NT]) for i in range(2)]
    hT2 = [sb("hT%d" % i, [128, 8, NT], BF16) for i in range(2)]
    xsq2 = [[sb("xsq%d_%d" % (ph, i), [128, NT], BF16) for i in range(2)] for ph in range(2)]
    rstd2 = [sb("rstd%d" % ph, [128, NT]) for ph in range(2)]
    sdt2 = [sb("sdt%d" % ph, [128, NT]) for ph in range(2)]
    ZW = NT + 1
    UW = 16 * 23
    zbuf = sb("zbuf", [128, 14, ZW])
    ubuf = sb("ubuf", [128, 4, UW])
    stg_z = sb("stg_z", [128, 14, 16])
    carry_z = sb("carry_z", [128, 2, 14])
    carry_u = sb("carry_u", [128, 2, 4, 15])
    Hst = sb("Hst", [128, 2, 4, 64])
    Hpad = sb("Hpad", [128, 2, 4, 128], BF16)
    tl = sb("tl", [128, NT], BF16)
    sgl = sb("sgl", [128, NT], BF16)
    tnames = ["td", "r_c", "k_c", "v_c", "sg", "csg", "cprev", "a_in", "ssd", "rn", "kkn", "bsc",
              "tmpa", "kmod", "Epv", "Emn", "Epl"]
    T = {n: sb("t_" + n, [128, NT]) for n in tnames}
    TB = {n: sb("tb_" + n, [128, NT], BF16) for n in ["kksq", "rkb", "yb", "ycsq"]}
    aT = sb("aT", [128, 4, NT], BF16)
    bT = sb("bT", [128, 4, NT], BF16)
    kT = sb("kT", [128, 4, NT], BF16)
    rT = sb("rT", [128, 4, NT], BF16)
    vb = sb("vb", [128, 4, NT], BF16)
    aTp = sb("aTp", [128, 4, 2, NT], BF16)
    rTp = sb("rTp", [128, 4, 2, NT], BF16)
    g_all = sb("g_all", [128, 4, NT])
    bonus = sb("bonus", [128, 4, NT])
    WC = sb("WC", [128, 4, 16])
    yT = sb("yT", [128, 4, NT])
    amix2 = [sb("amix%d" % i, [128, 4, NT], BF16) for i in range(2)]
    NCH = NT // 128
    Btok = sb("Btok", [128, NCH, 512], BF16)
    Ktok = sb("Ktok", [128, NCH, 512], BF16)
    Vtok = sb("Vtok", [128, NCH, 512], BF16)
    Vpad = sb("Vpad", [128, NCH, 8, 128], BF16)
    ArbT = sb("ArbT", [128, 4, 128], BF16)
    LakT = sb("LakT", [128, 4, 128], BF16)
    ArkT = sb("ArkT", [128, 4, 128], BF16)
    Mm = [sb("Mm%d" % i, [128, 4, 128], BF16) for i in range(2)]
    Nm = [sb("Nm%d" % i, [128, 4, 128], BF16) for i in range(2)]
    Qm = [sb("Qm%d" % i, [128, 4, 128], BF16) for i in range(2)]
    R0 = sb("R0", [128, 4, 64], BF16)
    Usb = sb("Usb", [128, 4, 64], BF16)
    Upad = sb("Upad", [128, 4, 128], BF16)
    tmpH = sb("tmpH", [128, 2, 64])
    H0s = sb("H0s", [128, 16, 64])
    Hpad_s = sb("Hpad_s", [128, 16, 128], BF16)
    Hout_s = H0s
    Uexp = sb("Uexp", [128, 16, 128], BF16)
    Vexp = sb("Vexp", [128, 16, 128], BF16)
    aTm = sb("aTm", [128, 16, 128], BF16)
    sA = sb("sA", [128, UW])
    sB = sb("sB", [128, UW])
    p_in = sb("p_in", [128, 4, NT], BF16)
    pmix2 = [sb("pmix%d" % i, [128, 4, NT], BF16) for i in range(2)]
    tmpp = sb("tmpp", [128, NT])
    ga = sb("ga", [128, NT])
    gb = sb("gb", [128, NT])
    merged = sb("merged", [128, 8, NT], BF16)
    rtmp = [sb("rtmp%d" % i, [128, NT]) for i in range(2)]
    NHID = 8
    hid = sb("hid", [128, NHID, NT], BF16)
    wslots = [sb("wslot%d" % i, [128, 8, 128], BF16) for i in range(NSLOT)]
    banks = [nc.alloc_psum_tensor("bank%d" % i, [128, 512], F32) for i in range(8)]
    bank_ctr = [0, 0]

    phase = [0]
    BANK_POOLS = ([0, 1, 2, 3], [4, 5, 6])

    def nb():
        ph = phase[0]
        pool_ = BANK_POOLS[ph]
        b = pool_[bank_ctr[ph] % len(pool_)]
        bank_ctr[ph] += 1
        return banks[b], ("ps", b)

    def dve(name, *a, reads, writes, **kw):
        P.add("dve", lambda e: getattr(e, name)(*a, **kw), reads, writes)

    def act(out, in_, func, reads, writes, bias=0.0, scale=1.0):
        P.add("act", lambda e: e.activation(out, in_, func, bias=bias, scale=scale), reads, writes)

    def pool(name, *a, reads, writes, **kw):
        P.add("pool", lambda e: getattr(e, name)(*a, **kw), reads, writes)

    def mm(out, lhsT, rhs, start, stop, reads, writes):
        P.add("pe", lambda e: e.matmul(out, lhsT, rhs, start=start, stop=stop), reads, writes)

    def dma(q, out, in_, reads, writes):
        P.add(q, lambda e: e.dma_start(out=out, in_=in_), reads, writes, dma=True)

    def bc(ap2d, n):
        return ap2d.unsqueeze(1).to_broadcast([ap2d.shape[0], n, ap2d.shape[1]])

    tiles = [Geo(NT, 1, False, ti == 0, ti == n_pt - 1, ti * NT) for ti in range(n_pt)]
    tiles.append(Geo(128, 16, True, False, False, 0))
    wst = {"issued": 0, "pos": 0, "order": [], "mode": "record"}
    seq = []

    def w_take(l, j):
        if wst["mode"] == "record" or P.dry:
            wst["order"].append((l, j))
            wst["pos"] = wst.get("pos", 0) + 1
            return wslots[0], ("ws", 0)
        i = wst["pos"]
        lim = min(len(seq), i + NSLOT - 1)
        while wst["issued"] < lim:
            ii = wst["issued"]
            ll, jj = seq[ii]
            dma("pool", wslots[ii % NSLOT][:].rearrange("p k n -> p (k n)"), wstream_d[ll, jj],
                reads=[], writes=[("ws", ii % NSLOT)])
            wst["issued"] += 1
        assert seq[i] == (l, j), (seq[i], l, j)
        wst["pos"] = i + 1
        return wslots[i % NSLOT], ("ws", i % NSLOT)

    pool("memset", ident[:], 0.0, reads=[], writes=["ident"])
    pool("affine_select", ident[:], ident[:], pattern=[[-1, 128]], compare_op=ALU.not_equal, fill=1.0,
         base=0, channel_multiplier=1, reads=["ident"], writes=["ident"])
    pool("memset", onesmean[:], 1.0 / 1024.0, reads=[], writes=["onesmean"])
    for nm, tt, val in (("blockmean", blockmean, 1.0 / 64.0), ("blockones", blockones, 1.0)):
        pool("memset", tt[:], val, reads=[], writes=[nm])
        v3 = tt[:].rearrange("p (a b) -> p a b", b=64)
        pool("affine_select", v3, v3, pattern=[[-64, 2], [0, 64]], compare_op=ALU.is_ge, fill=0.0,
             base=0, channel_multiplier=1, reads=[nm], writes=[nm])
        pool("affine_select", v3, v3, pattern=[[64, 2], [0, 64]], compare_op=ALU.is_ge, fill=0.0,
             base=63, channel_multiplier=-1, reads=[nm], writes=[nm])
    for i in range(2):
        for nm, tt, cmp_, cm in (("m_su", m_su[i], ALU.is_gt, -1), ("m_iu", m_iu[i], ALU.is_ge, -1),
                                 ("m_sl", m_sl[i], ALU.is_gt, 1)):
            key = nm + str(i)
            pool("memset", tt[:], 1.0, reads=[], writes=[key])
            pool("affine_select", tt[:], tt[:], pattern=[[-cm, 128]], compare_op=cmp_, fill=0.0,
                 base=0, channel_multiplier=cm, reads=[key], writes=[key])
            if i == 1:
                v3 = tt[:].rearrange("p (q r) -> p q r", r=8)
                pool("affine_select", v3, v3, pattern=[[-8, 16], [0, 8]], compare_op=ALU.is_ge, fill=0.0,
                     base=0, channel_multiplier=1, reads=[key], writes=[key])
                pool("affine_select", v3, v3, pattern=[[8, 16], [0, 8]], compare_op=ALU.is_ge, fill=0.0,
                     base=7, channel_multiplier=-1, reads=[key], writes=[key])
    pool("memset", qmask[:], 1.0, reads=[], writes=["qmask"])
    pool("affine_select", qmask[:], qmask[:], pattern=[[-8, 16]], compare_op=ALU.is_ge, fill=0.0,
         base=0, channel_multiplier=1, reads=["qmask"], writes=["qmask"])
    pool("affine_select", qmask[:], qmask[:], pattern=[[8, 16]], compare_op=ALU.is_ge, fill=0.0,
         base=7, channel_multiplier=-1, reads=["qmask"], writes=["qmask"])
    pool("memset", qsel[:], 0.0, reads=[], writes=["qsel"])
    for q in range(16):
        pool("memset", qsel[:, q, 8 * q:8 * q + 8], 1.0, reads=["qsel"], writes=["qsel"])
    for i, (tt, per) in enumerate(((rmask[0], 128), (rmask[1], 8))):
        pool("memset", tt[:], 1.0, reads=[], writes=["rmask%d" % i])
        pool("memset", tt[:].rearrange("p (a b) -> p a b", b=per)[:, :, 0:1], 0.0,
             reads=["rmask%d" % i], writes=["rmask%d" % i])
    for gi in range(4):
        win = 2 << gi
        pool("memset", rcs[:, gi, :], 1.0 / win, reads=[], writes=["rcs"])
        for t in range(win - 1):
            pool("memset", rcs[:, gi, t:t + 1], 1.0 / (t + 1), reads=["rcs"], writes=["rcs"])
    pool("memset", epsc[:, 0:1], RMS_EPS, reads=[], writes=["epsc"])
    pool("memset", epsc[:, 1:2], GN_EPS, reads=["epsc"], writes=["epsc"])
    pool("memset", epsc[:, 2:3], 0.0, reads=["epsc"], writes=["epsc"])
    pool("memset", carry_z[:], 0.0, reads=[], writes=["carry_z"])
    pool("memset", carry_u[:], 0.0, reads=[], writes=["carry_u"])
    pool("memset", Hst[:], 0.0, reads=[], writes=["Hst0", "Hst1"])
    pool("memset", Hpad[:], 0.0, reads=[], writes=["Hpad0", "Hpad1"])
    pool("memset", Hpad_s[:], 0.0, reads=[], writes=["Hpad_s"])
    pool("memset", Upad[:], 0.0, reads=[], writes=["Upad"])
    pool("memset", aTp[:], 0.0, reads=[], writes=[("aTp", c) for c in range(4)])
    pool("memset", rTp[:], 0.0, reads=[], writes=[("rTp", c) for c in range(4)])
    pool("memset", Vpad[:], 0.0, reads=[], writes=["Vpad"])
    dma("sp", pv[:], pvd.rearrange("l p n -> p l n"), reads=[], writes=["pv"])
    dma("pool", wsm[:], wsmall_d.rearrange("l p n -> p l n"), reads=[], writes=["wsm"])
    dve("tensor_scalar", omka[:], pv[:, :, 42:46], -1.0, 1.0, ALU.mult, ALU.add, reads=["pv"], writes=["omka"])
    dve("tensor_scalar_mul", halfp[:], pv[:, :, 30:38], 0.5, reads=["pv"], writes=["halfp"])

    def pcol(l, col):
        return pv[:, l, col:col + 1]

    def rmsnorm(geo, l, gcol0, out_bf16, pp, ph):
        Nt = geo.Nt
        x, hT, xsq, rstd, sdt = x2[pp], hT2[pp], xsq2[ph], rstd2[ph], sdt2[ph]
        bank, bk = nb()
        for kc in range(8):
            xs_ = xsq[kc % 2]
            act(xs_[:, :Nt], x[:, kc, :Nt], AF.Square, reads=[("x", pp, kc)], writes=[("xsq", ph, kc % 2)])
            mm(bank[:, :Nt], onesmean[:], xs_[:, :Nt], kc == 0, kc == 7,
               reads=["onesmean", ("xsq", ph, kc % 2)], writes=[bk])
        act(sdt[:, :Nt], bank[:, :Nt], AF.Sqrt, reads=[bk, "epsc"], writes=[("sdt", ph)], bias=epsc[:, 0:1])
        dve("reciprocal", rstd[:, :Nt], sdt[:, :Nt], reads=[("sdt", ph)], writes=[("rstd", ph)])
        if out_bf16:
            for kc in range(8):
                dve("scalar_tensor_tensor", hT[:, kc, :Nt], x[:, kc, :Nt], pcol(l, gcol0 + kc), rstd[:, :Nt],
                    ALU.mult, ALU.mult, reads=[("x", pp, kc), ("rstd", ph), "pv"], writes=[("hT", pp, kc)])

    def big_mm(geo, slot, sk, nk, rhs_fn, rhs_keys):
        Nt = geo.Nt
        bank, bk = nb()
        for k in range(nk):
            mm(bank[:, :Nt], slot[:, k, :], rhs_fn(k), k == 0, k == nk - 1,
               reads=[sk] + rhs_keys(k), writes=[bk])
        return bank, bk

    def g3(ap2d, geo):
        return ap2d.rearrange("p (g t) -> p g t", g=geo.G)

    def zview(c, geo):
        return zbuf[:, c, 0:geo.G * (1 + geo.L)].rearrange("p (g t) -> p g t", g=geo.G)

    def uview(gi, geo):
        return ubuf[:, gi, 0:geo.G * (15 + geo.L)].rearrange("p (g t) -> p g t", g=geo.G)

    def shift(geo, l, c, out_ap, wkey, rows=slice(0, 128)):
        Nt, L = geo.Nt, geo.L
        zv = zview(c, geo)
        dve("tensor_tensor", g3(T["td"][rows, :Nt], geo), zv[rows, :, 0:L], zv[rows, :, 1:1 + L], ALU.subtract,
            reads=[("z", c)], writes=["td"])
        dve("scalar_tensor_tensor", g3(out_ap, geo), g3(T["td"][rows, :Nt], geo), pv[rows, l, 16 + c:17 + c],
            zv[rows, :, 1:1 + L], ALU.mult, ALU.add, reads=["td", ("z", c), "pv"], writes=[wkey])

    def front(geo, l, pp):
        Nt, G, L, nch = geo.Nt, geo.G, geo.L, geo.nch
        sm = 1 if geo.sample else 0
        Hk = "Hst%d" % l
        Hpk = "Hpad%d" % l
        x, hT, amix, pmix = x2[pp], hT2[pp], amix2[pp], pmix2[pp]
        if l == 0:
            xkeys = [("x", pp, k) for k in range(8)]
            if geo.sample:
                dma("sp", x[:, :, :Nt], xs.rearrange("k p t -> p k t"), reads=[], writes=xkeys)
            else:
                dma("sp", x[:, :, :Nt], xp.rearrange("k p t -> p k t")[:, :, geo.tok0:geo.tok0 + Nt],
                    reads=[], writes=xkeys)
        zall = [("z", c) for c in range(14)]
        uall = [("u", gi) for gi in range(4)]
        if geo.sample:
            dma("sp", stg_z[:], st_shift[l], reads=[], writes=["stg_z"])
            dve("tensor_copy", zbuf[:, :, 0:G * (1 + L)].rearrange("p c (g t) -> p c g t", g=G)[:, :, :, 0], stg_z[:],
                reads=["stg_z"], writes=zall)
            for gi in range(4):
                dma("sp", uview(gi, geo)[:, :, 0:15], st_pool[l, :, gi], reads=[], writes=[("u", gi)])
        else:
            dve("tensor_copy", zbuf[:, :, 0:1], carry_z[:, l, :].unsqueeze(2), reads=["carry_z"], writes=zall)
            dve("tensor_copy", ubuf[:, :, 0:15], carry_u[:, l, :, :], reads=["carry_u"], writes=uall)
        rmsnorm(geo, l, 0, True, pp, 0)
        yield
        hkeys = lambda k: [("hT", pp, k)]
        hrhs = lambda k: hT[:, k, :Nt]
        for j, c in enumerate(WIN_ORDER):
            slot, sk = w_take(l, j)
            bank, bk = big_mm(geo, slot, sk, 8, hrhs, hkeys)
            if c < 14:
                act(zview(c, geo)[:, :, 1:1 + L], g3(bank[:, :Nt], geo), AF.Copy, reads=[bk], writes=[("z", c)])
            else:
                gi = c - 14
                act(uview(gi, geo)[:, :, 15:15 + L], g3(bank[:, :Nt], geo), AF.Copy, reads=[bk], writes=[("u", gi)])
            yield
        if geo.sample:
            dve("tensor_copy", stg_z[:], zbuf[:, :, 0:G * (1 + L)].rearrange("p c (g t) -> p c g t", g=G)[:, :, :, L],
                reads=zall, writes=["stg_z"])
            dma("sp", o_shift_s[l], stg_z[:], reads=["stg_z"], writes=["o_shift_s"])
            for gi in range(4):
                dma("sp", o_pool_s[l, :, gi], uview(gi, geo)[:, :, L:L + 15], reads=[("u", gi)], writes=["o_pool_s"])
        else:
            dve("tensor_copy", carry_z[:, l, :].unsqueeze(2), zbuf[:, :, L:L + 1], reads=zall, writes=["carry_z"])
            dve("tensor_copy", carry_u[:, l, :, :], ubuf[:, :, L:L + 15], reads=uall, writes=["carry_u"])
            if geo.last:
                dma("sp", o_shift_p[l], carry_z[:, l, :], reads=["carry_z"], writes=["o_shift_p"])
                dma("sp", o_pool_p[l], carry_u[:, l, :, :], reads=["carry_u"], writes=["o_pool_p"])
        shift(geo, l, 12, T["r_c"][:, :Nt], "r_c")
        act(tl[0:64, :Nt], T["r_c"][0:64, :Nt], AF.Tanh, reads=["r_c"], writes=["tl"])
        act(tl[64:128, :Nt], T["r_c"][64:128, :Nt], AF.Copy, reads=["r_c"], writes=["tl"])
        shift(geo, l, 13, T["k_c"][:, :Nt], "k_c")
        yield
        act(T["tmpa"][:, :Nt], T["k_c"][:, :Nt], AF.Tanh, reads=["k_c"], writes=["tmpa"], scale=0.5)
        dve("tensor_scalar", sgl[:, :Nt], T["tmpa"][:, :Nt], 0.5, 0.5, ALU.mult, ALU.add, reads=["tmpa"], writes=["sgl"])
        yield "pre_D"
        for c in range(4):
            r_c, k_c, v_c = T["r_c"][:, :Nt], T["k_c"][:, :Nt], T["v_c"][:, :Nt]
            bank_d, bk_d = nb()
            mm(bank_d[:, :Nt], wsm[:, l, c * 128:(c + 1) * 128], tl[:, :Nt], True, True,
               reads=["wsm", "tl"], writes=[bk_d])
            bank_a, bk_a = nb()
            mm(bank_a[:, :Nt], wsm[:, l, 512 + c * 128:512 + (c + 1) * 128], tl[:, :Nt], True, True,
               reads=["wsm", "tl"], writes=[bk_a])
            bank_g, bk_g = nb()
            mm(bank_g[:, :Nt], wsm[:, l, 1024 + c * 128:1024 + (c + 1) * 128], sgl[:, :Nt], True, True,
               reads=["wsm", "sgl"], writes=[bk_g])
            shift(geo, l, 4 + c, T["k_c"][:, :Nt], "k_c")
            yield
            act(T["sg"][:, :Nt], bank_d[:, :Nt], AF.Tanh, reads=[bk_d, "halfp"], writes=["sg"], bias=halfp[:, l, c:c + 1], scale=0.5)
            act(T["a_in"][:, :Nt], bank_a[:, :Nt], AF.Tanh, reads=[bk_a, "halfp"], writes=["a_in"], bias=halfp[:, l, 4 + c:5 + c], scale=0.5)
            act(g_all[:, c, :Nt], bank_g[:, :Nt], AF.Copy, reads=[bk_g], writes=[("g_all", c)])
            act(TB["kksq"][:, :Nt], k_c, AF.Square, reads=["k_c", "pv"], writes=["kksq"], scale=pcol(l, 38 + c))
            bank_s, bk_s = nb()
            mm(bank_s[:, :Nt], blockones[:], TB["kksq"][:, :Nt], True, True, reads=["blockones", "kksq"], writes=[bk_s])
            yield
            dve("tensor_scalar", T["sg"][:, :Nt], T["sg"][:, :Nt], 0.5, 0.5, ALU.mult, ALU.add, reads=["sg"], writes=["sg"])
            dve("tensor_tensor_scan", T["csg"][:, :Nt], rmask[sm][:, :Nt], T["sg"][:, :Nt], 0.0, ALU.mult, ALU.add,
                reads=["sg", "rmask%d" % sm], writes=["csg"])
            dve("tensor_tensor", T["cprev"][:, :Nt], T["csg"][:, :Nt], T["sg"][:, :Nt], ALU.subtract,
                reads=["csg", "sg"], writes=["cprev"])
            act(T["ssd"][:, :Nt], bank_s[:, :Nt], AF.Sqrt, reads=[bk_s], writes=["ssd"])
            act(T["Epl"][:, :Nt], T["csg"][:, :Nt], AF.Exp, reads=["csg"], writes=["Epl"], scale=-DEC_SCALE)
            act(T["Emn"][:, :Nt], T["csg"][:, :Nt], AF.Exp, reads=["csg"], writes=["Emn"], scale=DEC_SCALE)
            act(T["Epv"][:, :Nt], T["cprev"][:, :Nt], AF.Exp, reads=["cprev"], writes=["Epv"], scale=-DEC_SCALE)
            dve("tensor_scalar", T["a_in"][:, :Nt], T["a_in"][:, :Nt], 0.5, 0.5, ALU.mult, ALU.add, reads=["a_in"], writes=["a_in"])
            shift(geo, l, c, T["r_c"][:, :Nt], "r_c")
            shift(geo, l, 8 + c, T["v_c"][:, :Nt], "v_c")
            yield
            dve("tensor_scalar_max", T["ssd"][:, :Nt], T["ssd"][:, :Nt], 1e-12, reads=["ssd"], writes=["ssd"])
            dve("reciprocal", T["rn"][:, :Nt], T["ssd"][:, :Nt], reads=["ssd"], writes=["rn"])
            dve("scalar_tensor_tensor", T["kkn"][:, :Nt], k_c, pcol(l, 38 + c), T["rn"][:, :Nt], ALU.mult, ALU.mult,
                reads=["k_c", "rn", "pv"], writes=["kkn"])
            dve("tensor_scalar", T["tmpa"][:, :Nt], T["a_in"][:, :Nt], pcol(l, 42 + c), omka[:, l, c:c + 1],
                ALU.mult, ALU.add, reads=["a_in", "pv", "omka"], writes=["tmpa"])
            dve("tensor_tensor", T["kmod"][:, :Nt], k_c, T["tmpa"][:, :Nt], ALU.mult,
                reads=["k_c", "tmpa"], writes=["kmod"])
            dve("tensor_tensor", T["bsc"][:, :Nt], T["kkn"][:, :Nt], T["a_in"][:, :Nt], ALU.mult,
                reads=["kkn", "a_in"], writes=["bsc"])
            dve("scalar_tensor_tensor", TB["rkb"][:, :Nt], r_c, pcol(l, 46 + c), T["kmod"][:, :Nt], ALU.mult, ALU.mult,
                reads=["r_c", "kmod", "pv"], writes=["rkb"])
            bank_b, bk_b = nb()
            mm(bank_b[:, :Nt], blockones[:], TB["rkb"][:, :Nt], True, True, reads=["blockones", "rkb"], writes=[bk_b])
            act(vb[:, c, :Nt], v_c, AF.Copy, reads=["v_c"], writes=[("vb", c)])
            yield
            ngr = G if geo.sample else nch
            per = L if geo.sample else 128
            dve("tensor_copy", WC[:, c, 0:ngr],
                T["Epl"][:, :Nt].rearrange("p (a b) -> p a b", b=per)[:, :, per - 1],
                reads=["Epl"], writes=[("WC", c)])
            dve("scalar_tensor_tensor", aT[:, c, :Nt], T["kkn"][:, :Nt], -1.0, T["Epv"][:, :Nt], ALU.mult, ALU.mult,
                reads=["kkn", "Epv"], writes=[("aT", c)])
            dve("tensor_tensor", bT[:, c, :Nt], T["bsc"][:, :Nt], T["Emn"][:, :Nt], ALU.mult,
                reads=["bsc", "Emn"], writes=[("bT", c)])
            dve("tensor_tensor", kT[:, c, :Nt], T["kmod"][:, :Nt], T["Emn"][:, :Nt], ALU.mult,
                reads=["kmod", "Emn"], writes=[("kT", c)])
            dve("tensor_tensor", rT[:, c, :Nt], r_c, T["Epl"][:, :Nt], ALU.mult,
                reads=["r_c", "Epl"], writes=[("rT", c)])
            dve("tensor_tensor", bonus[:, c, :Nt], bank_b[:, :Nt], v_c, ALU.mult, reads=[bk_b, "v_c"], writes=[("bonus", c)])
            for hh in range(2):
                pr = slice(hh * 64, hh * 64 + 64)
                act(aTp[pr, c, hh, :Nt], aT[pr, c, :Nt], AF.Copy, reads=[("aT", c)], writes=[("aTp", c)])
                act(rTp[pr, c, hh, :Nt], rT[pr, c, :Nt], AF.Copy, reads=[("rT", c)], writes=[("rTp", c)])
            yield
        for ci in range(nch):
            tok = slice(ci * 128, (ci + 1) * 128)
            for nm, src, dst in (("bT", bT, Btok), ("kT", kT, Ktok), ("vb", vb, Vtok)):
                bank, bk = nb()
                bb = bank[:].bitcast(BF16)
                for c in range(4):
                    P.add("pe", lambda e, bb=bb, src=src, c=c, tok=tok: e.transpose(bb[:, c * 128:(c + 1) * 128], src[:, c, tok], ident[:]),
                          reads=[(nm, c), "ident"], writes=[bk])
                act(dst[:, ci, :], bb[:, 0:512], AF.Copy, reads=[bk], writes=[(nm + "tok", ci)])
            v4 = Vtok[:, ci, :].rearrange("p (c h i) -> p c h i", c=4, h=2)
            vp = Vpad[:, ci, :, :].rearrange("p (c h) n -> p c h n", h=2)
            for hh in range(2):
                act(vp[:, :, hh, hh * 64:(hh + 1) * 64], v4[:, :, hh, :], AF.Copy,
                    reads=[("vbtok", ci)], writes=[("Vpad", ci)])
            yield
        Kfac = 3 if geo.sample else 7
        jobs = [[0], [1], [2], [3]] if geo.sample else [[0, 1], [2, 3]]
        for ci in range(nch):
            tok = slice(ci * 128, (ci + 1) * 128)
            bankY, bkY = banks[7], ("ps", 7)
            for cs in jobs:
                nh = 2 * len(cs)
                heads = [(c, hh) for c in cs for hh in range(2)]

                def five(lhs, lhs_nm, rhs, rhs_nm):
                    bank, bk = nb()
                    for hl, (c, hh) in enumerate(heads):
                        la = lhs[:, c, hh, tok] if lhs_nm in ("aTp", "rTp") else lhs[:, c, tok]
                        ra = rhs[:, c, hh, tok] if rhs_nm in ("aTp", "rTp") else rhs[:, c, tok]
                        mm(bank[:, hl * 128:(hl + 1) * 128], la, ra, True, True,
                           reads=[(lhs_nm, c), (rhs_nm, c)], writes=[bk])
                    return bank, bk

                def evm(dst, dkey, bank, bk, mask, mkey):
                    dve("tensor_tensor", dst[:, 0:nh, :], bank[:, 0:nh * 128].rearrange("p (h t) -> p h t", h=nh),
                        bc(mask[:], nh), ALU.mult, reads=[bk, mkey], writes=[dkey])

                bank, bk = five(bT, "bT", aTp, "aTp")
                evm(Nm[0], "Nm0", bank, bk, m_su[sm], "m_su%d" % sm)
                bank, bk = five(aTp, "aTp", bT, "bT")
                evm(Mm[0], "Mm0", bank, bk, m_sl[sm], "m_sl%d" % sm)
                bank, bk = five(bT, "bT", rTp, "rTp")
                evm(ArbT, "ArbT", bank, bk, m_iu[sm], "m_iu%d" % sm)
                bank, bk = five(kT, "kT", aTp, "aTp")
                evm(LakT, "LakT", bank, bk, m_su[sm], "m_su%d" % sm)
                bank, bk = five(kT, "kT", rTp, "rTp")
                evm(ArkT, "ArkT", bank, bk, m_iu[sm], "m_iu%d" % sm)
                yield
                dve("tensor_tensor", Qm[0][:, 0:nh, :], Nm[0][:, 0:nh, :], bc(ident[:], nh), ALU.add,
                    reads=["Nm0", "ident"], writes=["Qm0"])
                qi = 0
                for k in range(Kfac - 1):
                    a, b = k % 2, (k + 1) % 2
                    bankM, bkM = nb()
                    for hl in range(nh):
                        mm(bankM[:, hl * 128:(hl + 1) * 128], Nm[a][:, hl, :], Mm[a][:, hl, :], True, True,
                           reads=["Nm%d" % a, "Mm%d" % a], writes=[bkM])
                    if k < Kfac - 2:
                        bankN, bkN = nb()
                        for hl in range(nh):
                            mm(bankN[:, hl * 128:(hl + 1) * 128], Mm[a][:, hl, :], Nm[a][:, hl, :], True, True,
                               reads=["Nm%d" % a, "Mm%d" % a], writes=[bkN])
                    yield
                    act(Mm[b][:, 0:nh, :], bankM[:, 0:nh * 128].rearrange("p (h t) -> p h t", h=nh), AF.Copy,
                        reads=[bkM], writes=["Mm%d" % b])
                    if k < Kfac - 2:
                        act(Nm[b][:, 0:nh, :], bankN[:, 0:nh * 128].rearrange("p (h t) -> p h t", h=nh), AF.Copy,
                            reads=[bkN], writes=["Nm%d" % b])
                    yield
                    bank, bk = nb()
                    for hl in range(nh):
                        mm(bank[:, hl * 128:(hl + 1) * 128], Mm[b][:, hl, :], Qm[qi][:, hl, :], True, True,
                           reads=["Mm%d" % b, "Qm%d" % qi], writes=[bk])
                    yield
                    dve("tensor_tensor", Qm[1 - qi][:, 0:nh, :], bank[:, 0:nh * 128].rearrange("p (h t) -> p h t", h=nh),
                        Qm[qi][:, 0:nh, :], ALU.add, reads=[bk, "Qm%d" % qi], writes=["Qm%d" % (1 - qi)])
                    qi = 1 - qi
                Q = Qm[qi]
                Qk = "Qm%d" % qi
                if geo.sample:
                    c = cs[0]
                    dma("sp", H0s[:], st_wkv[l, c], reads=[], writes=["H0s"])
                    for hh in range(2):
                        pr = slice(hh * 64, hh * 64 + 64)
                        act(Hpad_s[pr, :, hh * 64:(hh + 1) * 64], H0s[pr, :, :], AF.Copy, reads=["H0s"], writes=["Hpad_s"])
                    dve("tensor_tensor", aTm[:], aT[:, c, 0:128].unsqueeze(1).to_broadcast([128, 16, 128]), qsel[:], ALU.mult,
                        reads=[("aT", c), "qsel"], writes=["aTm"])
                bankR, bkR = nb()
                for cl, c in enumerate(cs):
                    if geo.sample:
                        for q in range(16):
                            mm(bankR[:, cl * 128:(cl + 1) * 128], aTm[:, q, :], Hpad_s[:, q, :], q == 0, False,
                               reads=["aTm", "Hpad_s"], writes=[bkR])
                    else:
                        mm(bankR[:, cl * 128:(cl + 1) * 128], aT[:, c, tok], Hpad[:, l, c, :], True, False,
                           reads=[("aT", c), Hpk], writes=[bkR])
                    for hh in range(2):
                        hl = 2 * cl + hh
                        h = 2 * c + hh
                        mm(bankR[:, hl * 64:(hl + 1) * 64], LakT[:, hl, :], Vtok[:, ci, h * 64:(h + 1) * 64], False, hh == 1,
                           reads=["LakT", ("vbtok", ci)], writes=[bkR])
                act(R0[:, 0:nh, :], bankR[:, 0:nh * 64].rearrange("p (h i) -> p h i", h=nh), AF.Copy, reads=[bkR], writes=["R0"])
                bankU, bkU = nb()
                for hl in range(nh):
                    mm(bankU[:, hl * 64:(hl + 1) * 64], Q[:, hl, :], R0[:, hl, :], True, True, reads=[Qk, "R0"], writes=[bkU])
                act(Usb[:, 0:nh, :], bankU[:, 0:nh * 64].rearrange("p (h i) -> p h i", h=nh), AF.Copy, reads=[bkU], writes=["Usb"])
                for hl, (c, hh) in enumerate(heads):
                    dve("tensor_copy", Upad[:, hl, hh * 64:(hh + 1) * 64], bankU[:, hl * 64:(hl + 1) * 64],
                        reads=[bkU], writes=["Upad"])
                yield
                for cl, c in enumerate(cs):
                    first = True
                    if not geo.sample:
                        mm(bankY[:, c * 128:(c + 1) * 128], Hpad[:, l, c, :], rT[:, c, tok], True, False,
                           reads=[Hpk, ("rT", c)], writes=[bkY])
                        first = False
                    for hh in range(2):
                        hl = 2 * cl + hh
                        h = 2 * c + hh
                        mm(bankY[:, c * 128:(c + 1) * 128], Upad[:, hl, :], ArbT[:, hl, :], first, False,
                           reads=["Upad", "ArbT"], writes=[bkY])
                        first = False
                        mm(bankY[:, c * 128:(c + 1) * 128], Vpad[:, ci, h, :], ArkT[:, hl, :], False, (not geo.sample) and hh == 1,
                           reads=[("Vpad", ci), "ArkT"], writes=[bkY])
                    if geo.sample:
                        for q in range(16):
                            mm(bankY[:, c * 128 + q * 8:c * 128 + q * 8 + 8], Hpad_s[:, q, :], rT[:, c, q * 8:q * 8 + 8],
                               False, q == 15, reads=["Hpad_s", ("rT", c)], writes=[bkY])
                yield
                if not geo.sample:
                    bankS, bkS = nb()
                    for cl, c in enumerate(cs):
                        mm(bankS[:, cl * 128:(cl + 1) * 128], Btok[:, ci, c * 128:(c + 1) * 128],
                           Usb[:, 2 * cl:2 * cl + 2, :].rearrange("p h i -> p (h i)"), True, False,
                           reads=[("bTtok", ci), "Usb"], writes=[bkS])
                        mm(bankS[:, cl * 128:(cl + 1) * 128], Ktok[:, ci, c * 128:(c + 1) * 128],
                           Vtok[:, ci, c * 128:(c + 1) * 128], False, True,
                           reads=[("kTtok", ci), ("vbtok", ci)], writes=[bkS])
                    c0 = cs[0]
                    ncs = len(cs)
                    for hh in range(2):
                        pr = slice(hh * 64, hh * 64 + 64)
                        psv = bankS[pr, 0:ncs * 128].rearrange("p (c n) -> p c n", c=ncs)[:, :, hh * 64:(hh + 1) * 64]
                        dve("tensor_tensor", tmpH[pr, 0:ncs, :], psv, Hst[pr, l, c0:c0 + ncs, :], ALU.add,
                            reads=[bkS, Hk], writes=["tmpH"])
                        dve("tensor_tensor", Hst[pr, l, c0:c0 + ncs, :], tmpH[pr, 0:ncs, :],
                            WC[pr, c0:c0 + ncs, ci:ci + 1].to_broadcast([64, ncs, 64]), ALU.mult,
                            reads=["tmpH"] + [("WC", c) for c in cs], writes=[Hk])
                        act(Hpad[pr, l, c0:c0 + ncs, hh * 64:(hh + 1) * 64], Hst[pr, l, c0:c0 + ncs, :], AF.Copy,
                            reads=[Hk], writes=[Hpk])
                else:
                    c = cs[0]
                    dve("tensor_tensor", Uexp[:], Usb[:, 0:2, :].rearrange("p h i -> p (h i)").unsqueeze(1).to_broadcast([128, 16, 128]),
                        qmask[:].unsqueeze(2).to_broadcast([128, 16, 128]), ALU.mult, reads=["Usb", "qmask"], writes=["Uexp"])
                    dve("tensor_tensor", Vexp[:], Vtok[:, ci, c * 128:(c + 1) * 128].unsqueeze(1).to_broadcast([128, 16, 128]),
                        qmask[:].unsqueeze(2).to_broadcast([128, 16, 128]), ALU.mult, reads=[("vbtok", ci), "qmask"], writes=["Vexp"])
                    for nbk in range(4):
                        bankS, bkS = nb()
                        mm(bankS[:], Btok[:, ci, c * 128:(c + 1) * 128],
                           Uexp[:, 4 * nbk:4 * nbk + 4, :].rearrange("p q n -> p (q n)"), True, False,
                           reads=[("bTtok", ci), "Uexp"], writes=[bkS])
                        mm(bankS[:], Ktok[:, ci, c * 128:(c + 1) * 128],
                           Vexp[:, 4 * nbk:4 * nbk + 4, :].rearrange("p q n -> p (q n)"), False, True,
                           reads=[("kTtok", ci), "Vexp"], writes=[bkS])
                        for hh in range(2):
                            pr = slice(hh * 64, hh * 64 + 64)
                            psv = bankS[pr, :].rearrange("p (q n) -> p q n", q=4)[:, :, hh * 64:(hh + 1) * 64]
                            dve("tensor_tensor", Hout_s[pr, 4 * nbk:4 * nbk + 4, :], psv, H0s[pr, 4 * nbk:4 * nbk + 4, :], ALU.add,
                                reads=[bkS, "H0s"], writes=["H0s"])
                            dve("tensor_tensor", Hout_s[pr, 4 * nbk:4 * nbk + 4, :], Hout_s[pr, 4 * nbk:4 * nbk + 4, :],
                                WC[pr, c, 4 * nbk:4 * nbk + 4].unsqueeze(2).to_broadcast([64, 4, 64]), ALU.mult,
                                reads=["H0s", ("WC", c)], writes=["H0s"])
                    dma("sp", o_wkv_s[l, c], Hout_s[:], reads=["H0s"], writes=["o_wkv_s"])
            act(yT[:, :, tok], bankY[:, :].rearrange("p (c t) -> p c t", c=4), AF.Copy, reads=[bkY], writes=[("yT", ci)])
            yield
        if geo.last:
            dve("tensor_copy", H0s[:, 0:4, :], Hst[:, l, :, :], reads=[Hk], writes=["H0s"])
            dma("sp", o_wkv_p[l], H0s[:, 0:4, :], reads=["H0s"], writes=["o_wkv_p"])
        W = G * (15 + L)
        for gi in range(4):
            uv = uview(gi, geo)
            sAv = sA[:, 0:W].rearrange("p (g t) -> p g t", g=G)
            sBv = sB[:, 0:W].rearrange("p (g t) -> p g t", g=G)
            E = 15 + L
            src, srck = uv, ("u", gi)
            bufs = [(sAv, "sA"), (sBv, "sB")]
            nsteps = gi + 1
            for s in range(nsteps):
                d = 1 << s
                lo = 15 if s == nsteps - 1 else (2 << s) - 1
                lo = max(lo, (2 << s) - 1) if s < nsteps - 1 else 15
                dst, dstk = bufs[s % 2]
                dve("tensor_tensor", dst[:, :, lo:E], src[:, :, lo:E], src[:, :, lo - d:E - d], ALU.add,
                    reads=[srck], writes=[dstk])
                src, srck = dst, dstk
            ws_ = src[:, :, 15:E]
            win = 2 << gi
            dve("scalar_tensor_tensor", g3(p_in[:, gi, :Nt], geo), ws_, 1.0 / win, uv[:, :, 15:E], ALU.mult, ALU.subtract,
                reads=[srck, ("u", gi)], writes=[("p_in", gi)])
            if geo.first:
                dve("tensor_tensor", tmpp[:, 0:16], src[:, 0, 15:31], rcs[:, gi, :], ALU.mult,
                    reads=[srck, "rcs"], writes=["tmpp"])
                dve("tensor_tensor", p_in[:, gi, 0:16], tmpp[:, 0:16], uv[:, 0, 15:31], ALU.subtract,
                    reads=["tmpp", ("u", gi)], writes=[("p_in", gi)])
            bank, bk = nb()
            mm(bank[:, :Nt], wsm[:, l, 1536 + gi * 128:1536 + (gi + 1) * 128], p_in[:, gi, :Nt], True, True,
               reads=["wsm", ("p_in", gi)], writes=[bk])
            act(pmix[:, gi, :Nt], bank[:, :Nt], AF.Identity, reads=[bk, "pv"], writes=[("pmix", pp, gi)], scale=pcol(l, 58 + gi))
            yield

    def back(geo, l, pp):
        Nt, G, L, nch = geo.Nt, geo.G, geo.L, geo.nch
        x, hT, amix, pmix = x2[pp], hT2[pp], amix2[pp], pmix2[pp]
        hkeys = lambda k: [("hT", pp, k)]
        hrhs = lambda k: hT[:, k, :Nt]
        ytk = [("yT", ci) for ci in range(nch)]
        for c in range(4):
            act(TB["yb"][:, :Nt], yT[:, c, :Nt], AF.Copy, reads=ytk, writes=["yb"])
            bank, bk = nb()
            mm(bank[:, :Nt], blockmean[:], TB["yb"][:, :Nt], True, True, reads=["blockmean", "yb"], writes=[bk])
            dve("tensor_tensor", T["kkn"][:, :Nt], yT[:, c, :Nt], bank[:, :Nt], ALU.subtract, reads=ytk + [bk], writes=["kkn"])
            act(TB["ycsq"][:, :Nt], T["kkn"][:, :Nt], AF.Square, reads=["kkn"], writes=["ycsq"])
            bank, bk = nb()
            mm(bank[:, :Nt], blockmean[:], TB["ycsq"][:, :Nt], True, True, reads=["blockmean", "ycsq"], writes=[bk])
            act(T["ssd"][:, :Nt], bank[:, :Nt], AF.Sqrt, reads=[bk, "epsc"], writes=["ssd"], bias=epsc[:, 1:2])
            dve("reciprocal", T["rn"][:, :Nt], T["ssd"][:, :Nt], reads=["ssd"], writes=["rn"])
            dve("tensor_tensor", T["bsc"][:, :Nt], T["kkn"][:, :Nt], T["rn"][:, :Nt], ALU.mult, reads=["kkn", "rn"], writes=["bsc"])
            dve("tensor_scalar", T["bsc"][:, :Nt], T["bsc"][:, :Nt], pcol(l, 50 + c), pcol(l, 54 + c), ALU.mult, ALU.add,
                reads=["bsc", "pv"], writes=["bsc"])
            dve("tensor_tensor", T["bsc"][:, :Nt], T["bsc"][:, :Nt], bonus[:, c, :Nt], ALU.add,
                reads=["bsc", ("bonus", c)], writes=["bsc"])
            dve("tensor_tensor", amix[:, c, :Nt], T["bsc"][:, :Nt], g_all[:, c, :Nt], ALU.mult,
                reads=["bsc", ("g_all", c)], writes=[("amix", pp, c)])
            yield
        yield "g_done"
        for m in range(8):
            slot, sk = w_take(l, 18 + 3 * m)
            bankA, bkA = big_mm(geo, slot, sk, 8, hrhs, hkeys)
            act(ga[:, :Nt], bankA[:, :Nt], AF.Tanh, reads=[bkA], writes=["ga"], scale=0.5)
            slot, sk = w_take(l, 18 + 3 * m + 1)
            bankB, bkB = big_mm(geo, slot, sk, 8, hrhs, hkeys)
            act(gb[:, :Nt], bankB[:, :Nt], AF.Tanh, reads=[bkB], writes=["gb"], scale=0.5)
            slot, sk = w_take(l, 18 + 3 * m + 2)
            banka, bka = nb()
            for c in range(4):
                mm(banka[:, :Nt], slot[:, c, :], amix[:, c, :Nt], c == 0, c == 3, reads=[sk, ("amix", pp, c)], writes=[bka])
            bankb, bkb = nb()
            for gi in range(4):
                mm(bankb[:, :Nt], slot[:, 4 + gi, :], pmix[:, gi, :Nt], gi == 0, gi == 3, reads=[sk, ("pmix", pp, gi)], writes=[bkb])
            dve("scalar_tensor_tensor", ga[:, :Nt], ga[:, :Nt], 1.0, banka[:, :Nt], ALU.add, ALU.mult, reads=["ga", bka], writes=["ga"])
            dve("scalar_tensor_tensor", gb[:, :Nt], gb[:, :Nt], 1.0, bankb[:, :Nt], ALU.add, ALU.mult, reads=["gb", bkb], writes=["gb"])
            dve("tensor_tensor", merged[:, m, :Nt], ga[:, :Nt], gb[:, :Nt], ALU.add, reads=["ga", "gb"], writes=[("merged", m)])
            yield
        for m in range(8):
            slot, sk = w_take(l, 42 + m)
            bank, bk = big_mm(geo, slot, sk, 8, lambda k: merged[:, k, :Nt], lambda k: [("merged", k)])
            dve("scalar_tensor_tensor", x[:, m, :Nt], bank[:, :Nt], 0.5, x[:, m, :Nt], ALU.mult, ALU.add,
                reads=[("x", pp, m), bk], writes=[("x", pp, m)])
        yield
        rmsnorm(geo, l, 8, True, pp, 1)
        yield
        pj = 50
        for fq in range(4):
            for fl in range(8):
                slot, sk = w_take(l, pj)
                pj += 1
                bank, bk = big_mm(geo, slot, sk, 8, hrhs, hkeys)
                rt = rtmp[fl % 2]
                act(rt[:, :Nt], bank[:, :Nt], AF.Relu, reads=[bk], writes=[("rtmp", fl % 2)])
                dve("tensor_tensor", hid[:, fl, :Nt], rt[:, :Nt], rt[:, :Nt], ALU.mult,
                    reads=[("rtmp", fl % 2)], writes=[("hid", fl)])
                yield
            for m in range(8):
                slot, sk = w_take(l, pj)
                pj += 1
                bank, bk = big_mm(geo, slot, sk, 8, lambda k: hid[:, k, :Nt], lambda k: [("hid", k)])
                dve("tensor_tensor", x[:, m, :Nt], x[:, m, :Nt], bank[:, :Nt], ALU.add, reads=[("x", pp, m), bk], writes=[("x", pp, m)])
        assert pj == NPIECE

    def back_full(geo, l, pp):
        yield from back(geo, l, pp)
        if l == 1:
            Nt = geo.Nt
            x = x2[pp]
            rmsnorm(geo, 0, 62, False, pp, 1)
            for kc in range(8):
                yt = rtmp[kc % 2]
                dve("scalar_tensor_tensor", yt[:, :Nt], x[:, kc, :Nt], pcol(0, 62 + kc), rstd2[1][:, :Nt], ALU.mult, ALU.mult,
                    reads=[("x", pp, kc), ("rstd", 1), "pv"], writes=[("rtmp", kc % 2)])
                if geo.sample:
                    dma("sp", ys[kc], yt[:, :Nt], reads=[("rtmp", kc % 2)], writes=["ys"])
                else:
                    dma("sp", yp[kc, :, geo.tok0:geo.tok0 + Nt], yt[:, :Nt], reads=[("rtmp", kc % 2)], writes=["yp"])
            yield

    def count_steps(gen):
        n = 0
        for _ in gen:
            n += 1
        return n

    def emit_all():
        bank_ctr[0] = 0
        bank_ctr[1] = 0
        wst["order"] = []
        nslots = 4 * ((len(tiles) - 1) // 2) + ((len(tiles) - 1) % 2) + 4
        for slot in range(nslots):
            active = []
            for i, geo in enumerate(tiles):
                st = slot - (4 * (i // 2) + (i % 2))
                if 0 <= st < 4:
                    l, isback = st // 2, st % 2
                    active.append((geo, l, i % 2, isback))
            gens = []
            for geo, l, pp, isback in active:
                key = (geo.sample, l, isback)
                if key not in step_cache:
                    was = P.dry
                    P.dry = True
                    saved = (list(bank_ctr), list(wst["order"]), wst.get("pos", 0))
                    phase[0] = isback
                    step_cache[key] = count_steps((back_full if isback else front)(geo, l, pp))
                    bank_ctr[0], bank_ctr[1] = saved[0]
                    wst["order"], wst["pos"] = saved[1], saved[2]
                    P.dry = was
                gens.append([(back_full if isback else front)(geo, l, pp), 0, step_cache[key], isback, False])
            gdone = not any(g[3] for g in gens)
            while gens:
                gens.sort(key=lambda g: g[1] / g[2])
                g = gens[0]
                if g[4] and not gdone:
                    others = [h for h in gens if h is not g]
                    if others:
                        g = others[0]
                try:
                    phase[0] = g[3]
                    v = next(g[0])
                    g[1] += 1
                    if v == "pre_D":
                        g[4] = True
                    elif v == "g_done":
                        gdone = True
                except StopIteration:
                    gens.remove(g)
                    if g[3]:
                        gdone = True

    step_cache = {}
    P.dry = True
    wst["mode"] = "record"
    emit_all()
    seq = wst["order"]
    P.dry = False
    wst["mode"] = "emit"
    wst["issued"] = 0
    wst["pos"] = 0
    emit_all()
    assert wst["pos"] == len(seq)
    P.finish(OUT_KEYS)
    return nc, P


def _wstream(w_in, w_a_up, w_b_up, w_o, w_ff1, w_ff2):
    L = w_in.shape[0]
    out = np.empty((L, NPIECE, 128, 1024), np.float32)

    def pk(mat, ncol0):
        K = mat.shape[0] // 128
        return mat[:, ncol0:ncol0 + 128].reshape(K, 128, 128).transpose(1, 0, 2)

    for l in range(L):
        j = 0
        for c in WIN_ORDER:
            out[l, j] = pk(w_in[l], c * 128).reshape(128, 1024)
            j += 1
        for m in range(8):
            out[l, j] = pk(w_in[l], (18 + m) * 128).reshape(128, 1024)
            j += 1
            out[l, j] = pk(w_in[l], (26 + m) * 128).reshape(128, 1024)
            j += 1
            ab = np.concatenate([pk(w_a_up[l], m * 128), pk(w_b_up[l], m * 128)], axis=1)
            out[l, j] = ab.reshape(128, 1024)
            j += 1
        for m in range(8):
            out[l, j] = pk(w_o[l], m * 128).reshape(128, 1024)
            j += 1
        for fq in range(4):
            for fl in range(8):
                out[l, j] = pk(w_ff1[l], (fq * 8 + fl) * 128).reshape(128, 1024)
                j += 1
            for m in range(8):
                out[l, j] = pk(w_ff2[l][fq * 1024:(fq + 1) * 1024], m * 128).reshape(128, 1024)
                j += 1
        assert j == NPIECE
    return out


_CACHE = {}


def kernel(x_prompt, x_sample, state_shift, state_pool, state_wkv,
           norm1_g, w_in, mu_shift, decay0, w_decay2, a0, w_a2, w_g2, k_k, k_a, r_k,
           ln_x_g, ln_x_b, w_a_up, w_pool, pool_scale, w_b_up, w_o,
           norm2_g, w_ff1, w_ff2, final_norm_g):
    f = lambda a: np.ascontiguousarray(np.asarray(a, dtype=np.float32))
    x_prompt, x_sample, state_shift, state_pool, state_wkv = map(f, (x_prompt, x_sample, state_shift, state_pool, state_wkv))
    B, TP, _ = x_prompt.shape
    assert B == NCORES and TP % NT == 0
    L = 2

    def cols(v, n):
        return f(v).reshape(n, 128).T

    pvs = np.zeros((L, 128, NPV), np.float32)
    for l in range(L):
        pvs[l, :, 0:8] = cols(norm1_g[l], 8)
        pvs[l, :, 8:16] = cols(norm2_g[l], 8)
        pvs[l, :, 16:30] = cols(mu_shift[l], 14)
        pvs[l, :, 30:34] = cols(decay0[l], 4)
        pvs[l, :, 34:38] = cols(a0[l], 4)
        pvs[l, :, 38:42] = cols(k_k[l], 4)
        pvs[l, :, 42:46] = cols(k_a[l], 4)
        pvs[l, :, 46:50] = cols(f(r_k[l]).reshape(-1), 4)
        pvs[l, :, 50:54] = cols(ln_x_g[l], 4)
        pvs[l, :, 54:58] = cols(ln_x_b[l], 4)
        pvs[l, :, 58:62] = cols(pool_scale[l], 4)
        pvs[l, :, 62:70] = cols(final_norm_g, 8)
    wsmall = np.zeros((L, 128, 2048), np.float32)
    for l in range(L):
        wsmall[l, 0:64, 0:512] = f(w_decay2[l])
        wsmall[l, 64:128, 512:1024] = f(w_a2[l])
        wsmall[l, :, 1024:1536] = f(w_g2[l])
        wsmall[l, :, 1536:2048] = f(w_pool[l]).transpose(1, 0, 2).reshape(128, 512)
    wstream = _wstream(f(w_in), f(w_a_up), f(w_b_up), f(w_o), f(w_ff1), f(w_ff2))

    key = TP
    if key not in _CACHE:
        _CACHE[key] = build_program(TP)[0]
    nc = _CACHE[key]
    in_maps = []
    for c in range(NCORES):
        q0 = 16 * c
        xpc = np.ascontiguousarray(x_prompt[c].T.reshape(8, 128, TP))
        xsc = np.ascontiguousarray(x_sample[q0:q0 + 16].reshape(128, D).T.reshape(8, 128, 128))
        sh = np.ascontiguousarray(state_shift[:, q0:q0 + 16, :].reshape(L, 16, 14, 128).transpose(0, 3, 2, 1))
        pl = np.ascontiguousarray(state_pool[:, q0:q0 + 16].reshape(L, 16, 15, 4, 128).transpose(0, 4, 3, 1, 2))
        wk = state_wkv[:, q0:q0 + 16].reshape(L, 16, 4, 2, 64, 64).transpose(0, 2, 3, 5, 1, 4)
        wk = np.ascontiguousarray(wk.reshape(L, 4, 128, 16, 64))
        in_maps.append({"xp": xpc, "xs": xsc, "st_shift": sh, "st_pool": pl, "st_wkv": wk,
                        "pv": pvs, "wsmall": wsmall, "wstream": wstream})
    res = run_bass_kernel_spmd(nc, in_maps, core_ids=list(range(NCORES)))
    R = res.results
    y_prompt = np.stack([R[c]["yp"].reshape(D, TP).T for c in range(NCORES)])
    y_sample = np.concatenate([R[c]["ys"].reshape(D, 128).T.reshape(16, 8, D) for c in range(NCORES)])
    p_shift = np.stack([R[c]["o_shift_p"].reshape(L, 128, 14).transpose(0, 2, 1).reshape(L, DSHIFT) for c in range(NCORES)], axis=1)
    s_shift = np.concatenate([R[c]["o_shift_s"].reshape(L, 128, 14, 16).transpose(0, 3, 2, 1).reshape(L, 16, DSHIFT)
                              for c in range(NCORES)], axis=1)
    p_pool = np.stack([R[c]["o_pool_p"].reshape(L, 128, 4, 15).transpose(0, 3, 2, 1).reshape(L, 15, 512) for c in range(NCORES)], axis=1)
    s_pool = np.concatenate([R[c]["o_pool_s"].reshape(L, 128, 4, 16, 15).transpose(0, 3, 4, 2, 1).reshape(L, 16, 15, 512)
                             for c in range(NCORES)], axis=1)
    p_wkv = np.stack([R[c]["o_wkv_p"].reshape(L, 2, 64, 4, 64).transpose(0, 3, 1, 4, 2).reshape(L, 8, 64, 64)
                      for c in range(NCORES)], axis=1)
    s_wkv = np.concatenate([R[c]["o_wkv_s"].reshape(L, 4, 2, 64, 16, 64).transpose(0, 4, 1, 2, 5, 3).reshape(L, 16, 8, 64, 64)
                            for c in range(NCORES)], axis=1)
    out = (y_prompt, y_sample, p_shift, p_pool, p_wkv, s_shift, s_pool, s_wkv)
    return tuple(np.ascontiguousarray(o, dtype=np.float32) for o in out)
```

```python
import contextlib
import numpy as np
import concourse.bass as bass
import concourse.mybir as mybir
from concourse.bass_utils import run_bass_kernel_spmd

F32 = mybir.dt.float32
BF16 = mybir.dt.bfloat16
AF = mybir.ActivationFunctionType
ALU = mybir.AluOpType

NCORES = 8
D = 1024
DA = 512
DSHIFT = 1792
DIN = 4352
DFF = 4096
NT = 256
NSLOT = 8
NPIECE = 114
NPV = 70
RMS_EPS = 1e-6
GN_EPS = 64 * 1e-5
DEC_SCALE = float(np.exp(-0.5))
WIN_ORDER = [12, 13] + list(range(12)) + [14, 15, 16, 17]


class Op:
    __slots__ = ("eng", "fn", "reads", "writes", "dma", "idx", "deps", "sem", "val", "has_dep")

    def __init__(self, eng, fn, reads, writes, dma):
        self.eng = eng
        self.fn = fn
        self.reads = reads
        self.writes = writes
        self.dma = dma
        self.deps = ()
        self.sem = None
        self.val = 0
        self.has_dep = False


class Prog:
    ENGS = ("pe", "act", "dve", "pool", "sp")
    NDMA = 8

    def __init__(self, nc):
        self.nc = nc
        self.ops = []
        self.dry = False

    def add(self, eng, fn, reads=(), writes=(), dma=False):
        if self.dry:
            return None
        op = Op(eng, fn, tuple(reads), tuple(writes), dma)
        op.idx = len(self.ops)
        self.ops.append(op)
        return op

    def finish(self, final_keys):
        nc = self.nc
        ops = self.ops
        last_w = {}
        readers = {}
        for op in ops:
            deps = set()
            for k in op.reads:
                if k in last_w:
                    deps.add(last_w[k])
            for k in op.writes:
                if k in last_w:
                    deps.add(last_w[k])
                latest = {}
                for r in readers.get(k, ()):
                    ro = ops[r]
                    if ro.dma:
                        deps.add(r)
                    elif ro.eng not in latest or latest[ro.eng] < r:
                        latest[ro.eng] = r
                deps.update(latest.values())
            deps.discard(op.idx)
            op.deps = deps
            for k in op.writes:
                last_w[k] = op.idx
                readers[k] = []
            for k in op.reads:
                readers.setdefault(k, []).append(op.idx)
        final_deps = set()
        for op in ops:
            if op.dma and any(k in final_keys for k in op.writes):
                final_deps.add(op.idx)
        for op in ops:
            for d in op.deps:
                if op.eng == "pe" and ops[d].eng == "pe" and not ops[d].dma and not op.dma:
                    continue
                ops[d].has_dep = True
        with contextlib.ExitStack() as st:
            sem_eng = {e: st.enter_context(nc.semaphore("s_" + e)) for e in ("pe", "act", "dve", "pool")}
            dma_sems = {e: [st.enter_context(nc.semaphore("d_%s%d" % (e, i))) for i in range(self.NDMA)]
                        for e in ("sp", "pool")}
            cnt = {e: 0 for e in self.ENGS}
            dcnt = {e: 0 for e in dma_sems}
            dvals = {e: [0] * self.NDMA for e in dma_sems}
            dlast = {e: [None] * self.NDMA for e in dma_sems}
            for op in ops:
                if op.dma:
                    i = dcnt[op.eng]
                    dcnt[op.eng] += 1
                    slot = i % self.NDMA
                    prev = dlast[op.eng][slot]
                    if prev is not None:
                        op.deps = set(op.deps) | {prev}
                    dvals[op.eng][slot] += 16
                    op.sem = dma_sems[op.eng][slot]
                    op.val = dvals[op.eng][slot]
                    dlast[op.eng][slot] = op.idx
                elif op.has_dep:
                    cnt[op.eng] += 1
                    op.sem = sem_eng[op.eng]
                    op.val = cnt[op.eng]
            self.sem_counts = dict(cnt)
            per_eng = {e: [o for o in ops if o.eng == e] for e in self.ENGS}
            with nc.Block() as block:
                def run(e, engobj):
                    waited = {}
                    for op in per_eng[e]:
                        need = {}
                        for d in op.deps:
                            p = ops[d]
                            if e == "pe" and p.eng == "pe" and not p.dma and not op.dma:
                                continue
                            key = id(p.sem)
                            if waited.get(key, 0) >= p.val:
                                continue
                            if key not in need or need[key][1] < p.val:
                                need[key] = (p.sem, p.val)
                        for key, (s, v) in need.items():
                            engobj.wait_ge(s, v)
                            waited[key] = v
                        ins = op.fn(engobj)
                        if op.sem is not None:
                            ins.then_inc(op.sem, 16 if op.dma else 1)
                    return waited

                @block.sync
                def _(eng):
                    waited = run("sp", eng)
                    need = {}
                    for d in final_deps:
                        p = ops[d]
                        key = id(p.sem)
                        if waited.get(key, 0) >= p.val:
                            continue
                        if key not in need or need[key][1] < p.val:
                            need[key] = (p.sem, p.val)
                    for key, (s, v) in need.items():
                        eng.wait_ge(s, v)

                @block.tensor
                def _(eng):
                    run("pe", eng)

                @block.scalar
                def _(eng):
                    run("act", eng)

                @block.vector
                def _(eng):
                    run("dve", eng)

                @block.gpsimd
                def _(eng):
                    run("pool", eng)


class Geo:
    def __init__(self, Nt, G, sample, first, last, tok0):
        self.Nt = Nt
        self.G = G
        self.L = Nt // G
        self.nch = Nt // 128
        self.sample = sample
        self.first = first
        self.last = last
        self.tok0 = tok0


def build_program(TP):
    nc = bass.Bass("TRN2", target_bir_lowering=False)
    n_pt = TP // NT
    P = Prog(nc)

    def din(name, shape):
        return nc.dram_tensor(name, shape, F32, kind="ExternalInput").ap()

    def dout(name, shape):
        return nc.dram_tensor(name, shape, F32, kind="ExternalOutput").ap()

    xp = din("xp", [8, 128, TP])
    xs = din("xs", [8, 128, 128])
    st_shift = din("st_shift", [2, 128, 14, 16])
    st_pool = din("st_pool", [2, 128, 4, 16, 15])
    st_wkv = din("st_wkv", [2, 4, 128, 16, 64])
    pvd = din("pv", [2, 128, NPV])
    wsmall_d = din("wsmall", [2, 128, 2048])
    wstream_d = din("wstream", [2, NPIECE, 128, 1024])
    yp = dout("yp", [8, 128, TP])
    ys = dout("ys", [8, 128, 128])
    o_shift_p = dout("o_shift_p", [2, 128, 14])
    o_shift_s = dout("o_shift_s", [2, 128, 14, 16])
    o_pool_p = dout("o_pool_p", [2, 128, 4, 15])
    o_pool_s = dout("o_pool_s", [2, 128, 4, 16, 15])
    o_wkv_p = dout("o_wkv_p", [2, 128, 4, 64])
    o_wkv_s = dout("o_wkv_s", [2, 4, 128, 16, 64])
    OUT_KEYS = {"yp", "ys", "o_shift_p", "o_shift_s", "o_pool_p", "o_pool_s", "o_wkv_p", "o_wkv_s"}

    def sb(name, shape, dt=F32):
        return nc.alloc_sbuf_tensor(name, shape, dt)

    ident = sb("ident", [128, 128], BF16)
    onesmean = sb("onesmean", [128, 128], BF16)
    blockmean = sb("blockmean", [128, 128], BF16)
    blockones = sb("blockones", [128, 128], BF16)
    m_su = [sb("m_su%d" % i, [128, 128], BF16) for i in range(2)]
    m_iu = [sb("m_iu%d" % i, [128, 128], BF16) for i in range(2)]
    m_sl = [sb("m_sl%d" % i, [128, 128], BF16) for i in range(2)]
    qmask = sb("qmask", [128, 16], BF16)
    qsel = sb("qsel", [128, 16, 128], BF16)
    rmask = [sb("rmask0", [128, NT]), sb("rmask1", [128, 128])]
    rcs = sb("rcs", [128, 4, 16])
    epsc = sb("epsc", [128, 4])
    pv = sb("pvt", [128, 2, NPV])
    omka = sb("omka", [128, 2, 4])
    halfp = sb("halfp", [128, 2, 8])
    wsm = sb("wsm", [128, 2, 2048], BF16)
    x2 = [sb("x%d" % i, [128, 8, NT]) for i in range(2)]
    hT2 = [sb("hT%d" % i, [128, 8, NT], BF16) for i in range(2)]
    xsq2 = [[sb("xsq%d_%d" % (ph, i), [128, NT], BF16) for i in range(2)] for ph in range(2)]
    rstd2 = [sb("rstd%d" % ph, [128, NT]) for ph in range(2)]
    sdt2 = [sb("sdt%d" % ph, [128, NT]) for ph in range(2)]
    ZW = NT + 1
    UW = 16 * 23
    zbuf = sb("zbuf", [128, 14, ZW])
    ubuf = sb("ubuf", [128, 4, UW])
    stg_z = sb("stg_z", [128, 14, 16])
    carry_z = sb("carry_z", [128, 2, 14])
    carry_u = sb("carry_u", [128, 2, 4, 15])
    Hst = sb("Hst", [128, 2, 4, 64])
    Hpad = sb("Hpad", [128, 2, 4, 128], BF16)
    tl = sb("tl", [128, NT], BF16)
    sgl = sb("sgl", [128, NT], BF16)
    tnames = ["td", "r_c", "k_c", "v_c", "sg", "csg", "cprev", "a_in", "ssd", "rn", "kkn", "bsc",
              "tmpa", "kmod", "Epv", "Emn", "Epl"]
    T = {n: sb("t_" + n, [128, NT]) for n in tnames}
    TB = {n: sb("tb_" + n, [128, NT], BF16) for n in ["kksq", "rkb", "yb", "ycsq"]}
    aT = sb("aT", [128, 4, NT], BF16)
    bT = sb("bT", [128, 4, NT], BF16)
    kT = sb("kT", [128, 4, NT], BF16)
    rT = sb("rT", [128, 4, NT], BF16)
    vb = sb("vb", [128, 4, NT], BF16)
    aTp = sb("aTp", [128, 4, 2, NT], BF16)
    rTp = sb("rTp", [128, 4, 2, NT], BF16)
    g_all = sb("g_all", [128, 4, NT])
    bonus = sb("bonus", [128, 4, NT])
    WC = sb("WC", [128, 4, 16])
    yT = sb("yT", [128, 4, NT])
    amix2 = [sb("amix%d" % i, [128, 4, NT], BF16) for i in range(2)]
    NCH = NT // 128
    Btok = sb("Btok", [128, NCH, 512], BF16)
    Ktok = sb("Ktok", [128, NCH, 512], BF16)
    Vtok = sb("Vtok", [128, NCH, 512], BF16)
    Vpad = sb("Vpad", [128, NCH, 8, 128], BF16)
    ArbT = sb("ArbT", [128, 4, 128], BF16)
    LakT = sb("LakT", [128, 4, 128], BF16)
    ArkT = sb("ArkT", [128, 4, 128], BF16)
    Mm = [sb("Mm%d" % i, [128, 4, 128], BF16) for i in range(2)]
    Nm = [sb("Nm%d" % i, [128, 4, 128], BF16) for i in range(2)]
    Qm = [sb("Qm%d" % i, [128, 4, 128], BF16) for i in range(2)]
    R0 = sb("R0", [128, 4, 64], BF16)
    Usb = sb("Usb", [128, 4, 64], BF16)
    Upad = sb("Upad", [128, 4, 128], BF16)
    tmpH = sb("tmpH", [128, 2, 64])
    H0s = sb("H0s", [128, 16, 64])
    Hpad_s = sb("Hpad_s", [128, 16, 128], BF16)
    Hout_s = H0s
    Uexp = sb("Uexp", [128, 16, 128], BF16)
    Vexp = sb("Vexp", [128, 16, 128], BF16)
    aTm = sb("aTm", [128, 16, 128], BF16)
    sA = sb("sA", [128, UW])
    sB = sb("sB", [128, UW])
    p_in = sb("p_in", [128, 4, NT], BF16)
    pmix2 = [sb("pmix%d" % i, [128, 4, NT], BF16) for i in range(2)]
    tmpp = sb("tmpp", [128, NT])
    ga = sb("ga", [128, NT])
    gb = sb("gb", [128, NT])
    merged = sb("merged", [128, 8, NT], BF16)
    rtmp = [sb("rtmp%d" % i, [128, NT]) for i in range(2)]
    NHID = 8
    hid = sb("hid", [128, NHID, NT], BF16)
    wslots = [sb("wslot%d" % i, [128, 8, 128], BF16) for i in range(NSLOT)]
    banks = [nc.alloc_psum_tensor("bank%d" % i, [128, 512], F32) for i in range(8)]
    bank_ctr = [0, 0]

    phase = [0]
    BANK_POOLS = ([0, 1, 2, 3], [4, 5, 6])

    def nb():
        ph = phase[0]
        pool_ = BANK_POOLS[ph]
        b = pool_[bank_ctr[ph] % len(pool_)]
        bank_ctr[ph] += 1
        return banks[b], ("ps", b)

    def dve(name, *a, reads, writes, **kw):
        P.add("dve", lambda e: getattr(e, name)(*a, **kw), reads, writes)

    def act(out, in_, func, reads, writes, bias=0.0, scale=1.0):
        P.add("act", lambda e: e.activation(out, in_, func, bias=bias, scale=scale), reads, writes)

    def pool(name, *a, reads, writes, **kw):
        P.add("pool", lambda e: getattr(e, name)(*a, **kw), reads, writes)

    def mm(out, lhsT, rhs, start, stop, reads, writes):
        P.add("pe", lambda e: e.matmul(out, lhsT, rhs, start=start, stop=stop), reads, writes)

    def dma(q, out, in_, reads, writes):
        P.add(q, lambda e: e.dma_start(out=out, in_=in_), reads, writes, dma=True)

    def bc(ap2d, n):
        return ap2d.unsqueeze(1).to_broadcast([ap2d.shape[0], n, ap2d.shape[1]])

    tiles = [Geo(NT, 1, False, ti == 0, ti == n_pt - 1, ti * NT) for ti in range(n_pt)]
    tiles.append(Geo(128, 16, True, False, False, 0))
    wst = {"issued": 0, "pos": 0, "order": [], "mode": "record"}
    seq = []

    def w_take(l, j):
        if wst["mode"] == "record" or P.dry:
            wst["order"].append((l, j))
            wst["pos"] = wst.get("pos", 0) + 1
            return wslots[0], ("ws", 0)
        i = wst["pos"]
        lim = min(len(seq), i + NSLOT - 1)
        while wst["issued"] < lim:
            ii = wst["issued"]
            ll, jj = seq[ii]
            dma("pool", wslots[ii % NSLOT][:].rearrange("p k n -> p (k n)"), wstream_d[ll, jj],
                reads=[], writes=[("ws", ii % NSLOT)])
            wst["issued"] += 1
        assert seq[i] == (l, j), (seq[i], l, j)
        wst["pos"] = i + 1
        return wslots[i % NSLOT], ("ws", i % NSLOT)

    pool("memset", ident[:], 0.0, reads=[], writes=["ident"])
    pool("affine_select", ident[:], ident[:], pattern=[[-1, 128]], compare_op=ALU.not_equal, fill=1.0,
         base=0, channel_multiplier=1, reads=["ident"], writes=["ident"])
    pool("memset", onesmean[:], 1.0 / 1024.0, reads=[], writes=["onesmean"])
    for nm, tt, val in (("blockmean", blockmean, 1.0 / 64.0), ("blockones", blockones, 1.0)):
        pool("memset", tt[:], val, reads=[], writes=[nm])
        v3 = tt[:].rearrange("p (a b) -> p a b", b=64)
        pool("affine_select", v3, v3, pattern=[[-64, 2], [0, 64]], compare_op=ALU.is_ge, fill=0.0,
             base=0, channel_multiplier=1, reads=[nm], writes=[nm])
        pool("affine_select", v3, v3, pattern=[[64, 2], [0, 64]], compare_op=ALU.is_ge, fill=0.0,
             base=63, channel_multiplier=-1, reads=[nm], writes=[nm])
    for i in range(2):
        for nm, tt, cmp_, cm in (("m_su", m_su[i], ALU.is_gt, -1), ("m_iu", m_iu[i], ALU.is_ge, -1),
                                 ("m_sl", m_sl[i], ALU.is_gt, 1)):
            key = nm + str(i)
            pool("memset", tt[:], 1.0, reads=[], writes=[key])
            pool("affine_select", tt[:], tt[:], pattern=[[-cm, 128]], compare_op=cmp_, fill=0.0,
                 base=0, channel_multiplier=cm, reads=[key], writes=[key])
            if i == 1:
                v3 = tt[:].rearrange("p (q r) -> p q r", r=8)
                pool("affine_select", v3, v3, pattern=[[-8, 16], [0, 8]], compare_op=ALU.is_ge, fill=0.0,
                     base=0, channel_multiplier=1, reads=[key], writes=[key])
                pool("affine_select", v3, v3, pattern=[[8, 16], [0, 8]], compare_op=ALU.is_ge, fill=0.0,
                     base=7, channel_multiplier=-1, reads=[key], writes=[key])
    pool("memset", qmask[:], 1.0, reads=[], writes=["qmask"])
    pool("affine_select", qmask[:], qmask[:], pattern=[[-8, 16]], compare_op=ALU.is_ge, fill=0.0,
         base=0, channel_multiplier=1, reads=["qmask"], writes=["qmask"])
    pool("affine_select", qmask[:], qmask[:], pattern=[[8, 16]], compare_op=ALU.is_ge, fill=0.0,
         base=7, channel_multiplier=-1, reads=["qmask"], writes=["qmask"])
    pool("memset", qsel[:], 0.0, reads=[], writes=["qsel"])
    for q in range(16):
        pool("memset", qsel[:, q, 8 * q:8 * q + 8], 1.0, reads=["qsel"], writes=["qsel"])
    for i, (tt, per) in enumerate(((rmask[0], 128), (rmask[1], 8))):
        pool("memset", tt[:], 1.0, reads=[], writes=["rmask%d" % i])
        pool("memset", tt[:].rearrange("p (a b) -> p a b", b=per)[:, :, 0:1], 0.0,
             reads=["rmask%d" % i], writes=["rmask%d" % i])
    for gi in range(4):
        win = 2 << gi
        pool("memset", rcs[:, gi, :], 1.0 / win, reads=[], writes=["rcs"])
        for t in range(win - 1):
            pool("memset", rcs[:, gi, t:t + 1], 1.0 / (t + 1), reads=["rcs"], writes=["rcs"])
    pool("memset", epsc[:, 0:1], RMS_EPS, reads=[], writes=["epsc"])
    pool("memset", epsc[:, 1:2], GN_EPS, reads=["epsc"], writes=["epsc"])
    pool("memset", epsc[:, 2:3], 0.0, reads=["epsc"], writes=["epsc"])
    pool("memset", carry_z[:], 0.0, reads=[], writes=["carry_z"])
    pool("memset", carry_u[:], 0.0, reads=[], writes=["carry_u"])
    pool("memset", Hst[:], 0.0, reads=[], writes=["Hst0", "Hst1"])
    pool("memset", Hpad[:], 0.0, reads=[], writes=["Hpad0", "Hpad1"])
    pool("memset", Hpad_s[:], 0.0, reads=[], writes=["Hpad_s"])
    pool("memset", Upad[:], 0.0, reads=[], writes=["Upad"])
    pool("memset", aTp[:], 0.0, reads=[], writes=[("aTp", c) for c in range(4)])
    pool("memset", rTp[:], 0.0, reads=[], writes=[("rTp", c) for c in range(4)])
    pool("memset", Vpad[:], 0.0, reads=[], writes=["Vpad"])
    dma("sp", pv[:], pvd.rearrange("l p n -> p l n"), reads=[], writes=["pv"])
    dma("pool", wsm[:], wsmall_d.rearrange("l p n -> p l n"), reads=[], writes=["wsm"])
    dve("tensor_scalar", omka[:], pv[:, :, 42:46], -1.0, 1.0, ALU.mult, ALU.add, reads=["pv"], writes=["omka"])
    dve("tensor_scalar_mul", halfp[:], pv[:, :, 30:38], 0.5, reads=["pv"], writes=["halfp"])

    def pcol(l, col):
        return pv[:, l, col:col + 1]

    def rmsnorm(geo, l, gcol0, out_bf16, pp, ph):
        Nt = geo.Nt
        x, hT, xsq, rstd, sdt = x2[pp], hT2[pp], xsq2[ph], rstd2[ph], sdt2[ph]
        bank, bk = nb()
        for kc in range(8):
            xs_ = xsq[kc % 2]
            act(xs_[:, :Nt], x[:, kc, :Nt], AF.Square, reads=[("x", pp, kc)], writes=[("xsq", ph, kc % 2)])
            mm(bank[:, :Nt], onesmean[:], xs_[:, :Nt], kc == 0, kc == 7,
               reads=["onesmean", ("xsq", ph, kc % 2)], writes=[bk])
        act(sdt[:, :Nt], bank[:, :Nt], AF.Sqrt, reads=[bk, "epsc"], writes=[("sdt", ph)], bias=epsc[:, 0:1])
        dve("reciprocal", rstd[:, :Nt], sdt[:, :Nt], reads=[("sdt", ph)], writes=[("rstd", ph)])
        if out_bf16:
            for kc in range(8):
                dve("scalar_tensor_tensor", hT[:, kc, :Nt], x[:, kc, :Nt], pcol(l, gcol0 + kc), rstd[:, :Nt],
                    ALU.mult, ALU.mult, reads=[("x", pp, kc), ("rstd", ph), "pv"], writes=[("hT", pp, kc)])

    def big_mm(geo, slot, sk, nk, rhs_fn, rhs_keys):
        Nt = geo.Nt
        bank, bk = nb()
        for k in range(nk):
            mm(bank[:, :Nt], slot[:, k, :], rhs_fn(k), k == 0, k == nk - 1,
               reads=[sk] + rhs_keys(k), writes=[bk])
        return bank, bk

    def g3(ap2d, geo):
        return ap2d.rearrange("p (g t) -> p g t", g=geo.G)

    def zview(c, geo):
        return zbuf[:, c, 0:geo.G * (1 + geo.L)].rearrange("p (g t) -> p g t", g=geo.G)

    def uview(gi, geo):
        return ubuf[:, gi, 0:geo.G * (15 + geo.L)].rearrange("p (g t) -> p g t", g=geo.G)

    def shift(geo, l, c, out_ap, wkey, rows=slice(0, 128)):
        Nt, L = geo.Nt, geo.L
        zv = zview(c, geo)
        dve("tensor_tensor", g3(T["td"][rows, :Nt], geo), zv[rows, :, 0:L], zv[rows, :, 1:1 + L], ALU.subtract,
            reads=[("z", c)], writes=["td"])
        dve("scalar_tensor_tensor", g3(out_ap, geo), g3(T["td"][rows, :Nt], geo), pv[rows, l, 16 + c:17 + c],
            zv[rows, :, 1:1 + L], ALU.mult, ALU.add, reads=["td", ("z", c), "pv"], writes=[wkey])

    def front(geo, l, pp):
        Nt, G, L, nch = geo.Nt, geo.G, geo.L, geo.nch
        sm = 1 if geo.sample else 0
        Hk = "Hst%d" % l
        Hpk = "Hpad%d" % l
        x, hT, amix, pmix = x2[pp], hT2[pp], amix2[pp], pmix2[pp]
        if l == 0:
            xkeys = [("x", pp, k) for k in range(8)]
            if geo.sample:
                dma("sp", x[:, :, :Nt], xs.rearrange("k p t -> p k t"), reads=[], writes=xkeys)
            else:
                dma("sp", x[:, :, :Nt], xp.rearrange("k p t -> p k t")[:, :, geo.tok0:geo.tok0 + Nt],
                    reads=[], writes=xkeys)
        zall = [("z", c) for c in range(14)]
        uall = [("u", gi) for gi in range(4)]
        if geo.sample:
            dma("sp", stg_z[:], st_shift[l], reads=[], writes=["stg_z"])
            dve("tensor_copy", zbuf[:, :, 0:G * (1 + L)].rearrange("p c (g t) -> p c g t", g=G)[:, :, :, 0], stg_z[:],
                reads=["stg_z"], writes=zall)
        else:
            dve("tensor_copy", zbuf[:, :, 0:1], carry_z[:, l, :].unsqueeze(2), reads=["carry_z"], writes=zall)
        rmsnorm(geo, l, 0, True, pp, 0)
        yield
        hkeys = lambda k: [("hT", pp, k)]
        hrhs = lambda k: hT[:, k, :Nt]
        for j, c in enumerate(WIN_ORDER):
            if c == 14:
                yield "pre_U"
                if geo.sample:
                    for gi in range(4):
                        dma("sp", uview(gi, geo)[:, :, 0:15], st_pool[l, :, gi], reads=[], writes=[("u", gi)])
                else:
                    dve("tensor_copy", ubuf[:, :, 0:15], carry_u[:, l, :, :], reads=["carry_u"], writes=uall)
            slot, sk = w_take(l, j)
            bank, bk = big_mm(geo, slot, sk, 8, hrhs, hkeys)
            if c < 14:
                act(zview(c, geo)[:, :, 1:1 + L], g3(bank[:, :Nt], geo), AF.Copy, reads=[bk], writes=[("z", c)])
            else:
                gi = c - 14
                act(uview(gi, geo)[:, :, 15:15 + L], g3(bank[:, :Nt], geo), AF.Copy, reads=[bk], writes=[("u", gi)])
            yield
        if geo.sample:
            dve("tensor_copy", stg_z[:], zbuf[:, :, 0:G * (1 + L)].rearrange("p c (g t) -> p c g t", g=G)[:, :, :, L],
                reads=zall, writes=["stg_z"])
            dma("sp", o_shift_s[l], stg_z[:], reads=["stg_z"], writes=["o_shift_s"])
            for gi in range(4):
                dma("sp", o_pool_s[l, :, gi], uview(gi, geo)[:, :, L:L + 15], reads=[("u", gi)], writes=["o_pool_s"])
        else:
            dve("tensor_copy", carry_z[:, l, :].unsqueeze(2), zbuf[:, :, L:L + 1], reads=zall, writes=["carry_z"])
            dve("tensor_copy", carry_u[:, l, :, :], ubuf[:, :, L:L + 15], reads=uall, writes=["carry_u"])
            if geo.last:
                dma("sp", o_shift_p[l], carry_z[:, l, :], reads=["carry_z"], writes=["o_shift_p"])
                dma("sp", o_pool_p[l], carry_u[:, l, :, :], reads=["carry_u"], writes=["o_pool_p"])
        shift(geo, l, 12, T["r_c"][:, :Nt], "r_c")
        act(tl[0:64, :Nt], T["r_c"][0:64, :Nt], AF.Tanh, reads=["r_c"], writes=["tl"])
        act(tl[64:128, :Nt], T["r_c"][64:128, :Nt], AF.Copy, reads=["r_c"], writes=["tl"])
        shift(geo, l, 13, T["k_c"][:, :Nt], "k_c")
        yield
        act(T["tmpa"][:, :Nt], T["k_c"][:, :Nt], AF.Tanh, reads=["k_c"], writes=["tmpa"], scale=0.5)
        dve("tensor_scalar", sgl[:, :Nt], T["tmpa"][:, :Nt], 0.5, 0.5, ALU.mult, ALU.add, reads=["tmpa"], writes=["sgl"])
        yield "pre_D"
        for c in range(4):
            r_c, k_c, v_c = T["r_c"][:, :Nt], T["k_c"][:, :Nt], T["v_c"][:, :Nt]
            bank_d, bk_d = nb()
            mm(bank_d[:, :Nt], wsm[:, l, c * 128:(c + 1) * 128], tl[:, :Nt], True, True,
               reads=["wsm", "tl"], writes=[bk_d])
            bank_a, bk_a = nb()
            mm(bank_a[:, :Nt], wsm[:, l, 512 + c * 128:512 + (c + 1) * 128], tl[:, :Nt], True, True,
               reads=["wsm", "tl"], writes=[bk_a])
            bank_g, bk_g = nb()
            mm(bank_g[:, :Nt], wsm[:, l, 1024 + c * 128:1024 + (c + 1) * 128], sgl[:, :Nt], True, True,
               reads=["wsm", "sgl"], writes=[bk_g])
            shift(geo, l, 4 + c, T["k_c"][:, :Nt], "k_c")
            yield
            act(T["sg"][:, :Nt], bank_d[:, :Nt], AF.Tanh, reads=[bk_d, "halfp"], writes=["sg"], bias=halfp[:, l, c:c + 1], scale=0.5)
            act(T["a_in"][:, :Nt], bank_a[:, :Nt], AF.Tanh, reads=[bk_a, "halfp"], writes=["a_in"], bias=halfp[:, l, 4 + c:5 + c], scale=0.5)
            act(g_all[:, c, :Nt], bank_g[:, :Nt], AF.Copy, reads=[bk_g], writes=[("g_all", c)])
            act(TB["kksq"][:, :Nt], k_c, AF.Square, reads=["k_c", "pv"], writes=["kksq"], scale=pcol(l, 38 + c))
            bank_s, bk_s = nb()
            mm(bank_s[:, :Nt], blockones[:], TB["kksq"][:, :Nt], True, True, reads=["blockones", "kksq"], writes=[bk_s])
            yield
            dve("tensor_scalar", T["sg"][:, :Nt], T["sg"][:, :Nt], 0.5, 0.5, ALU.mult, ALU.add, reads=["sg"], writes=["sg"])
            dve("tensor_tensor_scan", T["csg"][:, :Nt], rmask[sm][:, :Nt], T["sg"][:, :Nt], 0.0, ALU.mult, ALU.add,
                reads=["sg", "rmask%d" % sm], writes=["csg"])
            dve("tensor_tensor", T["cprev"][:, :Nt], T["csg"][:, :Nt], T["sg"][:, :Nt], ALU.subtract,
                reads=["csg", "sg"], writes=["cprev"])
            act(T["ssd"][:, :Nt], bank_s[:, :Nt], AF.Sqrt, reads=[bk_s], writes=["ssd"])
            act(T["Epl"][:, :Nt], T["csg"][:, :Nt], AF.Exp, reads=["csg"], writes=["Epl"], scale=-DEC_SCALE)
            act(T["Emn"][:, :Nt], T["csg"][:, :Nt], AF.Exp, reads=["csg"], writes=["Emn"], scale=DEC_SCALE)
            act(T["Epv"][:, :Nt], T["cprev"][:, :Nt], AF.Exp, reads=["cprev"], writes=["Epv"], scale=-DEC_SCALE)
            dve("tensor_scalar", T["a_in"][:, :Nt], T["a_in"][:, :Nt], 0.5, 0.5, ALU.mult, ALU.add, reads=["a_in"], writes=["a_in"])
            shift(geo, l, c, T["r_c"][:, :Nt], "r_c")
            shift(geo, l, 8 + c, T["v_c"][:, :Nt], "v_c")
            yield
            dve("tensor_scalar_max", T["ssd"][:, :Nt], T["ssd"][:, :Nt], 1e-12, reads=["ssd"], writes=["ssd"])
            dve("reciprocal", T["rn"][:, :Nt], T["ssd"][:, :Nt], reads=["ssd"], writes=["rn"])
            dve("scalar_tensor_tensor", T["kkn"][:, :Nt], k_c, pcol(l, 38 + c), T["rn"][:, :Nt], ALU.mult, ALU.mult,
                reads=["k_c", "rn", "pv"], writes=["kkn"])
            dve("tensor_scalar", T["tmpa"][:, :Nt], T["a_in"][:, :Nt], pcol(l, 42 + c), omka[:, l, c:c + 1],
                ALU.mult, ALU.add, reads=["a_in", "pv", "omka"], writes=["tmpa"])
            dve("tensor_tensor", T["kmod"][:, :Nt], k_c, T["tmpa"][:, :Nt], ALU.mult,
                reads=["k_c", "tmpa"], writes=["kmod"])
            dve("tensor_tensor", T["bsc"][:, :Nt], T["kkn"][:, :Nt], T["a_in"][:, :Nt], ALU.mult,
                reads=["kkn", "a_in"], writes=["bsc"])
            dve("scalar_tensor_tensor", TB["rkb"][:, :Nt], r_c, pcol(l, 46 + c), T["kmod"][:, :Nt], ALU.mult, ALU.mult,
                reads=["r_c", "kmod", "pv"], writes=["rkb"])
            bank_b, bk_b = nb()
            mm(bank_b[:, :Nt], blockones[:], TB["rkb"][:, :Nt], True, True, reads=["blockones", "rkb"], writes=[bk_b])
            act(vb[:, c, :Nt], v_c, AF.Copy, reads=["v_c"], writes=[("vb", c)])
            yield
            ngr = G if geo.sample else nch
            per = L if geo.sample else 128
            dve("tensor_copy", WC[:, c, 0:ngr],
                T["Epl"][:, :Nt].rearrange("p (a b) -> p a b", b=per)[:, :, per - 1],
                reads=["Epl"], writes=[("WC", c)])
            dve("scalar_tensor_tensor", aT[:, c, :Nt], T["kkn"][:, :Nt], -1.0, T["Epv"][:, :Nt], ALU.mult, ALU.mult,
                reads=["kkn", "Epv"], writes=[("aT", c)])
            dve("tensor_tensor", bT[:, c, :Nt], T["bsc"][:, :Nt], T["Emn"][:, :Nt], ALU.mult,
                reads=["bsc", "Emn"], writes=[("bT", c)])
            dve("tensor_tensor", kT[:, c, :Nt], T["kmod"][:, :Nt], T["Emn"][:, :Nt], ALU.mult,
                reads=["kmod", "Emn"], writes=[("kT", c)])
            dve("tensor_tensor", rT[:, c, :Nt], r_c, T["Epl"][:, :Nt], ALU.mult,
                reads=["r_c", "Epl"], writes=[("rT", c)])
            dve("tensor_tensor", bonus[:, c, :Nt], bank_b[:, :Nt], v_c, ALU.mult, reads=[bk_b, "v_c"], writes=[("bonus", c)])
            for hh in range(2):
                pr = slice(hh * 64, hh * 64 + 64)
                act(aTp[pr, c, hh, :Nt], aT[pr, c, :Nt], AF.Copy, reads=[("aT", c)], writes=[("aTp", c)])
                act(rTp[pr, c, hh, :Nt], rT[pr, c, :Nt], AF.Copy, reads=[("rT", c)], writes=[("rTp", c)])
            yield
        for ci in range(nch):
            tok = slice(ci * 128, (ci + 1) * 128)
            for nm, src, dst in (("bT", bT, Btok), ("kT", kT, Ktok), ("vb", vb, Vtok)):
                bank, bk = nb()
                bb = bank[:].bitcast(BF16)
                for c in range(4):
                    P.add("pe", lambda e, bb=bb, src=src, c=c, tok=tok: e.transpose(bb[:, c * 128:(c + 1) * 128], src[:, c, tok], ident[:]),
                          reads=[(nm, c), "ident"], writes=[bk])
                act(dst[:, ci, :], bb[:, 0:512], AF.Copy, reads=[bk], writes=[(nm + "tok", ci)])
            v4 = Vtok[:, ci, :].rearrange("p (c h i) -> p c h i", c=4, h=2)
            vp = Vpad[:, ci, :, :].rearrange("p (c h) n -> p c h n", h=2)
            for hh in range(2):
                act(vp[:, :, hh, hh * 64:(hh + 1) * 64], v4[:, :, hh, :], AF.Copy,
                    reads=[("vbtok", ci)], writes=[("Vpad", ci)])
            yield
        Kfac = 3 if geo.sample else 7
        jobs = [[0], [1], [2], [3]] if geo.sample else [[0, 1], [2, 3]]
        for ci in range(nch):
            tok = slice(ci * 128, (ci + 1) * 128)
            bankY, bkY = banks[7], ("ps", 7)
            for cs in jobs:
                nh = 2 * len(cs)
                heads = [(c, hh) for c in cs for hh in range(2)]

                def five(lhs, lhs_nm, rhs, rhs_nm):
                    bank, bk = nb()
                    for hl, (c, hh) in enumerate(heads):
                        la = lhs[:, c, hh, tok] if lhs_nm in ("aTp", "rTp") else lhs[:, c, tok]
                        ra = rhs[:, c, hh, tok] if rhs_nm in ("aTp", "rTp") else rhs[:, c, tok]
                        mm(bank[:, hl * 128:(hl + 1) * 128], la, ra, True, True,
                           reads=[(lhs_nm, c), (rhs_nm, c)], writes=[bk])
                    return bank, bk

                def evm(dst, dkey, bank, bk, mask, mkey):
                    dve("tensor_tensor", dst[:, 0:nh, :], bank[:, 0:nh * 128].rearrange("p (h t) -> p h t", h=nh),
                        bc(mask[:], nh), ALU.mult, reads=[bk, mkey], writes=[dkey])

                bank, bk = five(bT, "bT", aTp, "aTp")
                evm(Nm[0], "Nm0", bank, bk, m_su[sm], "m_su%d" % sm)
                bank, bk = five(aTp, "aTp", bT, "bT")
                evm(Mm[0], "Mm0", bank, bk, m_sl[sm], "m_sl%d" % sm)
                bank, bk = five(bT, "bT", rTp, "rTp")
                evm(ArbT, "ArbT", bank, bk, m_iu[sm], "m_iu%d" % sm)
                bank, bk = five(kT, "kT", aTp, "aTp")
                evm(LakT, "LakT", bank, bk, m_su[sm], "m_su%d" % sm)
                bank, bk = five(kT, "kT", rTp, "rTp")
                evm(ArkT, "ArkT", bank, bk, m_iu[sm], "m_iu%d" % sm)
                yield
                dve("tensor_tensor", Qm[0][:, 0:nh, :], Nm[0][:, 0:nh, :], bc(ident[:], nh), ALU.add,
                    reads=["Nm0", "ident"], writes=["Qm0"])
                qi = 0
                for k in range(Kfac - 1):
                    a, b = k % 2, (k + 1) % 2
                    bankM, bkM = nb()
                    for hl in range(nh):
                        mm(bankM[:, hl * 128:(hl + 1) * 128], Nm[a][:, hl, :], Mm[a][:, hl, :], True, True,
                           reads=["Nm%d" % a, "Mm%d" % a], writes=[bkM])
                    if k < Kfac - 2:
                        bankN, bkN = nb()
                        for hl in range(nh):
                            mm(bankN[:, hl * 128:(hl + 1) * 128], Mm[a][:, hl, :], Nm[a][:, hl, :], True, True,
                               reads=["Nm%d" % a, "Mm%d" % a], writes=[bkN])
                    yield
                    act(Mm[b][:, 0:nh, :], bankM[:, 0:nh * 128].rearrange("p (h t) -> p h t", h=nh), AF.Copy,
                        reads=[bkM], writes=["Mm%d" % b])
                    if k < Kfac - 2:
                        act(Nm[b][:, 0:nh, :], bankN[:, 0:nh * 128].rearrange("p (h t) -> p h t", h=nh), AF.Copy,
                            reads=[bkN], writes=["Nm%d" % b])
                    yield
                    bank, bk = nb()
                    for hl in range(nh):
                        mm(bank[:, hl * 128:(hl + 1) * 128], Mm[b][:, hl, :], Qm[qi][:, hl, :], True, True,
                           reads=["Mm%d" % b, "Qm%d" % qi], writes=[bk])
                    yield
                    dve("tensor_tensor", Qm[1 - qi][:, 0:nh, :], bank[:, 0:nh * 128].rearrange("p (h t) -> p h t", h=nh),
                        Qm[qi][:, 0:nh, :], ALU.add, reads=[bk, "Qm%d" % qi], writes=["Qm%d" % (1 - qi)])
                    qi = 1 - qi
                Q = Qm[qi]
                Qk = "Qm%d" % qi
                if geo.sample:
                    c = cs[0]
                    dma("sp", H0s[:], st_wkv[l, c], reads=[], writes=["H0s"])
                    for hh in range(2):
                        pr = slice(hh * 64, hh * 64 + 64)
                        act(Hpad_s[pr, :, hh * 64:(hh + 1) * 64], H0s[pr, :, :], AF.Copy, reads=["H0s"], writes=["Hpad_s"])
                    dve("tensor_tensor", aTm[:], aT[:, c, 0:128].unsqueeze(1).to_broadcast([128, 16, 128]), qsel[:], ALU.mult,
                        reads=[("aT", c), "qsel"], writes=["aTm"])
                bankR, bkR = nb()
                for cl, c in enumerate(cs):
                    if geo.sample:
                        for q in range(16):
                            mm(bankR[:, cl * 128:(cl + 1) * 128], aTm[:, q, :], Hpad_s[:, q, :], q == 0, False,
                               reads=["aTm", "Hpad_s"], writes=[bkR])
                    else:
                        mm(bankR[:, cl * 128:(cl + 1) * 128], aT[:, c, tok], Hpad[:, l, c, :], True, False,
                           reads=[("aT", c), Hpk], writes=[bkR])
                    for hh in range(2):
                        hl = 2 * cl + hh
                        h = 2 * c + hh
                        mm(bankR[:, hl * 64:(hl + 1) * 64], LakT[:, hl, :], Vtok[:, ci, h * 64:(h + 1) * 64], False, hh == 1,
                           reads=["LakT", ("vbtok", ci)], writes=[bkR])
                act(R0[:, 0:nh, :], bankR[:, 0:nh * 64].rearrange("p (h i) -> p h i", h=nh), AF.Copy, reads=[bkR], writes=["R0"])
                bankU, bkU = nb()
                for hl in range(nh):
                    mm(bankU[:, hl * 64:(hl + 1) * 64], Q[:, hl, :], R0[:, hl, :], True, True, reads=[Qk, "R0"], writes=[bkU])
                act(Usb[:, 0:nh, :], bankU[:, 0:nh * 64].rearrange("p (h i) -> p h i", h=nh), AF.Copy, reads=[bkU], writes=["Usb"])
                for hl, (c, hh) in enumerate(heads):
                    dve("tensor_copy", Upad[:, hl, hh * 64:(hh + 1) * 64], bankU[:, hl * 64:(hl + 1) * 64],
                        reads=[bkU], writes=["Upad"])
                yield
                for cl, c in enumerate(cs):
                    first = True
                    if not geo.sample:
                        mm(bankY[:, c * 128:(c + 1) * 128], Hpad[:, l, c, :], rT[:, c, tok], True, False,
                           reads=[Hpk, ("rT", c)], writes=[bkY])
                        first = False
                    for hh in range(2):
                        hl = 2 * cl + hh
                        h = 2 * c + hh
                        mm(bankY[:, c * 128:(c + 1) * 128], Upad[:, hl, :], ArbT[:, hl, :], first, False,
                           reads=["Upad", "ArbT"], writes=[bkY])
                        first = False
                        mm(bankY[:, c * 128:(c + 1) * 128], Vpad[:, ci, h, :], ArkT[:, hl, :], False, (not geo.sample) and hh == 1,
                           reads=[("Vpad", ci), "ArkT"], writes=[bkY])
                    if geo.sample:
                        for q in range(16):
                            mm(bankY[:, c * 128 + q * 8:c * 128 + q * 8 + 8], Hpad_s[:, q, :], rT[:, c, q * 8:q * 8 + 8],
                               False, q == 15, reads=["Hpad_s", ("rT", c)], writes=[bkY])
                yield
                if not geo.sample:
                    bankS, bkS = nb()
                    for cl, c in enumerate(cs):
                        mm(bankS[:, cl * 128:(cl + 1) * 128], Btok[:, ci, c * 128:(c + 1) * 128],
                           Usb[:, 2 * cl:2 * cl + 2, :].rearrange("p h i -> p (h i)"), True, False,
                           reads=[("bTtok", ci), "Usb"], writes=[bkS])
                        mm(bankS[:, cl * 128:(cl + 1) * 128], Ktok[:, ci, c * 128:(c + 1) * 128],
                           Vtok[:, ci, c * 128:(c + 1) * 128], False, True,
                           reads=[("kTtok", ci), ("vbtok", ci)], writes=[bkS])
                    c0 = cs[0]
                    ncs = len(cs)
                    for hh in range(2):
                        pr = slice(hh * 64, hh * 64 + 64)
                        psv = bankS[pr, 0:ncs * 128].rearrange("p (c n) -> p c n", c=ncs)[:, :, hh * 64:(hh + 1) * 64]
                        dve("tensor_tensor", tmpH[pr, 0:ncs, :], psv, Hst[pr, l, c0:c0 + ncs, :], ALU.add,
                            reads=[bkS, Hk], writes=["tmpH"])
                        dve("tensor_tensor", Hst[pr, l, c0:c0 + ncs, :], tmpH[pr, 0:ncs, :],
                            WC[pr, c0:c0 + ncs, ci:ci + 1].to_broadcast([64, ncs, 64]), ALU.mult,
                            reads=["tmpH"] + [("WC", c) for c in cs], writes=[Hk])
                        act(Hpad[pr, l, c0:c0 + ncs, hh * 64:(hh + 1) * 64], Hst[pr, l, c0:c0 + ncs, :], AF.Copy,
                            reads=[Hk], writes=[Hpk])
                else:
                    c = cs[0]
                    dve("tensor_tensor", Uexp[:], Usb[:, 0:2, :].rearrange("p h i -> p (h i)").unsqueeze(1).to_broadcast([128, 16, 128]),
                        qmask[:].unsqueeze(2).to_broadcast([128, 16, 128]), ALU.mult, reads=["Usb", "qmask"], writes=["Uexp"])
                    dve("tensor_tensor", Vexp[:], Vtok[:, ci, c * 128:(c + 1) * 128].unsqueeze(1).to_broadcast([128, 16, 128]),
                        qmask[:].unsqueeze(2).to_broadcast([128, 16, 128]), ALU.mult, reads=[("vbtok", ci), "qmask"], writes=["Vexp"])
                    for nbk in range(4):
                        bankS, bkS = nb()
                        mm(bankS[:], Btok[:, ci, c * 128:(c + 1) * 128],
                           Uexp[:, 4 * nbk:4 * nbk + 4, :].rearrange("p q n -> p (q n)"), True, False,
                           reads=[("bTtok", ci), "Uexp"], writes=[bkS])
                        mm(bankS[:], Ktok[:, ci, c * 128:(c + 1) * 128],
                           Vexp[:, 4 * nbk:4 * nbk + 4, :].rearrange("p q n -> p (q n)"), False, True,
                           reads=[("kTtok", ci), "Vexp"], writes=[bkS])
                        for hh in range(2):
                            pr = slice(hh * 64, hh * 64 + 64)
                            psv = bankS[pr, :].rearrange("p (q n) -> p q n", q=4)[:, :, hh * 64:(hh + 1) * 64]
                            dve("tensor_tensor", Hout_s[pr, 4 * nbk:4 * nbk + 4, :], psv, H0s[pr, 4 * nbk:4 * nbk + 4, :], ALU.add,
                                reads=[bkS, "H0s"], writes=["H0s"])
                            dve("tensor_tensor", Hout_s[pr, 4 * nbk:4 * nbk + 4, :], Hout_s[pr, 4 * nbk:4 * nbk + 4, :],
                                WC[pr, c, 4 * nbk:4 * nbk + 4].unsqueeze(2).to_broadcast([64, 4, 64]), ALU.mult,
                                reads=["H0s", ("WC", c)], writes=["H0s"])
                    dma("sp", o_wkv_s[l, c], Hout_s[:], reads=["H0s"], writes=["o_wkv_s"])
            act(yT[:, :, tok], bankY[:, :].rearrange("p (c t) -> p c t", c=4), AF.Copy, reads=[bkY], writes=[("yT", ci)])
            yield
        if geo.last:
            dve("tensor_copy", H0s[:, 0:4, :], Hst[:, l, :, :], reads=[Hk], writes=["H0s"])
            dma("sp", o_wkv_p[l], H0s[:, 0:4, :], reads=["H0s"], writes=["o_wkv_p"])

    def back(geo, l, pp):
        Nt, G, L, nch = geo.Nt, geo.G, geo.L, geo.nch
        x, hT, amix, pmix = x2[pp], hT2[pp], amix2[pp], pmix2[pp]
        hkeys = lambda k: [("hT", pp, k)]
        hrhs = lambda k: hT[:, k, :Nt]
        W = G * (15 + L)
        for gi in range(4):
            uv = uview(gi, geo)
            sAv = sA[:, 0:W].rearrange("p (g t) -> p g t", g=G)
            sBv = sB[:, 0:W].rearrange("p (g t) -> p g t", g=G)
            E = 15 + L
            src, srck = uv, ("u", gi)
            bufs = [(sAv, "sA"), (sBv, "sB")]
            nsteps = gi + 1
            for s in range(nsteps):
                d = 1 << s
                lo = 15 if s == nsteps - 1 else (2 << s) - 1
                lo = max(lo, (2 << s) - 1) if s < nsteps - 1 else 15
                dst, dstk = bufs[s % 2]
                dve("tensor_tensor", dst[:, :, lo:E], src[:, :, lo:E], src[:, :, lo - d:E - d], ALU.add,
                    reads=[srck], writes=[dstk])
                src, srck = dst, dstk
            ws_ = src[:, :, 15:E]
            win = 2 << gi
            dve("scalar_tensor_tensor", g3(p_in[:, gi, :Nt], geo), ws_, 1.0 / win, uv[:, :, 15:E], ALU.mult, ALU.subtract,
                reads=[srck, ("u", gi)], writes=[("p_in", gi)])
            if geo.first:
                dve("tensor_tensor", tmpp[:, 0:16], src[:, 0, 15:31], rcs[:, gi, :], ALU.mult,
                    reads=[srck, "rcs"], writes=["tmpp"])
                dve("tensor_tensor", p_in[:, gi, 0:16], tmpp[:, 0:16], uv[:, 0, 15:31], ALU.subtract,
                    reads=["tmpp", ("u", gi)], writes=[("p_in", gi)])
            bank, bk = nb()
            mm(bank[:, :Nt], wsm[:, l, 1536 + gi * 128:1536 + (gi + 1) * 128], p_in[:, gi, :Nt], True, True,
               reads=["wsm", ("p_in", gi)], writes=[bk])
            act(pmix[:, gi, :Nt], bank[:, :Nt], AF.Identity, reads=[bk, "pv"], writes=[("pmix", pp, gi)], scale=pcol(l, 58 + gi))
            yield
        yield "h_done"
        ytk = [("yT", ci) for ci in range(nch)]
        for c in range(4):
            act(TB["yb"][:, :Nt], yT[:, c, :Nt], AF.Copy, reads=ytk, writes=["yb"])
            bank, bk = nb()
            mm(bank[:, :Nt], blockmean[:], TB["yb"][:, :Nt], True, True, reads=["blockmean", "yb"], writes=[bk])
            dve("tensor_tensor", T["kkn"][:, :Nt], yT[:, c, :Nt], bank[:, :Nt], ALU.subtract, reads=ytk + [bk], writes=["kkn"])
            act(TB["ycsq"][:, :Nt], T["kkn"][:, :Nt], AF.Square, reads=["kkn"], writes=["ycsq"])
            bank, bk = nb()
            mm(bank[:, :Nt], blockmean[:], TB["ycsq"][:, :Nt], True, True, reads=["blockmean", "ycsq"], writes=[bk])
            act(T["ssd"][:, :Nt], bank[:, :Nt], AF.Sqrt, reads=[bk, "epsc"], writes=["ssd"], bias=epsc[:, 1:2])
            dve("reciprocal", T["rn"][:, :Nt], T["ssd"][:, :Nt], reads=["ssd"], writes=["rn"])
            dve("tensor_tensor", T["bsc"][:, :Nt], T["kkn"][:, :Nt], T["rn"][:, :Nt], ALU.mult, reads=["kkn", "rn"], writes=["bsc"])
            dve("tensor_scalar", T["bsc"][:, :Nt], T["bsc"][:, :Nt], pcol(l, 50 + c), pcol(l, 54 + c), ALU.mult, ALU.add,
                reads=["bsc", "pv"], writes=["bsc"])
            dve("tensor_tensor", T["bsc"][:, :Nt], T["bsc"][:, :Nt], bonus[:, c, :Nt], ALU.add,
                reads=["bsc", ("bonus", c)], writes=["bsc"])
            dve("tensor_tensor", amix[:, c, :Nt], T["bsc"][:, :Nt], g_all[:, c, :Nt], ALU.mult,
                reads=["bsc", ("g_all", c)], writes=[("amix", pp, c)])
            yield
        yield "g_done"
        for m in range(8):
            slot, sk = w_take(l, 18 + 3 * m)
            bankA, bkA = big_mm(geo, slot, sk, 8, hrhs, hkeys)
            act(ga[:, :Nt], bankA[:, :Nt], AF.Tanh, reads=[bkA], writes=["ga"], scale=0.5)
            slot, sk = w_take(l, 18 + 3 * m + 1)
            bankB, bkB = big_mm(geo, slot, sk, 8, hrhs, hkeys)
            act(gb[:, :Nt], bankB[:, :Nt], AF.Tanh, reads=[bkB], writes=["gb"], scale=0.5)
            slot, sk = w_take(l, 18 + 3 * m + 2)
            banka, bka = nb()
            for c in range(4):
                mm(banka[:, :Nt], slot[:, c, :], amix[:, c, :Nt], c == 0, c == 3, reads=[sk, ("amix", pp, c)], writes=[bka])
            bankb, bkb = nb()
            for gi in range(4):
                mm(bankb[:, :Nt], slot[:, 4 + gi, :], pmix[:, gi, :Nt], gi == 0, gi == 3, reads=[sk, ("pmix", pp, gi)], writes=[bkb])
            dve("scalar_tensor_tensor", ga[:, :Nt], ga[:, :Nt], 1.0, banka[:, :Nt], ALU.add, ALU.mult, reads=["ga", bka], writes=["ga"])
            dve("scalar_tensor_tensor", gb[:, :Nt], gb[:, :Nt], 1.0, bankb[:, :Nt], ALU.add, ALU.mult, reads=["gb", bkb], writes=["gb"])
            dve("tensor_tensor", merged[:, m, :Nt], ga[:, :Nt], gb[:, :Nt], ALU.add, reads=["ga", "gb"], writes=[("merged", m)])
            yield
        for m in range(8):
            slot, sk = w_take(l, 42 + m)
            bank, bk = big_mm(geo, slot, sk, 8, lambda k: merged[:, k, :Nt], lambda k: [("merged", k)])
            dve("scalar_tensor_tensor", x[:, m, :Nt], bank[:, :Nt], 0.5, x[:, m, :Nt], ALU.mult, ALU.add,
                reads=[("x", pp, m), bk], writes=[("x", pp, m)])
        yield
        rmsnorm(geo, l, 8, True, pp, 1)
        yield
        pj = 50
        for fq in range(4):
            for fl in range(8):
                slot, sk = w_take(l, pj)
                pj += 1
                bank, bk = big_mm(geo, slot, sk, 8, hrhs, hkeys)
                rt = rtmp[fl % 2]
                act(rt[:, :Nt], bank[:, :Nt], AF.Relu, reads=[bk], writes=[("rtmp", fl % 2)])
                dve("tensor_tensor", hid[:, fl, :Nt], rt[:, :Nt], rt[:, :Nt], ALU.mult,
                    reads=[("rtmp", fl % 2)], writes=[("hid", fl)])
                yield
            for m in range(8):
                slot, sk = w_take(l, pj)
                pj += 1
                bank, bk = big_mm(geo, slot, sk, 8, lambda k: hid[:, k, :Nt], lambda k: [("hid", k)])
                dve("tensor_tensor", x[:, m, :Nt], x[:, m, :Nt], bank[:, :Nt], ALU.add, reads=[("x", pp, m), bk], writes=[("x", pp, m)])
        assert pj == NPIECE

    def back_full(geo, l, pp):
        yield from back(geo, l, pp)
        if l == 1:
            Nt = geo.Nt
            x = x2[pp]
            rmsnorm(geo, 0, 62, False, pp, 1)
            for kc in range(8):
                yt = rtmp[kc % 2]
                dve("scalar_tensor_tensor", yt[:, :Nt], x[:, kc, :Nt], pcol(0, 62 + kc), rstd2[1][:, :Nt], ALU.mult, ALU.mult,
                    reads=[("x", pp, kc), ("rstd", 1), "pv"], writes=[("rtmp", kc % 2)])
                if geo.sample:
                    dma("sp", ys[kc], yt[:, :Nt], reads=[("rtmp", kc % 2)], writes=["ys"])
                else:
                    dma("sp", yp[kc, :, geo.tok0:geo.tok0 + Nt], yt[:, :Nt], reads=[("rtmp", kc % 2)], writes=["yp"])
            yield

    def count_steps(gen):
        n = 0
        for _ in gen:
            n += 1
        return n

    def emit_all():
        bank_ctr[0] = 0
        bank_ctr[1] = 0
        wst["order"] = []
        nslots = 4 * ((len(tiles) - 1) // 2) + ((len(tiles) - 1) % 2) + 4
        for slot in range(nslots):
            active = []
            for i, geo in enumerate(tiles):
                st = slot - (4 * (i // 2) + (i % 2))
                if 0 <= st < 4:
                    l, isback = st // 2, st % 2
                    active.append((geo, l, i % 2, isback))
            gens = []
            for geo, l, pp, isback in active:
                key = (geo.sample, l, isback)
                if key not in step_cache:
                    was = P.dry
                    P.dry = True
                    saved = (list(bank_ctr), list(wst["order"]), wst.get("pos", 0))
                    phase[0] = isback
                    step_cache[key] = count_steps((back_full if isback else front)(geo, l, pp))
                    bank_ctr[0], bank_ctr[1] = saved[0]
                    wst["order"], wst["pos"] = saved[1], saved[2]
                    P.dry = was
                gens.append([(back_full if isback else front)(geo, l, pp), 0, step_cache[key], isback, False])
            NEED = {"pre_U": "h_done", "pre_D": "g_done"}
            events = set() if any(g[3] for g in gens) else {"h_done", "g_done"}
            while gens:
                gens.sort(key=lambda g: g[1] / g[2])
                g = gens[0]
                if g[4] and g[4] not in events:
                    others = [h for h in gens if h is not g]
                    if others:
                        g = others[0]
                try:
                    phase[0] = g[3]
                    v = next(g[0])
                    g[1] += 1
                    if v in NEED:
                        g[4] = NEED[v]
                    elif v in ("h_done", "g_done"):
                        events.add(v)
                except StopIteration:
                    gens.remove(g)
                    if g[3]:
                        events.update(("h_done", "g_done"))

    step_cache = {}
    P.dry = True
    wst["mode"] = "record"
    emit_all()
    seq = wst["order"]
    P.dry = False
    wst["mode"] = "emit"
    wst["issued"] = 0
    wst["pos"] = 0
    emit_all()
    assert wst["pos"] == len(seq)
    P.finish(OUT_KEYS)
    return nc, P


def _wstream(w_in, w_a_up, w_b_up, w_o, w_ff1, w_ff2):
    L = w_in.shape[0]
    out = np.empty((L, NPIECE, 128, 1024), np.float32)

    def pk(mat, ncol0):
        K = mat.shape[0] // 128
        return mat[:, ncol0:ncol0 + 128].reshape(K, 128, 128).transpose(1, 0, 2)

    for l in range(L):
        j = 0
        for c in WIN_ORDER:
            out[l, j] = pk(w_in[l], c * 128).reshape(128, 1024)
            j += 1
        for m in range(8):
            out[l, j] = pk(w_in[l], (18 + m) * 128).reshape(128, 1024)
            j += 1
            out[l, j] = pk(w_in[l], (26 + m) * 128).reshape(128, 1024)
            j += 1
            ab = np.concatenate([pk(w_a_up[l], m * 128), pk(w_b_up[l], m * 128)], axis=1)
            out[l, j] = ab.reshape(128, 1024)
            j += 1
        for m in range(8):
            out[l, j] = pk(w_o[l], m * 128).reshape(128, 1024)
            j += 1
        for fq in range(4):
            for fl in range(8):
                out[l, j] = pk(w_ff1[l], (fq * 8 + fl) * 128).reshape(128, 1024)
                j += 1
            for m in range(8):
                out[l, j] = pk(w_ff2[l][fq * 1024:(fq + 1) * 1024], m * 128).reshape(128, 1024)
                j += 1
        assert j == NPIECE
    return out


_CACHE = {}


def kernel(x_prompt, x_sample, state_shift, state_pool, state_wkv,
           norm1_g, w_in, mu_shift, decay0, w_decay2, a0, w_a2, w_g2, k_k, k_a, r_k,
           ln_x_g, ln_x_b, w_a_up, w_pool, pool_scale, w_b_up, w_o,
           norm2_g, w_ff1, w_ff2, final_norm_g):
    f = lambda a: np.ascontiguousarray(np.asarray(a, dtype=np.float32))
    x_prompt, x_sample, state_shift, state_pool, state_wkv = map(f, (x_prompt, x_sample, state_shift, state_pool, state_wkv))
    B, TP, _ = x_prompt.shape
    assert B == NCORES and TP % NT == 0
    L = 2

    def cols(v, n):
        return f(v).reshape(n, 128).T

    pvs = np.zeros((L, 128, NPV), np.float32)
    for l in range(L):
        pvs[l, :, 0:8] = cols(norm1_g[l], 8)
        pvs[l, :, 8:16] = cols(norm2_g[l], 8)
        pvs[l, :, 16:30] = cols(mu_shift[l], 14)
        pvs[l, :, 30:34] = cols(decay0[l], 4)
        pvs[l, :, 34:38] = cols(a0[l], 4)
        pvs[l, :, 38:42] = cols(k_k[l], 4)
        pvs[l, :, 42:46] = cols(k_a[l], 4)
        pvs[l, :, 46:50] = cols(f(r_k[l]).reshape(-1), 4)
        pvs[l, :, 50:54] = cols(ln_x_g[l], 4)
        pvs[l, :, 54:58] = cols(ln_x_b[l], 4)
        pvs[l, :, 58:62] = cols(pool_scale[l], 4)
        pvs[l, :, 62:70] = cols(final_norm_g, 8)
    wsmall = np.zeros((L, 128, 2048), np.float32)
    for l in range(L):
        wsmall[l, 0:64, 0:512] = f(w_decay2[l])
        wsmall[l, 64:128, 512:1024] = f(w_a2[l])
        wsmall[l, :, 1024:1536] = f(w_g2[l])
        wsmall[l, :, 1536:2048] = f(w_pool[l]).transpose(1, 0, 2).reshape(128, 512)
    wstream = _wstream(f(w_in), f(w_a_up), f(w_b_up), f(w_o), f(w_ff1), f(w_ff2))

    key = TP
    if key not in _CACHE:
        _CACHE[key] = build_program(TP)[0]
    nc = _CACHE[key]
    in_maps = []
    for c in range(NCORES):
        q0 = 16 * c
        xpc = np.ascontiguousarray(x_prompt[c].T.reshape(8, 128, TP))
        xsc = np.ascontiguousarray(x_sample[q0:q0 + 16].reshape(128, D).T.reshape(8, 128, 128))
        sh = np.ascontiguousarray(state_shift[:, q0:q0 + 16, :].reshape(L, 16, 14, 128).transpose(0, 3, 2, 1))
        pl = np.ascontiguousarray(state_pool[:, q0:q0 + 16].reshape(L, 16, 15, 4, 128).transpose(0, 4, 3, 1, 2))
        wk = state_wkv[:, q0:q0 + 16].reshape(L, 16, 4, 2, 64, 64).transpose(0, 2, 3, 5, 1, 4)
        wk = np.ascontiguousarray(wk.reshape(L, 4, 128, 16, 64))
        in_maps.append({"xp": xpc, "xs": xsc, "st_shift": sh, "st_pool": pl, "st_wkv": wk,
                        "pv": pvs, "wsmall": wsmall, "wstream": wstream})
    res = run_bass_kernel_spmd(nc, in_maps, core_ids=list(range(NCORES)))
    R = res.results
    y_prompt = np.stack([R[c]["yp"].reshape(D, TP).T for c in range(NCORES)])
    y_sample = np.concatenate([R[c]["ys"].reshape(D, 128).T.reshape(16, 8, D) for c in range(NCORES)])
    p_shift = np.stack([R[c]["o_shift_p"].reshape(L, 128, 14).transpose(0, 2, 1).reshape(L, DSHIFT) for c in range(NCORES)], axis=1)
    s_shift = np.concatenate([R[c]["o_shift_s"].reshape(L, 128, 14, 16).transpose(0, 3, 2, 1).reshape(L, 16, DSHIFT)
                              for c in range(NCORES)], axis=1)
    p_pool = np.stack([R[c]["o_pool_p"].reshape(L, 128, 4, 15).transpose(0, 3, 2, 1).reshape(L, 15, 512) for c in range(NCORES)], axis=1)
    s_pool = np.concatenate([R[c]["o_pool_s"].reshape(L, 128, 4, 16, 15).transpose(0, 3, 4, 2, 1).reshape(L, 16, 15, 512)
                             for c in range(NCORES)], axis=1)
    p_wkv = np.stack([R[c]["o_wkv_p"].reshape(L, 2, 64, 4, 64).transpose(0, 3, 1, 4, 2).reshape(L, 8, 64, 64)
                      for c in range(NCORES)], axis=1)
    s_wkv = np.concatenate([R[c]["o_wkv_s"].reshape(L, 4, 2, 64, 16, 64).transpose(0, 4, 1, 2, 5, 3).reshape(L, 16, 8, 64, 64)
                            for c in range(NCORES)], axis=1)
    out = (y_prompt, y_sample, p_shift, p_pool, p_wkv, s_shift, s_pool, s_wkv)
    return tuple(np.ascontiguousarray(o, dtype=np.float32) for o in out)
```
